# Optimizing a Trainium2 kernel written in Bass

```python
import jax, jax.numpy as jnp
from jax import lax
import numpy as np

D_MODEL = 1024
BATCH = 8
SEQ = 2048
DEPTH = 4
DEC_BATCH = 128
DEC_SEQ = 1
PAST_LEN = 8192
PAGE_SIZE = 128

N_MIXERS = 3
N_MLA = (DEPTH + 2) // 3
N_CMLP = (DEPTH + 1) // 3
N_POOLMIX = DEPTH // 3
DEEPNORM_ALPHA = (2 * DEPTH) ** 0.25
DEEPNORM_BETA = (8 * DEPTH) ** -0.25
LN_EPS = 1e-5
RMS_EPS = 1e-6
NEG_INF = -1e30

MLA_HEADS = D_MODEL // 128
NOPE_DIM = 128
ROPE_DIM = 64
V_DIM = 128
Q_RANK = 3 * D_MODEL // 8
KV_RANK = D_MODEL // 4
ROPE_THETA = 10000.0
Q_BLOCK = 128
ATTN_SCALE = (NOPE_DIM + ROPE_DIM) ** -0.5

CHUNK = 128
CMLP_HEADS = 8
D_CMLP = 3 * D_MODEL
CMLP_HEAD_DIM = D_CMLP // CMLP_HEADS

POOL_WINDOWS = (2, 4, 8, 16)
POOL_GROUPS = 4
D_POOL = D_MODEL
POOL_GROUP_DIM = D_POOL // POOL_GROUPS
POOL_BUF = 15

D_FF = 2688

kernel_name = "hybrid_mla_chunkmlp_pool_macaron_step"


def _layernorm(x, g, b):
    xf = x.astype(jnp.float32)
    mu = jnp.mean(xf, axis=-1, keepdims=True)
    var = jnp.mean(jnp.square(xf - mu), axis=-1, keepdims=True)
    return ((xf - mu) * lax.rsqrt(var + LN_EPS) * g + b).astype(x.dtype)


def _rmsnorm(x, g):
    xf = x.astype(jnp.float32)
    return (xf * lax.rsqrt(jnp.mean(jnp.square(xf), axis=-1, keepdims=True) + RMS_EPS) * g).astype(x.dtype)


def _post(x, delta, g, b):
    return _layernorm(DEEPNORM_ALPHA * x + delta, g, b)


def _swiglu(x, wg, wu, wd):
    return (jax.nn.silu(x @ wg) * (x @ wu)) @ wd


def _rope(x, pos):
    half = x.shape[-1] // 2
    inv = ROPE_THETA ** (-jnp.arange(half, dtype=jnp.float32) / half)
    ang = pos.astype(jnp.float32)[:, None] * inv[None, :]
    cos = jnp.cos(ang)[None, :, None, :]
    sin = jnp.sin(ang)[None, :, None, :]
    xf = x.astype(jnp.float32)
    x1, x2 = xf[..., :half], xf[..., half:]
    return jnp.concatenate([x1 * cos - x2 * sin, x1 * sin + x2 * cos], axis=-1).astype(x.dtype)


def _mla_project(x, pos, w_in, q_norm, kv_norm, w_uq, w_uk):
    B, S, _ = x.shape
    lat = x @ w_in
    c_q = _rmsnorm(lat[..., :Q_RANK], q_norm)
    c_kv = _rmsnorm(lat[..., Q_RANK:Q_RANK + KV_RANK], kv_norm)
    k_rope = _rope(lat[..., Q_RANK + KV_RANK:][:, :, None, :], pos)[:, :, 0]
    q = (c_q @ w_uq).reshape(B, S, MLA_HEADS, NOPE_DIM + ROPE_DIM)
    q_rope = _rope(q[..., NOPE_DIM:], pos)
    q_lat = jnp.einsum('bshd,chd->bshc', q[..., :NOPE_DIM], w_uk)
    return q_lat, q_rope, c_kv, k_rope


def _mla_output(out_lat, w_uv, w_o):
    B, S = out_lat.shape[:2]
    o = jnp.einsum('bshc,chv->bshv', out_lat, w_uv).reshape(B, S, MLA_HEADS * V_DIM)
    return o @ w_o


def _prompt_attention(q_lat, q_rope, c_kv, k_rope):
    B, S, H, C = q_lat.shape
    nb = S // Q_BLOCK
    ql = q_lat.reshape(B, nb, Q_BLOCK, H, C).transpose(1, 0, 2, 3, 4)
    qr = q_rope.reshape(B, nb, Q_BLOCK, H, ROPE_DIM).transpose(1, 0, 2, 3, 4)
    kpos = jnp.arange(S)

    def block(args):
        qlb, qrb, bi = args
        s = jnp.einsum('bqhc,bkc->bhqk', qlb, c_kv) + jnp.einsum('bqhr,bkr->bhqk', qrb, k_rope)
        qpos = bi * Q_BLOCK + jnp.arange(Q_BLOCK)
        s = jnp.where(kpos[None, :] <= qpos[:, None], s.astype(jnp.float32) * ATTN_SCALE, NEG_INF)
        p = jax.nn.softmax(s, axis=-1).astype(c_kv.dtype)
        return jnp.einsum('bhqk,bkc->bqhc', p, c_kv)

    out = lax.map(block, (ql, qr, jnp.arange(nb)))
    return out.transpose(1, 0, 2, 3, 4).reshape(B, S, H, C)


def _sample_attention(q_lat, q_rope, c_kv, k_rope, past_ckv, past_kr):
    T = q_lat.shape[1]
    P = past_ckv.shape[1]
    s_past = jnp.einsum('bqhc,bkc->bhqk', q_lat, past_ckv) + jnp.einsum('bqhr,bkr->bhqk', q_rope, past_kr)
    s_new = jnp.einsum('bqhc,bkc->bhqk', q_lat, c_kv) + jnp.einsum('bqhr,bkr->bhqk', q_rope, k_rope)
    causal = jnp.tril(jnp.ones((T, T), dtype=bool))
    s_new = jnp.where(causal, s_new.astype(jnp.float32) * ATTN_SCALE, NEG_INF)
    s = jnp.concatenate([s_past.astype(jnp.float32) * ATTN_SCALE, s_new], axis=-1)
    p = jax.nn.softmax(s, axis=-1).astype(c_kv.dtype)
    return (jnp.einsum('bhqk,bkc->bqhc', p[..., :P], past_ckv)
            + jnp.einsum('bhqk,bkc->bqhc', p[..., P:], c_kv))


def _chunk_mlp(x, w_in, b_in, v_g, v_b, w_s, b_s, w_out):
    B, S, _ = x.shape
    pad = (-S) % CHUNK
    xp = jnp.pad(x, ((0, 0), (0, pad), (0, 0)))
    Sp = S + pad
    nc = Sp // CHUNK
    z = jax.nn.gelu(xp @ w_in + b_in)
    u, v = z[..., :D_CMLP], z[..., D_CMLP:]
    v = _layernorm(v, v_g, v_b)
    tril = jnp.tril(jnp.ones((CHUNK, CHUNK), dtype=w_s.dtype))
    vc = v.reshape(B, nc, CHUNK, CMLP_HEADS, CMLP_HEAD_DIM)
    mixed = jnp.einsum('hij,bnjhd->bnihd', w_s * tril, vc) + b_s.T[None, None, :, :, None]
    out = (u * mixed.reshape(B, Sp, D_CMLP)) @ w_out
    return out[:, :S], v[:, :S]


def _pool_tokens(h, scale, w_grp):
    B, L, _ = h.shape
    hf = h.astype(jnp.float32)
    cs = jnp.concatenate([jnp.zeros((B, 1, D_POOL), jnp.float32), jnp.cumsum(hf, axis=1)], axis=1)
    hi = jnp.arange(L) + 1
    outs = []
    for g, w in enumerate(POOL_WINDOWS):
        lo = jnp.maximum(hi - w, 0)
        sl = slice(g * POOL_GROUP_DIM, (g + 1) * POOL_GROUP_DIM)
        csg = cs[..., sl]
        mean = (csg[:, hi] - csg[:, lo]) / (hi - lo).astype(jnp.float32)[None, :, None]
        outs.append(mean - hf[..., sl])
    pooled = jnp.stack(outs, axis=2).astype(h.dtype)
    z = jnp.einsum('blgd,gde->blge', pooled, w_grp).reshape(B, L, D_POOL)
    return z * scale


def setup_inputs(seed: int = 0) -> dict:
    key = jax.random.key(seed)
    ks = iter(jax.random.split(key, 48))
    f32 = jnp.float32

    def nrm(shape, fan_in, mult=1.0):
        return jax.random.normal(next(ks), shape, f32) * (mult * fan_in ** -0.5)

    def near_one(shape):
        return 1.0 + 0.05 * jax.random.normal(next(ks), shape, f32)

    def small(shape):
        return 0.02 * jax.random.normal(next(ks), shape, f32)

    n_pages = PAST_LEN // PAGE_SIZE
    n_used = DEC_BATCH * n_pages
    n_pool = n_used + n_used // 4
    inp = {}
    inp['x_prompt'] = jax.random.normal(next(ks), (BATCH, SEQ, D_MODEL), f32)
    inp['x_sample'] = jax.random.normal(next(ks), (DEC_BATCH, DEC_SEQ, D_MODEL), f32)
    inp['cache_kv_latent'] = jax.random.normal(next(ks), (N_MLA, n_pool, PAGE_SIZE, KV_RANK), f32)
    inp['cache_k_rope'] = jax.random.normal(next(ks), (N_MLA, n_pool, PAGE_SIZE, ROPE_DIM), f32)
    inp['state_pool'] = jax.random.normal(next(ks), (N_POOLMIX, DEC_BATCH, POOL_BUF, D_POOL), f32)
    inp['page_table'] = jax.random.permutation(next(ks), n_pool)[:n_used].reshape(DEC_BATCH, n_pages).astype(jnp.int32)
    inp['ln_g'] = near_one((DEPTH, 3, D_MODEL))
    inp['ln_b'] = small((DEPTH, 3, D_MODEL))
    inp['ffn_w_gate'] = nrm((DEPTH, 2, D_MODEL, D_FF), D_MODEL)
    inp['ffn_w_up'] = nrm((DEPTH, 2, D_MODEL, D_FF), D_MODEL)
    inp['ffn_w_down'] = nrm((DEPTH, 2, D_FF, D_MODEL), D_FF, DEEPNORM_BETA)
    inp['a_w_in'] = nrm((N_MLA, D_MODEL, Q_RANK + KV_RANK + ROPE_DIM), D_MODEL)
    inp['a_q_norm'] = near_one((N_MLA, Q_RANK))
    inp['a_kv_norm'] = near_one((N_MLA, KV_RANK))
    inp['a_w_uq'] = nrm((N_MLA, Q_RANK, MLA_HEADS * (NOPE_DIM + ROPE_DIM)), Q_RANK)
    inp['a_w_uk'] = nrm((N_MLA, KV_RANK, MLA_HEADS, NOPE_DIM), KV_RANK)
    inp['a_w_uv'] = nrm((N_MLA, KV_RANK, MLA_HEADS, V_DIM), KV_RANK)
    inp['a_w_o'] = nrm((N_MLA, MLA_HEADS * V_DIM, D_MODEL), MLA_HEADS * V_DIM, DEEPNORM_BETA)
    inp['b_w_in'] = nrm((N_CMLP, D_MODEL, 2 * D_CMLP), D_MODEL)
    inp['b_b_in'] = small((N_CMLP, 2 * D_CMLP))
    inp['b_v_norm_g'] = near_one((N_CMLP, D_CMLP))
    inp['b_v_norm_b'] = small((N_CMLP, D_CMLP))
    inp['b_w_s'] = nrm((N_CMLP, CMLP_HEADS, CHUNK, CHUNK), CHUNK)
    inp['b_b_s'] = near_one((N_CMLP, CMLP_HEADS, CHUNK))
    inp['b_w_out'] = nrm((N_CMLP, D_CMLP, D_MODEL), D_CMLP, DEEPNORM_BETA)
    inp['c_w_in'] = nrm((N_POOLMIX, D_MODEL, D_POOL), D_MODEL)
    inp['c_w_grp'] = nrm((N_POOLMIX, POOL_GROUPS, POOL_GROUP_DIM, POOL_GROUP_DIM), POOL_GROUP_DIM)
    inp['c_scale'] = near_one((N_POOLMIX, D_POOL))
    inp['c_w_out'] = nrm((N_POOLMIX, D_POOL, D_MODEL), D_POOL, DEEPNORM_BETA)
    return inp


def reference(x_prompt, x_sample, cache_kv_latent, cache_k_rope, state_pool, page_table,
              ln_g, ln_b, ffn_w_gate, ffn_w_up, ffn_w_down,
              a_w_in, a_q_norm, a_kv_norm, a_w_uq, a_w_uk, a_w_uv, a_w_o,
              b_w_in, b_b_in, b_v_norm_g, b_v_norm_b, b_w_s, b_b_s, b_w_out,
              c_w_in, c_w_grp, c_scale, c_w_out):
    S_p = x_prompt.shape[1]
    DB, T = x_sample.shape[:2]
    past_len = page_table.shape[1] * PAGE_SIZE
    pos_p = jnp.arange(S_p)
    pos_s = past_len + jnp.arange(T)
    y_p, y_s = x_prompt, x_sample
    ckv_p, kr_p, ckv_s, kr_s, v_s_rows, pool_p, pool_s = [], [], [], [], [], [], []

    for i in range(DEPTH):
        kind = i % N_MIXERS
        j = i // N_MIXERS
        y_p = _post(y_p, 0.5 * _swiglu(y_p, ffn_w_gate[i, 0], ffn_w_up[i, 0], ffn_w_down[i, 0]), ln_g[i, 0], ln_b[i, 0])
        y_s = _post(y_s, 0.5 * _swiglu(y_s, ffn_w_gate[i, 0], ffn_w_up[i, 0], ffn_w_down[i, 0]), ln_g[i, 0], ln_b[i, 0])

        if kind == 0:
            ql, qr, ckv, kr = _mla_project(y_p, pos_p, a_w_in[j], a_q_norm[j], a_kv_norm[j], a_w_uq[j], a_w_uk[j])
            mix_p = _mla_output(_prompt_attention(ql, qr, ckv, kr), a_w_uv[j], a_w_o[j])
            ckv_p.append(ckv)
            kr_p.append(kr)
            ql, qr, ckv, kr = _mla_project(y_s, pos_s, a_w_in[j], a_q_norm[j], a_kv_norm[j], a_w_uq[j], a_w_uk[j])
            past_ckv = cache_kv_latent[j, page_table].reshape(DB, past_len, KV_RANK)
            past_kr = cache_k_rope[j, page_table].reshape(DB, past_len, ROPE_DIM)
            mix_s = _mla_output(_sample_attention(ql, qr, ckv, kr, past_ckv, past_kr), a_w_uv[j], a_w_o[j])
            ckv_s.append(ckv)
            kr_s.append(kr)
        elif kind == 1:
            mix_p, _ = _chunk_mlp(y_p, b_w_in[j], b_b_in[j], b_v_norm_g[j], b_v_norm_b[j], b_w_s[j], b_b_s[j], b_w_out[j])
            mix_s, v_rows = _chunk_mlp(y_s, b_w_in[j], b_b_in[j], b_v_norm_g[j], b_v_norm_b[j], b_w_s[j], b_b_s[j], b_w_out[j])
            v_s_rows.append(v_rows)
        else:
            h_p = y_p @ c_w_in[j]
            mix_p = _pool_tokens(h_p, c_scale[j], c_w_grp[j]) @ c_w_out[j]
            pool_p.append(h_p[:, -POOL_BUF:])
            h_s = jnp.concatenate([state_pool[j], y_s @ c_w_in[j]], axis=1)
            mix_s = _pool_tokens(h_s, c_scale[j], c_w_grp[j])[:, -T:] @ c_w_out[j]
            pool_s.append(h_s[:, -POOL_BUF:])

        y_p = _post(y_p, mix_p, ln_g[i, 1], ln_b[i, 1])
        y_s = _post(y_s, mix_s, ln_g[i, 1], ln_b[i, 1])
        y_p = _post(y_p, 0.5 * _swiglu(y_p, ffn_w_gate[i, 1], ffn_w_up[i, 1], ffn_w_down[i, 1]), ln_g[i, 2], ln_b[i, 2])
        y_s = _post(y_s, 0.5 * _swiglu(y_s, ffn_w_gate[i, 1], ffn_w_up[i, 1], ffn_w_down[i, 1]), ln_g[i, 2], ln_b[i, 2])

    new_kv_latent_prompt = jnp.stack(ckv_p)
    new_k_rope_prompt = jnp.stack(kr_p)
    new_kv_latent_sample = jnp.stack(ckv_s)
    new_k_rope_sample = jnp.stack(kr_s)
    new_chunk_v_sample = jnp.stack(v_s_rows)
    new_pool_prompt = jnp.stack(pool_p)
    new_pool_sample = jnp.stack(pool_s)
    return (y_p, y_s, new_kv_latent_prompt, new_k_rope_prompt, new_kv_latent_sample, new_k_rope_sample, new_chunk_v_sample, new_pool_prompt, new_pool_sample)
```

```python
import numpy as np
import concourse.bass as bass
import concourse.mybir as mybir
from concourse.bass_utils import run_bass_kernel_spmd

F32 = mybir.dt.float32
BF16 = mybir.dt.bfloat16
I32 = mybir.dt.int32
AF = mybir.ActivationFunctionType
ALU = mybir.AluOpType
AX = mybir.AxisListType

D = 1024
KD = 8
DEPTH = 4
ALPHA = (2 * DEPTH) ** 0.25
LN_EPS = 1e-5
RMS_EPS = 1e-6
DFF = 2688
NFF = 21
HEADS = 8
QR = 384
KVR = 256
ROPE = 64
DC = 3072
NSAMP = 16
PAGE = 128
ATT_SCALE = (128 + 64) ** -0.5
NEG = -30000.0
SBUF_BYTES = 212800

ENGS = ("pe", "act", "dve", "pool", "sp")


class Op:
    __slots__ = ("eng", "fn", "deps", "sig", "idx", "pos", "dkey", "dcnt", "waits", "gid")

    def __init__(self, eng, fn):
        self.eng = eng
        self.fn = fn
        self.deps = ()
        self.sig = False
        self.idx = 0
        self.pos = 0
        self.dkey = None
        self.dcnt = 0
        self.waits = ()


class Prog:
    def __init__(self):
        self.ops = {e: [] for e in ENGS}
        self.lastw = {}
        self.readers = {}
        self.dma_tot = {}
        self.dma_eng = {}
        self.n = 0
        self.pending = {}
        self.last_dma = {}
        self.bank_rd = {}

    def fence(self):
        deps = [self.ops[e][-1] for e in ENGS if self.ops[e]]
        deps += list(self.last_dma.values())
        for e in ENGS:
            self.pending[e] = list(deps)

    def add(self, eng, fn, reads=(), writes=(), dkey=None, dinc=16):
        op = Op(eng, fn)
        op.gid = self.n
        self.n += 1
        deps = {}
        for r in reads:
            w = self.lastw.get(r)
            if w is not None:
                deps[id(w)] = w
        for wr in writes:
            w = self.lastw.get(wr)
            if w is not None:
                deps[id(w)] = w
            rd = self.readers.get(wr)
            if rd:
                for o in rd.values():
                    deps[id(o)] = o
        if eng in ("act", "dve"):
            other = "dve" if eng == "act" else "act"
            for r in reads:
                if r[0] == "ps":
                    o = self.bank_rd.get((r[1], other))
                    if o is not None:
                        deps[id(o)] = o
                    self.bank_rd[(r[1], eng)] = op
        pend = self.pending.pop(eng, None)
        if pend:
            for o in pend:
                deps[id(o)] = o
        deps.pop(id(op), None)
        op.deps = tuple(deps.values())
        op.pos = len(self.ops[eng])
        self.ops[eng].append(op)
        if dkey is not None:
            assert self.dma_eng.setdefault(dkey, eng) == eng
            self.dma_tot[dkey] = self.dma_tot.get(dkey, 0) + dinc
            op.dkey = dkey
            op.dcnt = self.dma_tot[dkey]
            self.last_dma[dkey] = op
        for r in reads:
            d = self.readers.setdefault(r, {})
            d[eng if dkey is None else ("dma", id(op))] = op
        for wr in writes:
            self.lastw[wr] = op
            self.readers[wr] = {}
        return op

    def resolve(self):
        for e in ENGS:
            for op in self.ops[e]:
                for d in op.deps:
                    if d.dkey is not None:
                        continue
                    if d.eng == op.eng:
                        if op.eng == "pe" or op.dkey is not None:
                            if op.eng == "pe":
                                continue
                        if op.pos - d.pos > 3 and op.dkey is None:
                            continue
                    d.sig = True
        for e in ENGS:
            c = 0
            for op in self.ops[e]:
                if op.sig:
                    c += 1
                    op.idx = c
        for e in ENGS:
            known = {}
            for op in self.ops[e]:
                need = {}
                for d in op.deps:
                    if d.dkey is not None:
                        k = ("d", d.dkey)
                        v = d.dcnt
                    else:
                        if not d.sig:
                            continue
                        if d.eng == op.eng:
                            if op.eng == "pe":
                                continue
                            if op.pos - d.pos > 3 and op.dkey is None:
                                continue
                        k = ("e", d.eng)
                        v = d.idx
                    if v > need.get(k, 0):
                        need[k] = v
                w = []
                for k, v in need.items():
                    if v > known.get(k, 0):
                        known[k] = v
                        w.append((k, v))
                op.waits = tuple(w)


def _np_bf16():
    import ml_dtypes
    return ml_dtypes.bfloat16


class Cfg:
    def __init__(self, S=2048, NPG=64, NPOOL=10240, layers=(0, 1, 2, 3), mixers=True):
        self.S = S
        self.NT = S // 128
        self.NPG = NPG
        self.NPOOL = NPOOL
        self.TT = S + NSAMP
        self.layers = layers
        self.mixers = mixers


def rope_tables(cfg):
    half = ROPE // 2
    inv = (np.float32(10000.0) ** (-np.arange(half, dtype=np.float32) / np.float32(half))).astype(np.float32)
    pos = np.concatenate([np.arange(cfg.S), np.full(NSAMP, cfg.NPG * PAGE)]).astype(np.float32)
    ang = (pos[:, None] * inv[None, :]).astype(np.float32)
    cos = np.cos(ang).astype(np.float32)
    sin = np.sin(ang).astype(np.float32)
    tm = np.zeros((128, cfg.NT + 1, 64), np.float32)
    for t in range(cfg.NT):
        tm[:, t, :32] = cos[t * 128:(t + 1) * 128]
        tm[:, t, 32:] = sin[t * 128:(t + 1) * 128]
    tm[:NSAMP, cfg.NT, :32] = cos[cfg.S:]
    tm[:NSAMP, cfg.NT, 32:] = sin[cfg.S:]
    fm = np.zeros((64, 2, cfg.TT), np.float32)
    fm[:32, 0] = cos.T
    fm[32:, 0] = cos.T
    fm[:32, 1] = sin.T
    fm[32:, 1] = sin.T
    return tm, fm


def const_inputs(cfg):
    tm, fm = rope_tables(cfg)
    ident = np.eye(128, dtype=np.float32)
    cm = np.where(np.arange(128)[None, :] <= np.arange(128)[:, None], 0.0, NEG).astype(np.float32)
    tril = np.tril(np.ones((128, 128), np.float32))
    sm = np.full((HEADS, NSAMP, NSAMP), NEG, np.float32)
    for b in range(NSAMP):
        sm[:, b, b] = 0.0
    pc = np.ones((128, 8, 15), np.float32)
    for g, w in enumerate((2, 4, 8, 16)):
        for t in range(15):
            pc[:, 2 * g:2 * g + 2, t] = w / min(w, t + 1)
    return {"c_ident": ident, "c_cmask": cm, "c_tril": tril, "c_smask": sm.reshape(HEADS, NSAMP * NSAMP),
            "c_rope_tm": tm, "c_rope_fm": fm, "c_poolc": pc.reshape(128, 120),
            "c_pmod": (np.arange(128) % 16).astype(np.float32).reshape(128, 1)}


class K:
    def __init__(self, cfg):
        self.cfg = cfg
        self.nc = bass.Bass("TRN2", target_bir_lowering=False)
        self.P = Prog()
        self.dram = {}
        self.off = 0
        self.sems = {}

    def din(self, name, shape, dt=F32):
        self.dram[name] = self.nc.dram_tensor(name, list(shape), dt, kind="ExternalInput").ap()
        return self.dram[name]

    def dout(self, name, shape, dt=F32):
        self.dram[name] = self.nc.dram_tensor(name, list(shape), dt, kind="ExternalOutput").ap()
        return self.dram[name]

    def alloc(self, nbytes):
        o = self.off
        self.off += (nbytes + 63) // 64 * 64
        assert self.off <= self.arena_bytes, (self.off, self.arena_bytes, getattr(self, 'phase_kind', '?'), self.phase_off)
        return o

    def view(self, off, shape, dt=F32, parts=128):
        n = int(np.prod(shape))
        esz = 4 if dt in (F32, I32) else 2
        nb = n * esz
        assert off % 4 == 0 and nb % 4 == 0
        ap = self.arena[0:parts, off // 4:(off + nb) // 4]
        if dt != F32:
            ap = ap.bitcast(dt)
        if len(shape) == 2:
            ap = ap.rearrange("p (a b) -> p a b", a=shape[0])
        elif len(shape) == 3:
            ap = ap.rearrange("p (a b c) -> p a b c", a=shape[0], b=shape[1])
        elif len(shape) == 4:
            ap = ap.rearrange("p (a b c d) -> p a b c d", a=shape[0], b=shape[1], c=shape[2])
        return ap

    def buf(self, shape, dt=F32):
        n = int(np.prod(shape))
        esz = 4 if dt in (F32, I32) else 2
        off = self.alloc(n * esz)
        return self.view(off, shape, dt)

    def pe(self, fn, reads=(), writes=()):
        return self.P.add("pe", fn, reads, writes)

    def act(self, fn, reads=(), writes=()):
        return self.P.add("act", fn, reads, writes)

    def dve(self, fn, reads=(), writes=()):
        return self.P.add("dve", fn, reads, writes)

    def dma(self, q, out, in_, key, reads=(), writes=(), **kw):
        return self.P.add(q, lambda e: e.dma_start(out=out, in_=in_, **kw), reads, writes, dkey=key)

    def mm(self, out, lhsT, rhs, start, stop, reads=(), writes=()):
        return self.pe(lambda e: e.matmul(out, lhsT=lhsT, rhs=rhs, start=start, stop=stop), reads, writes)

    def tp(self, out, in_, ident, reads=(), writes=()):
        return self.pe(lambda e: e.transpose(out=out, in_=in_, identity=ident), reads, writes)

    def emit(self):
        P = self.P
        P.resolve()
        nc = self.nc
        esem = self.esem
        dsem = self.dsem
        engobj = {"pe": "tensor", "act": "scalar", "dve": "vector", "pool": "gpsimd", "sp": "sync"}

        def run(ename):
            def body(eng):
                for op in P.ops[ename]:
                    for (k, v) in op.waits:
                        if k[0] == "d":
                            eng.wait_ge(dsem[k[1]], v)
                        else:
                            eng.wait_ge(esem[k[1]], v)
                    ins = op.fn(eng)
                    if op.dkey is not None:
                        ins.then_inc(dsem[op.dkey], 16)
                    elif op.sig:
                        ins.then_inc(esem[ename], 1)
                if ename == "sp":
                    for key, tot in P.dma_tot.items():
                        eng.wait_ge(dsem[key], tot)
            return body

        with nc.Block() as block:
            for ename in ENGS:
                if P.ops[ename] or ename == "sp":
                    getattr(block, engobj[ename])(run(ename))


def _needs(op, d):
    if d.dkey is not None:
        return True
    if d.eng != op.eng:
        return True
    if op.dkey is not None:
        return True
    if op.eng == "pe":
        return False
    return (op.pos - d.pos) <= 3


def _resolve(self):
    for e in ENGS:
        for op in self.ops[e]:
            for d in op.deps:
                if d.dkey is None and _needs(op, d):
                    d.sig = True
    for e in ENGS:
        c = 0
        for op in self.ops[e]:
            if op.sig:
                c += 1
                op.idx = c
    for e in ENGS:
        known = {}
        for op in self.ops[e]:
            need = {}
            for d in op.deps:
                if not _needs(op, d):
                    continue
                if d.dkey is not None:
                    k = ("d", d.dkey)
                    v = d.dcnt
                else:
                    k = ("e", d.eng)
                    v = d.idx
                if v > need.get(k, 0):
                    need[k] = v
            w = []
            for k, v in need.items():
                if v > known.get(k, 0):
                    known[k] = v
                    w.append((k, v))
            op.waits = tuple(w)


Prog.resolve = _resolve


def _stt(k, eng, out, in0, scalar, in1, op0, op1, reads, writes):
    return k.P.add(eng, lambda e: e.scalar_tensor_tensor(out=out, in0=in0, scalar=scalar, in1=in1, op0=op0, op1=op1),
                   reads, writes)


def _tt(k, eng, out, in0, in1, op, reads, writes):
    return k.P.add(eng, lambda e: e.tensor_tensor(out=out, in0=in0, in1=in1, op=op), reads, writes)


def _ts(k, eng, out, in0, s1, s2, op0, op1, reads, writes):
    if s2 is None:
        return k.P.add(eng, lambda e: e.tensor_scalar(out=out, in0=in0, scalar1=s1, scalar2=None, op0=op0), reads, writes)
    return k.P.add(eng, lambda e: e.tensor_scalar(out=out, in0=in0, scalar1=s1, scalar2=s2, op0=op0, op1=op1),
                   reads, writes)


def _cp(k, eng, out, in_, reads, writes):
    if eng == "act":
        return k.P.add(eng, lambda e: e.activation(out=out, in_=in_, func=AF.Copy), reads, writes)
    return k.P.add(eng, lambda e: e.tensor_copy(out=out, in_=in_), reads, writes)


def _af(k, out, in_, func, reads, writes, **kw):
    return k.P.add("act", lambda e: e.activation(out=out, in_=in_, func=func, **kw), reads, writes)


def _red(k, eng, out, in_, op, reads, writes):
    return k.P.add(eng, lambda e: e.tensor_reduce(out=out, in_=in_, axis=AX.X, op=op), reads, writes)


def _recip(k, out, in_, reads, writes):
    return k.P.add("dve", lambda e: e.reciprocal(out=out, in_=in_), reads, writes)


def _memset(k, eng, ap, val, reads, writes):
    return k.P.add(eng, lambda e: e.memset(ap, val), reads, writes)


K.stt = _stt
K.tt = _tt
K.ts = _ts
K.cp = _cp
K.af = _af
K.red = _red
K.recip = _recip
K.memset = _memset


def _xr(self, gi, k, lo=0, hi=512):
    if gi == self.NG:
        return [("xT", gi, 0, k)]
    return [("xT", gi, hp, k) for hp in range(2) if lo < (hp + 1) * 256 and hi > hp * 256]


def _nb(self):
    b = self.bank
    self.bank = (self.bank + 1) % 8
    return b


def _setup(self):
    cfg = self.cfg
    S, NT, TT = cfg.S, cfg.NT, cfg.TT
    self.bank = 0
    self.tiles = [(t, t * 128, 128) for t in range(NT)] + [(NT, S, NSAMP)]
    ng = NT // 4
    self.groups = [(g, g * 512, 512, [self.tiles[4 * g + i] for i in range(4)]) for g in range(ng)]
    self.groups.append((ng, S, NSAMP, [self.tiles[NT]]))
    self.NG = ng
    self.passes = [[self.groups[g]] for g in range(ng)]
    self.passes[-1].append(self.groups[ng])
    self.x = self.buf([NT + 1, D])
    self.xT = self.buf([KD, TT], BF16)
    self.gb = [self.buf([2, D])]
    self.ident = self.buf([128])
    self.identb = self.buf([128], BF16)
    self.st = self.buf([4, 32])
    self.ones_row = self.buf([128], BF16)
    self.gbcnt = 0
    self.qls = self.buf([2, NSAMP, HEADS], BF16)
    self.qrs = self.buf([NSAMP, HEADS], BF16)
    nch = max(cfg.NPG // 8, 1)
    self.idxi = self.buf([NSAMP, nch], I32)
    self.idxf = self.buf([NSAMP, nch])
    self.pmod = self.buf([1])
    self.cmaskb = self.buf([128], BF16)
    self.cmaskf = self.buf([128])
    self.smask = self.buf([NSAMP * NSAMP])
    self.phase_off = self.off
    d = self.dram
    self.dma("sp", self.ident, d["c_ident"], "ident", writes=[("ident",)])
    self.cp("dve", self.identb, self.ident, [("ident",)], [("identb",)])
    self.memset("dve", self.ones_row, 1.0, [], [("ones",)])
    pt3 = d["page_table"].rearrange("b (c k) -> b c k", k=8)
    for k8 in range(8):
        self.dma("sp", self.idxi[16 * k8:16 * k8 + 16, :, :], pt3[:, :, k8].partition_broadcast(16), ("pt", k8),
                 writes=[("idxi", k8)], allow_slow_non_contiguous=True)
    self.dma("sp", self.pmod, d["c_pmod"], "pmod", writes=[("pmod",)])
    self.cp("dve", self.idxf, self.idxi, [("idxi", k8) for k8 in range(8)], [("idxf",)])
    self.ts("dve", self.idxf, self.idxf, 16.0, self.pmod[:, 0:1], ALU.mult, ALU.add, [("idxf",), ("pmod",)], [("idxf",)])
    self.dma("sp", self.cmaskf, d["c_cmask"], "cmask", writes=[("cmaskf",)])
    self.cp("dve", self.cmaskb, self.cmaskf, [("cmaskf",)], [("cmaskb",)])
    self.dma("sp", self.smask[:HEADS, :], d["c_smask"], "smask", writes=[("smask",)])
    xp = d["x_prompt"].rearrange("(t p) d -> p t d", p=128)
    for g in range(ng):
        self.dma("sp", self.x[:, 4 * g:4 * g + 4, :], xp[:, 4 * g:4 * g + 4, :], ("xin", g),
                 writes=[("x", t, h) for t in range(4 * g, 4 * g + 4) for h in range(2)])
    self.dma("sp", self.x[:NSAMP, NT, :], d["x_sample"], ("xin", ng), writes=[("x", NT, 0), ("x", NT, 1)])
    for grp in self.groups:
        self.t_phase(grp)


def _load_gb(self, li, w):
    slot = 0
    d = self.dram
    self.dma("sp", self.gb[slot][:, 0, :], d["ln_g"][li, w, :].partition_broadcast(128), ("gb", slot, 0),
             writes=[("gb", slot, 0)])
    self.dma("sp", self.gb[slot][:, 1, :], d["ln_b"][li, w, :].partition_broadcast(128), ("gb", slot, 1),
             writes=[("gb", slot, 1)])
    return slot


def _ln_tile(self, tile, banks, coef, gslot):
    t, c0, n = tile
    x, ps = self.x, self.ps
    sl = t % 4
    st = self.st
    rx = [("x", t, 0), ("x", t, 1)]
    rs = ("st", sl)
    if banks is not None:
        for hf in range(2):
            self.stt("dve", x[:n, t, hf * 512:(hf + 1) * 512], ps[:n, banks[hf], :], float(coef),
                     x[:n, t, hf * 512:(hf + 1) * 512], ALU.mult, ALU.add,
                     [("ps", banks[hf]), ("x", t, hf)], [("x", t, hf)])
    for hf in range(2):
        self.P.add("dve", (lambda e, hf=hf: e.bn_stats(out=st[:n, sl, hf * 6:hf * 6 + 6],
                                                       in_=x[:n, t, hf * 512:(hf + 1) * 512])),
                   [("x", t, hf)], [("st", sl, hf)])
    self.P.add("dve", lambda e: e.bn_aggr(out=st[:n, sl, 12:14],
                                          in_=st[:n, sl, 0:12].rearrange("p (a b) -> p a b", a=2)),
               [("st", sl, 0), ("st", sl, 1)], [("st", sl, "mv")])
    self.af(st[:n, sl, 14:15], st[:n, sl, 13:14], AF.Sqrt, [("st", sl, "mv")], [("st", sl, "sd")],
            bias=float(LN_EPS / ALPHA ** 2), scale=1.0)
    self.recip(st[:n, sl, 15:16], st[:n, sl, 14:15], [("st", sl, "sd")], [("st", sl, "rs")])
    g = self.gb[gslot]
    for hf in range(2):
        sli = slice(hf * 512, (hf + 1) * 512)
        self.stt("dve", x[:n, t, sli], x[:n, t, sli], st[:n, sl, 12:13], g[:n, 0, sli], ALU.subtract, ALU.mult,
                 [("x", t, hf), ("st", sl, "mv"), ("gb", gslot, 0)], [("x", t, hf)])
    for hf in range(2):
        sli = slice(hf * 512, (hf + 1) * 512)
        self.stt("dve", x[:n, t, sli], x[:n, t, sli], st[:n, sl, 15:16], g[:n, 1, sli], ALU.mult, ALU.add,
                 [("x", t, hf), ("st", sl, "rs"), ("gb", gslot, 1)], [("x", t, hf)])


def _t_phase(self, grp):
    gi, c0, ncols, tiles = grp
    x, ps, xT = self.x, self.ps, self.xT
    for k in range(KD):
        b = self.nb()
        for (t, tc0, n) in tiles:
            off = tc0 - c0
            self.tp(ps[:, b, off:off + n], x[:n, t, k * 128:(k + 1) * 128], self.ident[:n, :n],
                    [("x", t, k // 4), ("ident",)], [("ps", b)])
        g0 = (c0 // 512) * 512 if gi != self.NG else c0
        self.cp("act", xT[:, k, c0:c0 + ncols], ps[:, b, 0:ncols], [("ps", b)], self.xr(gi, k, c0 - g0, c0 - g0 + ncols))


def _ffn(self, li, w, lnw):
    cfg = self.cfg
    d = self.dram
    ps, xT = self.ps, self.xT
    wg = d["ffn_w_gate"][li, w].rearrange("(k p) n -> p k n", p=128)
    wu = d["ffn_w_up"][li, w].rearrange("(k p) n -> p k n", p=128)
    wd = d["ffn_w_down"][li, w].rearrange("(c p) n -> p c n", p=128)
    gslot = self.load_gb(li, lnw)
    coef = 0.5 / ALPHA
    for pgroups in self.passes:
        pc0 = pgroups[0][1]
        for u in range(7):
            slot = self.cnt_gu % 2
            self.cnt_gu += 1
            wt = self.wgu[slot]
            self.dma("pool", wt[:, 0, :, :], wg[:, :, u * 384:(u + 1) * 384], ("wgu", slot, 0), writes=[("wgu", slot, 0)])
            self.dma("pool", wt[:, 1, :, :], wu[:, :, u * 384:(u + 1) * 384], ("wgu", slot, 1), writes=[("wgu", slot, 1)])
            for c in range(3):
                ff = u * 3 + c
                for gidx, (gi, c0, ncols, tiles) in enumerate(pgroups):
                    bA, bB = self.nb(), self.nb()
                    for which, b in ((0, bA), (1, bB)):
                        for k in range(KD):
                            self.mm(ps[:, b, 0:ncols], wt[:, which, k, c * 128:(c + 1) * 128], xT[:, k, c0:c0 + ncols],
                                    k == 0, k == KD - 1, [("wgu", slot, which)] + self.xr(gi, k), [("ps", b)])
                    ss = self.cnt_sg % 2
                    self.cnt_sg += 1
                    self.af(self.sg[ss][:, 0:ncols], ps[:, bA, 0:ncols], AF.Silu, [("ps", bA)], [("sg", ss)])
                    self.tt("dve", self.hT[:, ff, c0 - pc0:c0 - pc0 + ncols], self.sg[ss][:, 0:ncols], ps[:, bB, 0:ncols],
                            ALU.mult, [("sg", ss), ("ps", bB)], [("hT", ff, gidx)])
        for gidx, (gi, c0, ncols, tiles) in enumerate(pgroups):
            for u in range(7):
                slot = self.cnt_wd % 2
                self.cnt_wd += 1
                wt = self.wdb[slot]
                self.dma("pool", wt, wd[:, u * 3:(u + 1) * 3, :], ("wd", slot), writes=[("wd", slot)])
                for i, (t, tc0, n) in enumerate(tiles):
                    for hf in range(2):
                        for c in range(3):
                            ff = u * 3 + c
                            self.mm(ps[:n, 2 * i + hf, :], self.hT[:, ff, tc0 - pc0:tc0 - pc0 + n],
                                    wt[:, c, hf * 512:(hf + 1) * 512], u == 0 and c == 0, u == 6 and c == 2,
                                    [("hT", ff, gidx), ("wd", slot)], [("ps", 2 * i + hf)])
            for i, tile in enumerate(tiles):
                self.ln_tile(tile, (2 * i, 2 * i + 1), coef, gslot)
            self.t_phase((gi, c0, ncols, tiles))


def _alloc_ffn(self):
    if self.phase_kind != "ffn":
        self.P.fence()
    self.phase_kind = "ffn"
    self.off = self.phase_off
    pw = 512 + NSAMP
    self.hT = self.buf([NFF, pw], BF16)
    self.wgu = [self.buf([2, KD, 384], BF16) for _ in range(2)]
    self.wdb = [self.buf([3, D], BF16) for _ in range(2)]
    self.sg = [self.buf([512]) for _ in range(2)]
    self.ffn_end = self.off


def _ln_only(self, li, lnw):
    gslot = self.load_gb(li, lnw)
    for grp in self.groups:
        for tile in grp[3]:
            self.ln_tile(tile, None, 0.0, gslot)
        self.t_phase(grp)


def _finish(self):
    d = self.dram
    cfg = self.cfg
    yp = d["y_prompt"].rearrange("(t p) d -> p t d", p=128)
    for g in range(self.NG):
        self.dma("sp", yp[:, 4 * g:4 * g + 4, :], self.x[:, 4 * g:4 * g + 4, :], ("yout", g),
                 reads=[("x", t, h) for t in range(4 * g, 4 * g + 4) for h in range(2)])
    self.dma("sp", d["y_sample"], self.x[:NSAMP, cfg.NT, :], ("yout", self.NG),
             reads=[("x", cfg.NT, 0), ("x", cfg.NT, 1)])


K.nb = _nb
K.xr = _xr
K.setup = _setup
K.load_gb = _load_gb
K.ln_tile = _ln_tile
K.t_phase = _t_phase
K.ffn = _ffn
K.alloc_ffn = _alloc_ffn
K.ln_only = _ln_only
K.finish = _finish


IN_SPECS = [
    ("x_prompt", lambda c: (c.S, D), F32), ("x_sample", lambda c: (NSAMP, D), F32),
    ("cache_kv_latent", lambda c: (2, c.NPOOL, PAGE, KVR), F32), ("cache_k_rope", lambda c: (2, c.NPOOL, PAGE, ROPE), F32),
    ("state_pool", lambda c: (NSAMP * 15, D), F32), ("page_table", lambda c: (NSAMP, c.NPG), I32),
    ("ln_g", lambda c: (DEPTH, 3, D), F32), ("ln_b", lambda c: (DEPTH, 3, D), F32),
    ("ffn_w_gate", lambda c: (DEPTH, 2, D, DFF), F32), ("ffn_w_up", lambda c: (DEPTH, 2, D, DFF), F32),
    ("ffn_w_down", lambda c: (DEPTH, 2, DFF, D), F32),
    ("a_w_in", lambda c: (2, D, 704), F32), ("a_q_norm", lambda c: (2, QR), F32), ("a_kv_norm", lambda c: (2, KVR), F32),
    ("a_w_uq", lambda c: (2, QR, 1536), F32), ("a_w_uk", lambda c: (2, KVR, HEADS, 128), F32),
    ("a_w_uv", lambda c: (2, KVR, HEADS, 128), F32), ("a_w_o", lambda c: (2, D, D), F32),
    ("b_w_in", lambda c: (1, D, 2 * DC), F32), ("b_b_in", lambda c: (1, 2 * DC), F32),
    ("b_v_norm_g", lambda c: (1, DC), F32), ("b_v_norm_b", lambda c: (1, DC), F32),
    ("b_w_s", lambda c: (1, HEADS, 128, 128), F32), ("b_b_s", lambda c: (1, HEADS, 128), F32),
    ("b_w_out", lambda c: (1, DC, D), F32),
    ("c_w_in", lambda c: (1, D, D), F32), ("c_w_grp", lambda c: (1, 4, 256, 256), F32),
    ("c_scale", lambda c: (1, D), F32), ("c_w_out", lambda c: (1, D, D), F32),
    ("c_ident", lambda c: (128, 128), F32), ("c_cmask", lambda c: (128, 128), F32), ("c_tril", lambda c: (128, 128), F32),
    ("c_smask", lambda c: (HEADS, NSAMP * NSAMP), F32), ("c_rope_tm", lambda c: (128, c.NT + 1, 64), F32),
    ("c_rope_fm", lambda c: (64, 2, c.TT), F32), ("c_poolc", lambda c: (128, 120), F32),
    ("c_pmod", lambda c: (128, 1), F32),
]
OUT_SPECS = [
    ("y_prompt", lambda c: (c.S, D)), ("y_sample", lambda c: (NSAMP, D)),
    ("o_ckv_p", lambda c: (2, c.S, KVR)), ("o_kr_p", lambda c: (2, c.S, ROPE)),
    ("o_ckv_s", lambda c: (2, NSAMP, KVR)), ("o_kr_s", lambda c: (2, NSAMP, ROPE)),
    ("o_v_s", lambda c: (NSAMP, DC)), ("o_pool_p", lambda c: (15, D)), ("o_pool_s", lambda c: (NSAMP, 15, D)),
]
DEBUG = False


def build(cfg):
    from contextlib import ExitStack
    k = K(cfg)
    nc = k.nc
    for name, shp, dt in IN_SPECS:
        k.din(name, shp(cfg), dt)
    for name, shp in OUT_SPECS:
        k.dout(name, shp(cfg))
    if DEBUG:
        k.dout("o_dbg", (128, 8192))
    k.dbgcol = 0
    k.arena_bytes = SBUF_BYTES
    k.cnt_gu = k.cnt_wd = k.cnt_sg = 0
    k.cnt_wv = k.cnt_vg = k.cnt_wu = k.cnt_wo = k.cnt_uc = 0
    k.phase_kind = "ffn"
    with ExitStack() as es:
        k.arena = es.enter_context(nc.sbuf_tensor("arena", [128, SBUF_BYTES // 4], F32))
        k.ps = es.enter_context(nc.psum_tensor("ps", [128, 8, 512], F32))
        k.setup()
        for li in cfg.layers:
            kind = li % 3
            j = li // 3
            k.alloc_ffn()
            k.ffn(li, 0, 0)
            if not cfg.mixers:
                k.ln_only(li, 1)
            elif kind == 0:
                k.mla(li, j)
            elif kind == 1:
                k.cmlp(li)
            else:
                k.poolmix(li)
            k.alloc_ffn()
            k.ffn(li, 1, 2)
        k.finish()
        k.esem = {e: es.enter_context(nc.semaphore("e_" + e)) for e in ENGS}
        k.dsem = {}
        for i, key in enumerate(k.P.dma_tot):
            k.dsem[key] = es.enter_context(nc.semaphore("d%d" % i))
        k.emit()
    return nc, k


def make_in_maps(cfg, inputs, ncores=8):
    consts = const_inputs(cfg)
    maps = []
    for c in range(ncores):
        m = {}
        m["x_prompt"] = np.ascontiguousarray(inputs["x_prompt"][c])
        m["x_sample"] = np.ascontiguousarray(inputs["x_sample"][c * NSAMP:(c + 1) * NSAMP, 0])
        m["cache_kv_latent"] = inputs["cache_kv_latent"]
        m["cache_k_rope"] = inputs["cache_k_rope"]
        m["state_pool"] = np.ascontiguousarray(inputs["state_pool"][0, c * NSAMP:(c + 1) * NSAMP]).reshape(NSAMP * 15, D)
        m["page_table"] = np.ascontiguousarray(inputs["page_table"][c * NSAMP:(c + 1) * NSAMP]).astype(np.int32)
        for name, _, _ in IN_SPECS:
            if name in m:
                continue
            if name in consts:
                m[name] = consts[name]
            else:
                m[name] = np.asarray(inputs[name])
        maps.append(m)
    return maps


def gather_outputs(cfg, results, ncores=8):
    S = cfg.S
    R = results
    y_p = np.stack([R[c]["y_prompt"] for c in range(ncores)])
    y_s = np.concatenate([R[c]["y_sample"] for c in range(ncores)])[:, None, :]
    ckv_p = np.stack([R[c]["o_ckv_p"] for c in range(ncores)], axis=1)
    kr_p = np.stack([R[c]["o_kr_p"] for c in range(ncores)], axis=1)
    ckv_s = np.concatenate([R[c]["o_ckv_s"] for c in range(ncores)], axis=1)[:, :, None, :]
    kr_s = np.concatenate([R[c]["o_kr_s"] for c in range(ncores)], axis=1)[:, :, None, :]
    v_s = np.concatenate([R[c]["o_v_s"] for c in range(ncores)])[None, :, None, :]
    pool_p = np.stack([R[c]["o_pool_p"] for c in range(ncores)])[None]
    pool_s = np.concatenate([R[c]["o_pool_s"] for c in range(ncores)])[None]
    outs = (y_p, y_s, ckv_p, kr_p, ckv_s, kr_s, v_s, pool_p, pool_s)
    return tuple(np.ascontiguousarray(o, dtype=np.float32) for o in outs)


_CACHE = {}


def kernel(**inputs):
    cfg = Cfg()
    inputs = {k_: np.asarray(v) for k_, v in inputs.items()}
    if "nc" not in _CACHE:
        _CACHE["nc"] = build(cfg)[0]
    nc = _CACHE["nc"]
    maps = make_in_maps(cfg, inputs)
    res = run_bass_kernel_spmd(nc, maps, core_ids=list(range(8)))
    return gather_outputs(cfg, res.results)


def _load_colvec(self, dram1d, nk, dst, key, rname):
    tmp = self.buf([128])
    self.dma("sp", tmp[:nk, :], dram1d.rearrange("(k p) -> k p", p=128), key, writes=[(rname, "tmp")])
    b = self.nb()
    self.tp(self.ps[:, b, 0:nk], tmp[:nk, :], self.ident[:nk, :nk], [(rname, "tmp"), ("ident",)], [("ps", b)])
    self.cp("dve", dst, self.ps[:, b, 0:nk], [("ps", b)], [(rname,)])


def _proj_out_ln(self, grp, actT, kc_n, w_sb, wres, act_res, coef, gslot):
    gi, c0, ncols, tiles = grp
    ps = self.ps
    for (t, tc0, n) in tiles:
        b0, b1 = self.nb(), self.nb()
        for hf, b in ((0, b0), (1, b1)):
            for kc in range(kc_n):
                self.mm(ps[:n, b, :], actT[:, kc, tc0 - c0:tc0 - c0 + n], w_sb[:, kc, hf * 512:(hf + 1) * 512],
                        kc == 0, kc == kc_n - 1, [act_res(kc), wres], [("ps", b)])
        self.ln_tile((t, tc0, n), (b0, b1), coef, gslot)
    self.t_phase(grp)


def _poolmix(self, li):
    cfg = self.cfg
    d = self.dram
    S, NT = cfg.S, cfg.NT
    ps, xT = self.ps, self.xT
    self.P.fence()
    self.phase_kind = "pool"
    self.off = self.phase_off
    cwin = self.buf([KD, D], BF16)
    cwout = self.buf([KD, D], BF16)
    wgrp = self.buf([4, 2, 256], BF16)
    cscale = self.buf([8])
    poolc = self.buf([8, 15])
    hb_off = self.off
    hb = [self.buf([8, 527]) for _ in range(2)]
    tmpA = self.buf([2, 527])
    tmpB = self.buf([2, 527])
    plT = self.buf([8, 512], BF16)
    zT = self.buf([8, 512], BF16)
    hrow = self.view(hb_off, [D])
    hsrow = self.view(hb_off + 4096, [D])
    stt_tm = self.view(hb_off + 8192, [2, D])
    hsT = self.view(hb_off + 16384, [8, NSAMP, 16])
    sw = self.view(hb_off + 16384 + 8192, [8, NSAMP])
    assert 16384 + 8192 + 512 <= 2 * 8 * 527 * 4
    gslot = self.load_gb(li, 1)
    coef = 1.0 / ALPHA
    WINS = (2, 4, 8, 16)
    self.dma("pool", cwin, d["c_w_in"][0].rearrange("(k p) n -> p k n", p=128), "cwin", writes=[("cwin",)])
    self.dma("pool", cwout, d["c_w_out"][0].rearrange("(k p) n -> p k n", p=128), "cwout", writes=[("cwout",)])
    self.dma("pool", wgrp, d["c_w_grp"][0].rearrange("g (c p) e -> p g c e", p=128), "wgrp", writes=[("wgrp",)])
    self.dma("sp", poolc, d["c_poolc"].rearrange("p (a b) -> p a b", a=8), "poolc", writes=[("poolc",)])
    self.load_colvec(d["c_scale"][0], 8, cscale, "cscale", "cscale")
    lt = self.tiles[NT - 1]
    for (tile, stage, n) in ((lt, hrow, 128), (self.tiles[NT], hsrow, NSAMP)):
        t, tc0, _ = tile
        gi = min(t // 4, self.NG)
        for hf in range(2):
            b = self.nb()
            for k in range(KD):
                self.mm(ps[:n, b, :], xT[:, k, tc0:tc0 + n], cwin[:, k, hf * 512:(hf + 1) * 512], k == 0, k == KD - 1,
                        self.xr(gi, k) + [("cwin",)], [("ps", b)])
            self.cp("act", stage[:n, hf * 512:(hf + 1) * 512], ps[:n, b, :], [("ps", b)], [("hrow", t, hf)])
    self.dma("sp", d["o_pool_p"], hrow[113:128, :], "o_pool_p", reads=[("hrow", NT - 1, 0), ("hrow", NT - 1, 1)])
    self.dma("sp", d["o_pool_s"][:, 14, :], hsrow[:NSAMP, :], "o_pool_s", reads=[("hrow", NT, 0), ("hrow", NT, 1)])
    sp3 = d["state_pool"].rearrange("(b r) d -> b r d", r=15)
    self.dma("sp", d["o_pool_s"][:, 0:14, :], sp3[:, 1:15, :], "o_pool_s2")
    P_ = self.P
    P_.fence()
    prev = None
    for grp in self.groups:
        gi, c0, ncols, tiles = grp
        samp = gi == self.NG
        cur = gi % 2
        H = hb[cur]
        if samp:
            P_.fence()
            self.dma("sp", stt_tm[:120, :, :], d["state_pool"].rearrange("(a p) d -> p a d", p=120), "stt", writes=[("stt",)])
            for a in range(2):
                for c in range(8):
                    b = self.nb()
                    self.tp(ps[:, b, 0:120], stt_tm[:120, a, c * 128:(c + 1) * 128], self.ident[:120, :120],
                            [("stt",), ("ident",)], [("ps", b)])
                    self.cp("dve", hsT[:, c, 8 * a:8 * a + 8, 0:15], ps[:, b, 0:120].rearrange("p (s r) -> p s r", r=15),
                            [("ps", b)], [("hsT", c, a)])
        if not samp:
            if gi == 0:
                self.memset("dve", H[:, :, 0:15], 0.0, [], [("hb", cur, "halo")])
            else:
                self.cp("act", H[:, :, 0:15], hb[prev][:, :, 512:527], [("hb", prev, c) for c in range(8)],
                        [("hb", cur, "halo")])
        for c in range(8):
            b = self.nb()
            for k in range(KD):
                self.mm(ps[:, b, 0:ncols], cwin[:, k, c * 128:(c + 1) * 128], xT[:, k, c0:c0 + ncols], k == 0, k == KD - 1,
                        [("cwin",)] + self.xr(gi, k), [("ps", b)])
            if samp:
                self.cp("act", hsT[:, c, :, 15], ps[:, b, 0:ncols], [("ps", b)], [("hsT", c, 2)])
            else:
                self.cp("act", H[:, c, 15:527], ps[:, b, 0:512], [("ps", b)], [("hb", cur, c)])
        for g, w in enumerate(WINS):
            cs = slice(2 * g, 2 * g + 2)
            rin = [("hb", cur, 2 * g), ("hb", cur, 2 * g + 1), ("hb", cur, "halo")]
            if samp:
                for c in (2 * g, 2 * g + 1):
                    self.red("dve", sw[:, c, :], hsT[:, c, :, 16 - w:16], ALU.add,
                             [("hsT", c, 0), ("hsT", c, 1), ("hsT", c, 2)], [("sw", c)])
                    self.stt("dve", plT[:, c, 0:NSAMP], sw[:, c, :], 1.0 / w, hsT[:, c, :, 15], ALU.mult, ALU.subtract,
                             [("sw", c), ("hsT", c, 2)], [("plT", c)])
                continue
            src = H[:, cs, :]
            lo = 0
            bufs = (tmpA, tmpB)
            for si in range(g + 1):
                sh = 1 << si
                dst = bufs[si % 2]
                nlo = lo + sh
                self.tt("dve", dst[:, :, nlo:527], src[:, :, nlo:527], src[:, :, nlo - sh:527 - sh], ALU.add,
                        rin + [("ptmp", (si + 1) % 2)], [("ptmp", si % 2)])
                src = dst
                lo = nlo
            last = ("ptmp", g % 2)
            if gi == 0:
                self.tt("dve", src[:, :, 15:30], src[:, :, 15:30], poolc[:, cs, :], ALU.mult, [last, ("poolc",)], [last])
            self.stt("dve", plT[:, cs, :], src[:, :, 15:527], 1.0 / w, H[:, cs, 15:527], ALU.mult, ALU.subtract,
                     [last] + rin, [("plT", 2 * g), ("plT", 2 * g + 1)])
        for g in range(4):
            for ec in range(2):
                b = self.nb()
                for dc in range(2):
                    self.mm(ps[:, b, 0:ncols], wgrp[:, g, dc, ec * 128:(ec + 1) * 128], plT[:, 2 * g + dc, 0:ncols],
                            dc == 0, dc == 1, [("wgrp",), ("plT", 2 * g + dc)], [("ps", b)])
                self.af(zT[:, 2 * g + ec, 0:ncols], ps[:, b, 0:ncols], AF.Identity, [("ps", b), ("cscale",)],
                        [("zT", 2 * g + ec)], scale=cscale[:, 2 * g + ec:2 * g + ec + 1])
        if DEBUG and gi == 0:
            self.dbgmap = {}
            self.dbgmap["cscale"] = self.dbg(cscale, 128, 8, [("cscale",)])
            self.dbgmap["hb0"] = self.dbg(H[:, 0, :], 128, 527, [("hb", cur, 0), ("hb", cur, "halo")])
            self.dbgmap["hb7"] = self.dbg(H[:, 7, :], 128, 527, [("hb", cur, 7), ("hb", cur, "halo")])
            self.dbgmap["pl0"] = self.dbg(plT[:, 0, :], 128, 512, [("plT", 0)])
            self.dbgmap["pl7"] = self.dbg(plT[:, 7, :], 128, 512, [("plT", 7)])
            self.dbgmap["z0"] = self.dbg(zT[:, 0, :], 128, 512, [("zT", 0)])
            self.dbgmap["z7"] = self.dbg(zT[:, 7, :], 128, 512, [("zT", 7)])
            self.dbgmap["cwout"] = self.dbg(cwout[:, 0, :], 128, 1024, [("cwout",)])
        self.proj_out_ln(grp, zT, 8, cwout, ("cwout",), lambda kc: ("zT", kc), coef, gslot)
        prev = cur


def _dbg(self, ap, parts, n, reads):
    col = self.dbgcol
    self.dbgcol += n
    q = "sp" if ap.dtype == F32 else "pool"
    self.dma(q, self.dram["o_dbg"][0:parts, col:col + n], ap, ("dbg", col), reads=reads)
    return col


K.dbg = _dbg
K.load_colvec = _load_colvec
K.proj_out_ln = _proj_out_ln
K.poolmix = _poolmix


def _cmlp(self, li):
    cfg = self.cfg
    d = self.dram
    S, NT = cfg.S, cfg.NT
    ps, xT = self.ps, self.xT
    self.P.fence()
    self.phase_kind = "cmlp"
    self.off = self.phase_off
    winu = [self.buf([KD, 256], BF16) for _ in range(2)]
    winv = [self.buf([KD, 512], BF16) for _ in range(2)]
    binv = [self.buf([512], BF16) for _ in range(2)]
    wout = [self.buf([2, D], BF16) for _ in range(2)]
    vgb = [self.buf([2, 512]) for _ in range(2)]
    vf_off = self.off
    v_f = self.buf([2, DC])
    vn = self.buf([2, DC], BF16)
    uc = [self.buf([256], BF16) for _ in range(2)]
    huc = [self.buf([256], BF16) for _ in range(2)]
    WsT = self.buf([HEADS, 128], BF16)
    wsS = self.buf([HEADS, NSAMP], BF16)
    bsrow = self.buf([HEADS, 128], BF16)
    bsS = self.buf([HEADS, NSAMP], BF16)
    tril = self.buf([128])
    w00 = self.buf([HEADS])
    bs0 = self.buf([HEADS])
    binu = self.buf([24])
    vst = self.buf([2, 48])
    stage = self.view(vf_off, [HEADS, 128])
    gslot = self.load_gb(li, 1)
    coef = 1.0 / ALPHA
    w_in = d["b_w_in"][0].rearrange("(k p) n -> p k n", p=128)
    w_out = d["b_w_out"][0].rearrange("(c p) n -> p c n", p=128)
    b_in = d["b_b_in"][0]
    vfall = [("vf", ti, b) for ti in range(2) for b in range(6)]
    self.dma("sp", stage, d["b_w_s"][0].rearrange("h i j -> i h j"), "ws", writes=vfall)
    self.dma("sp", tril, d["c_tril"], "tril", writes=[("tril",)])
    for h in range(HEADS):
        self.tt("dve", stage[:, h, :], stage[:, h, :], tril, ALU.mult, vfall + [("tril",)], [("wsm", h)])
    for h in range(HEADS):
        b = self.nb()
        self.tp(ps[:, b, 0:128], stage[:, h, :], self.ident, [("wsm", h), ("ident",)], [("ps", b)])
        self.cp("act", WsT[:, h, :], ps[:, b, 0:128], [("ps", b)], [("WsT",)])
    self.P.add("dve", lambda e: e.memset(vst[:, 0, 0:1], 0.0), [("wsm", h) for h in range(HEADS)], vfall)
    self.dma("pool", bsrow[0:1, :, :], d["b_b_s"][0:1], "bsrow", writes=[("bsrow",)])
    self.dma("sp", w00[:NSAMP, :], d["b_w_s"][0][:, 0, 0].partition_broadcast(NSAMP), "w00", writes=[("w00",)],
             allow_slow_non_contiguous=True)
    self.dma("sp", bs0[0:1, :], d["b_b_s"][0:1, :, 0], "bs0", writes=[("bs0",)], allow_slow_non_contiguous=True)
    for h in range(HEADS):
        self.ts("dve", wsS[:NSAMP, h, :], self.ident[:NSAMP, :NSAMP], w00[:NSAMP, h:h + 1], None, ALU.mult, None,
                [("w00",), ("ident",)], [("wsS",)])
        self.ts("dve", bsS[0:1, h, :], self.ones_row[0:1, 0:NSAMP], bs0[0:1, h:h + 1], None, ALU.mult, None,
                [("bs0",), ("ones",)], [("bsS",)])
    self.load_colvec(b_in[0:DC], 24, binu, "binu", "binu")
    plist = []
    for g in range(self.NG):
        for hp in range(2):
            tl = [self.tiles[4 * g + 2 * hp], self.tiles[4 * g + 2 * hp + 1]]
            plist.append((g, g * 512 + hp * 256, 256, tl))
    plist.append((self.NG, S, NSAMP, [self.tiles[NT]]))
    for (gi, c0, ncols, tiles) in plist:
        samp = gi == self.NG
        nt = len(tiles)
        lo = c0 - (gi * 512 if not samp else c0)
        for blk in range(6):
            sl = self.cnt_wv % 2
            self.cnt_wv += 1
            self.dma("pool", winv[sl], w_in[:, :, DC + blk * 512:DC + (blk + 1) * 512], ("winv", sl), writes=[("winv", sl)])
            self.dma("pool", binv[sl][0:1, :], b_in[DC + blk * 512:DC + (blk + 1) * 512].rearrange("(o n) -> o n", o=1),
                     ("binv", sl), writes=[("binv", sl)])
            for ti, (t, tc0, n) in enumerate(tiles):
                b = self.nb()
                for k in range(KD):
                    self.mm(ps[:n, b, :], xT[:, k, tc0:tc0 + n], winv[sl][:, k, :], k == 0, False,
                            self.xr(gi, k, lo, lo + ncols) + [("winv", sl)], [("ps", b)])
                self.mm(ps[:n, b, :], self.ones_row[0:1, 0:n], binv[sl][0:1, :], False, True,
                        [("ones",), ("binv", sl)], [("ps", b)])
                self.af(v_f[:n, ti, blk * 512:(blk + 1) * 512], ps[:n, b, :], AF.Gelu_apprx_tanh, [("ps", b)],
                        [("vf", ti, blk)])
                self.P.add("dve", (lambda e, n=n, ti=ti, blk=blk: e.bn_stats(
                    out=vst[:n, ti, blk * 6:blk * 6 + 6], in_=v_f[:n, ti, blk * 512:(blk + 1) * 512])),
                    [("vf", ti, blk)], [("vst", ti, blk)])
        for ti, (t, tc0, n) in enumerate(tiles):
            self.P.add("dve", (lambda e, n=n, ti=ti: e.bn_aggr(
                out=vst[:n, ti, 36:38], in_=vst[:n, ti, 0:36].rearrange("p (a b) -> p a b", a=6))),
                [("vst", ti, b_) for b_ in range(6)], [("vst", ti, "mv")])
            self.af(vst[:n, ti, 38:39], vst[:n, ti, 37:38], AF.Sqrt, [("vst", ti, "mv")], [("vst", ti, "sd")],
                    bias=float(LN_EPS), scale=1.0)
            self.recip(vst[:n, ti, 39:40], vst[:n, ti, 38:39], [("vst", ti, "sd")], [("vst", ti, "rs")])
        for blk in range(6):
            sl = self.cnt_vg % 2
            self.cnt_vg += 1
            bsl = slice(blk * 512, (blk + 1) * 512)
            self.dma("sp", vgb[sl][:, 0, :], d["b_v_norm_g"][0, bsl].partition_broadcast(128), ("vgb", sl, 0),
                     writes=[("vgb", sl, 0)])
            self.dma("sp", vgb[sl][:, 1, :], d["b_v_norm_b"][0, bsl].partition_broadcast(128), ("vgb", sl, 1),
                     writes=[("vgb", sl, 1)])
            for ti, (t, tc0, n) in enumerate(tiles):
                self.stt("dve", v_f[:n, ti, bsl], v_f[:n, ti, bsl], vst[:n, ti, 36:37], vgb[sl][:n, 0, :],
                         ALU.subtract, ALU.mult, [("vf", ti, blk), ("vst", ti, "mv"), ("vgb", sl, 0)], [("vf", ti, blk)])
                if samp:
                    self.stt("dve", v_f[:n, ti, bsl], v_f[:n, ti, bsl], vst[:n, ti, 39:40], vgb[sl][:n, 1, :],
                             ALU.mult, ALU.add, [("vf", ti, blk), ("vst", ti, "rs"), ("vgb", sl, 1)], [("vf", ti, blk)])
                    self.cp("act", vn[:n, ti, bsl], v_f[:n, ti, bsl], [("vf", ti, blk)], [("vn", ti, blk)])
                else:
                    self.stt("dve", vn[:n, ti, bsl], v_f[:n, ti, bsl], vst[:n, ti, 39:40], vgb[sl][:n, 1, :],
                             ALU.mult, ALU.add, [("vf", ti, blk), ("vst", ti, "rs"), ("vgb", sl, 1)], [("vn", ti, blk)])
        if samp:
            self.dma("sp", d["o_v_s"], v_f[:NSAMP, 0, :], "o_v_s", reads=[("vf", 0, b_) for b_ in range(6)])
        accb = [[self.nb(), self.nb()] for _ in range(nt)]
        used = set(b for pr in accb for b in pr)
        for ch in range(24):
            h = ch // 3
            if ch % 2 == 0:
                su = self.cnt_wu % 2
                self.cnt_wu += 1
                self.dma("pool", winu[su], w_in[:, :, ch * 128:(ch + 2) * 128], ("winu", su), writes=[("winu", su)])
                so = self.cnt_wo % 2
                self.cnt_wo += 1
                self.dma("pool", wout[so], w_out[:, ch:ch + 2, :], ("wout", so), writes=[("wout", so)])
            b = self.nb()
            while b in used:
                b = self.nb()
            for k in range(KD):
                self.mm(ps[:, b, 0:ncols], winu[su][:, k, (ch % 2) * 128:(ch % 2 + 1) * 128], xT[:, k, c0:c0 + ncols],
                        k == 0, k == KD - 1, [("winu", su)] + self.xr(gi, k, lo, lo + ncols), [("ps", b)])
            us = self.cnt_uc % 2
            self.cnt_uc += 1
            self.af(uc[us][:, 0:ncols], ps[:, b, 0:ncols], AF.Gelu_apprx_tanh, [("ps", b), ("binu",)], [("uc", us)],
                    bias=binu[:, ch:ch + 1], scale=1.0)
            b2 = self.nb()
            while b2 in used:
                b2 = self.nb()
            for ti, (t, tc0, n) in enumerate(tiles):
                o = tc0 - c0
                blk = ch // 4
                if samp:
                    self.mm(ps[:, b2, o:o + n], vn[:n, ti, ch * 128:(ch + 1) * 128], wsS[:n, h, :], True, False,
                            [("vn", ti, blk), ("wsS",)], [("ps", b2)])
                    self.mm(ps[:, b2, o:o + n], self.ones_row[0:1, 0:128], bsS[0:1, h, :], False, True,
                            [("ones",), ("bsS",)], [("ps", b2)])
                else:
                    self.mm(ps[:, b2, o:o + n], vn[:n, ti, ch * 128:(ch + 1) * 128], WsT[:, h, :], True, False,
                            [("vn", ti, blk), ("WsT",)], [("ps", b2)])
                    self.mm(ps[:, b2, o:o + n], self.ones_row[0:1, 0:128], bsrow[0:1, h, :], False, True,
                            [("ones",), ("bsrow",)], [("ps", b2)])
            self.tt("dve", huc[us][:, 0:ncols], uc[us][:, 0:ncols], ps[:, b2, 0:ncols], ALU.mult,
                    [("uc", us), ("ps", b2)], [("huc", us)])
            for ti, (t, tc0, n) in enumerate(tiles):
                o = tc0 - c0
                for hf in range(2):
                    self.mm(ps[:n, accb[ti][hf], :], huc[us][:, o:o + n], wout[so][:, ch % 2, hf * 512:(hf + 1) * 512],
                            ch == 0, ch == 23, [("huc", us), ("wout", so)], [("ps", accb[ti][hf])])
        for ti, tile in enumerate(tiles):
            self.ln_tile(tile, accb[ti], coef, gslot)
        self.t_phase((gi, c0, ncols, tiles))


K.cmlp = _cmlp


def _mla(self, li, j):
    cfg = self.cfg
    d = self.dram
    S, NT, TT, NPG = cfg.S, cfg.NT, cfg.TT, cfg.NPG
    ps, xT = self.ps, self.xT
    P = self.P
    P.fence()
    self.phase_kind = "mla"
    self.off = self.phase_off
    gslot = self.load_gb(li, 1)
    coef = 1.0 / ALPHA
    SC = float(ATT_SCALE)
    wuv = self.buf([2, HEADS, 128], BF16)
    wo = self.buf([HEADS, D], BF16)
    qn_c = self.buf([3])
    kvn_bc = self.buf([KVR])
    ast = self.buf([2, 8])
    ckvT_s = self.buf([2, NSAMP], BF16)
    krT_s = self.buf([NSAMP], BF16)
    ckv_tm_s = self.buf([KVR], BF16)
    c_off = self.off
    cqT = self.buf([3, TT], BF16)
    ckvT = self.buf([2, TT], BF16)
    krT = self.buf([TT], BF16)
    ckv_tm = self.buf([NT, KVR], BF16)
    mid_off = self.off
    cs_tm = self.buf([NT + 1, 64])
    self.dma("pool", wuv, d["a_w_uv"][j].rearrange("(c p) h v -> p c h v", p=128), "wuv", writes=[("wuv",)])
    self.dma("pool", wo, d["a_w_o"][j].rearrange("(k p) n -> p k n", p=128), "wo", writes=[("wo",)])
    self.dma("sp", kvn_bc, d["a_kv_norm"][j].partition_broadcast(128), "kvn", writes=[("kvn",)])
    self.dma("sp", cs_tm, d["c_rope_tm"], "cs_tm", writes=[("cs_tm",)])
    self.load_colvec(d["a_q_norm"][j], 3, qn_c, "qn", "qn")
    win = self.buf([KD, 704], BF16)
    cqn = [self.buf([QR]) for _ in range(2)]
    ckvf = [self.buf([KVR]) for _ in range(2)]
    krf = [self.buf([128]) for _ in range(2)]
    for kk in range(2):
        self.memset("dve", krf[kk], 0.0, [], [("krf", kk, 0), ("krf", kk, 1)])
    ktmp = [self.buf([128]) for _ in range(2)]
    junk = self.buf([QR])
    self.dma("pool", win, d["a_w_in"][j].rearrange("(k p) n -> p k n", p=128), "win", writes=[("win",)])
    for (t, tc0, n) in self.tiles:
        samp = t == NT
        gi = min(t // 4, self.NG)
        lo = tc0 - gi * 512 if not samp else 0
        s2 = t % 2
        bq, bk = self.nb(), self.nb()
        for k in range(KD):
            self.mm(ps[:n, bq, 0:QR], xT[:, k, tc0:tc0 + n], win[:, k, 0:QR], k == 0, k == KD - 1,
                    self.xr(gi, k, lo, lo + n) + [("win",)], [("ps", bq)])
        for k in range(KD):
            self.mm(ps[:n, bk, 0:320], xT[:, k, tc0:tc0 + n], win[:, k, QR:704], k == 0, k == KD - 1,
                    self.xr(gi, k, lo, lo + n) + [("win",)], [("ps", bk)])
        self.memset("dve", ast[:n, s2, 0:2], 0.0, [], [("ast", s2, "ss")])
        self.af(junk[:n, :], ps[:n, bq, 0:QR], AF.Square, [("ps", bq), ("ast", s2, "ss")], [("junk",), ("ast", s2, "ssq")],
                accum_out=ast[:n, s2, 0:1])
        self.af(junk[:n, 0:KVR], ps[:n, bk, 0:KVR], AF.Square, [("ps", bk), ("ast", s2, "ss")],
                [("junk",), ("ast", s2, "ssk")], accum_out=ast[:n, s2, 1:2])
        self.af(ast[:n, s2, 2:3], ast[:n, s2, 0:1], AF.Sqrt, [("ast", s2, "ssq")], [("ast", s2, "sdq")],
                bias=float(RMS_EPS), scale=1.0 / QR)
        self.af(ast[:n, s2, 3:4], ast[:n, s2, 1:2], AF.Sqrt, [("ast", s2, "ssk")], [("ast", s2, "sdk")],
                bias=float(RMS_EPS), scale=1.0 / KVR)
        self.recip(ast[:n, s2, 4:6], ast[:n, s2, 2:4], [("ast", s2, "sdq"), ("ast", s2, "sdk")], [("ast", s2, "rs")])
        self.af(cqn[s2][:n, :], ps[:n, bq, 0:QR], AF.Identity, [("ps", bq), ("ast", s2, "rs")], [("cqn", s2)],
                scale=ast[:n, s2, 4:5])
        self.stt("dve", ckvf[s2][:n, :], ps[:n, bk, 0:KVR], ast[:n, s2, 5:6], kvn_bc[:n, :], ALU.mult, ALU.mult,
                 [("ps", bk), ("ast", s2, "rs"), ("kvn",)], [("ckvf", s2)])
        if samp:
            self.dma("sp", d["o_ckv_s"][j], ckvf[s2][:n, :], ("o_ckv", s2), reads=[("ckvf", s2)])
        else:
            self.dma("sp", d["o_ckv_p"][j, tc0:tc0 + n, :], ckvf[s2][:n, :], ("o_ckv", s2), reads=[("ckvf", s2)])
        if samp:
            self.cp("act", ckv_tm_s[:n, :], ckvf[s2][:n, :], [("ckvf", s2)], [("ckv_tm", t)])
        else:
            self.cp("act", ckv_tm[:n, t, :], ckvf[s2][:n, :], [("ckvf", s2)], [("ckv_tm", t)])
        sk = getattr(cfg, "skip", ())
        if "all" in sk:
            continue
        x1 = ps[:n, bk, 256:288]
        x2 = ps[:n, bk, 288:320]
        cosv = cs_tm[:n, t, 0:32]
        sinv = cs_tm[:n, t, 32:64]
        kt_ = ktmp[s2]
        for (o_, a_, b_) in ((0, x1, cosv), (32, x2, sinv), (64, x1, sinv), (96, x2, cosv)):
            self.tt("dve", kt_[:n, o_:o_ + 32], a_, b_, ALU.mult, [("ps", bk), ("cs_tm",)], [("ktmp", s2, o_)])
        self.tt("dve", krf[s2][:n, 0:32], kt_[:n, 0:32], kt_[:n, 32:64], ALU.subtract,
                [("ktmp", s2, 0), ("ktmp", s2, 32)], [("krf", s2, 0)])
        self.tt("dve", krf[s2][:n, 32:64], kt_[:n, 64:96], kt_[:n, 96:128], ALU.add,
                [("ktmp", s2, 64), ("ktmp", s2, 96)], [("krf", s2, 1)])
        if samp:
            self.dma("sp", d["o_kr_s"][j], krf[s2][:n, 0:ROPE], ("o_kr", s2), reads=[("krf", s2, 0), ("krf", s2, 1)])
        else:
            self.dma("sp", d["o_kr_p"][j, tc0:tc0 + n, :], krf[s2][:n, 0:ROPE], ("o_kr", s2),
                     reads=[("krf", s2, 0), ("krf", s2, 1)])
        if "tr" in sk:
            continue
        bX, bY = self.nb(), self.nb()
        for k in range(3):
            self.tp(ps[:, bX, k * 128:k * 128 + n], cqn[s2][:n, k * 128:(k + 1) * 128], self.ident[:n, :n],
                    [("cqn", s2), ("ident",)], [("ps", bX)])
        for c in range(2):
            self.tp(ps[:, bY, c * 128:c * 128 + n], ckvf[s2][:n, c * 128:(c + 1) * 128], self.ident[:n, :n],
                    [("ckvf", s2), ("ident",)], [("ps", bY)])
        self.tp(ps[:, bY, 256:256 + n], krf[s2][:n, :], self.ident[:n, :n],
                [("krf", s2, 0), ("krf", s2, 1), ("ident",)], [("ps", bY)])
        for k in range(3):
            if "e1" in sk:
                continue
            self.af(cqT[:, k, tc0:tc0 + n], ps[:, bX, k * 128:k * 128 + n], AF.Identity, [("ps", bX), ("qn",)],
                    [("cqT", t, k)], scale=qn_c[:, k:k + 1])
        kdst = ckvT_s[:, :, 0:n] if samp else ckvT[:, :, tc0:tc0 + n]
        rdst = krT_s[:64, 0:n] if samp else krT[:64, tc0:tc0 + n]
        if "e2" not in sk:
            self.cp("dve", kdst, ps[:, bY, 0:256].rearrange("p (c q) -> p c q", c=2)[:, :, 0:n],
                    [("ps", bY)], [("ckvT", t)])
        if "e3" not in sk:
            self.cp("act", rdst, ps[:64, bY, 256:256 + n], [("ps", bY)], [("krT", t)])
    stop = getattr(cfg, "stop", "")
    if stop == "A":
        for grp in self.groups:
            for tile in grp[3]:
                self.ln_tile(tile, None, 0.0, gslot)
            self.t_phase(grp)
        return
    P.fence()
    self.off = mid_off
    wuq = self.buf([3, 1536], BF16)
    wqs = self.buf([3, HEADS, ROPE], BF16)
    wukT = self.buf([HEADS, KVR], BF16)
    csT = self.buf([2, 128])
    qnT = [self.buf([128], BF16) for _ in range(2)]
    qlT = self.buf([HEADS, 2, 128], BF16)
    qrT = self.buf([HEADS, 128], BF16)
    t12 = [self.buf([2, 128]) for _ in range(2)]
    pb_off = self.off
    pb = self.buf([max(NT * 128, 2048)], BF16)
    wuk = self.view(pb_off, [2, HEADS, 128], BF16)
    pT = self.buf([NT * 128], BF16)
    ol = [self.buf([KVR], BF16) for _ in range(2)]
    olT = [self.buf([KVR], BF16) for _ in range(2)]
    oT = [self.buf([HEADS, 128], BF16) for _ in range(2)]
    ats = self.buf([2, 8])
    self.dma("pool", wuq, d["a_w_uq"][j].rearrange("(k p) n -> p k n", p=128), "wuq", writes=[("wuq",)])
    self.dma("pool", wuk, d["a_w_uk"][j].rearrange("(c p) h d -> p c h d", p=128), "wuk", writes=[("pb",)])
    wr = wuq.rearrange("p k (h e) -> p k h e", h=HEADS)
    self.ts("dve", wqs[:, :, :, 0:32], wr[:, :, :, 160:192], -1.0, None, ALU.mult, None, [("wuq",)], [("wqs", 0)])
    self.cp("dve", wqs[:, :, :, 32:64], wr[:, :, :, 128:160], [("wuq",)], [("wqs", 1)])
    for h in range(HEADS):
        b = 7
        pv = ps[:, b, 0:128].bitcast(BF16)
        for cc in range(2):
            self.tp(pv[:, cc * 128:(cc + 1) * 128], wuk[:, cc, h, :], self.identb, [("pb",), ("identb",)], [("ps", b)])
        self.cp("dve", wukT[:, h, :], pv, [("ps", b)], [("wukT",)])
    pTps = ps[:, 4:6, :].rearrange("p a b -> p (a b)").bitcast(BF16)
    olTps = ps[:, 7, 0:128].bitcast(BF16)
    sall = ps[:, 0:4, :].rearrange("p a b -> p (a b)")
    for (t, tc0, n) in self.tiles:
        samp = t == NT
        gi = min(t // 4, self.NG)
        self.dma("sp", csT[:64, :, 0:n], d["c_rope_fm"][:, :, tc0:tc0 + n], "csT", writes=[("csT",)])
        os_ = t % 2
        for h in range(HEADS):
            s2 = h % 2
            b = 4 + (h % 2)
            for k in range(3):
                self.mm(ps[:, b, 0:n], wuq[:, k, h * 192:h * 192 + 128], cqT[:, k, tc0:tc0 + n], k == 0, k == 2,
                        [("wuq",), ("cqT", t, k)], [("ps", b)])
            self.cp("act", qnT[s2][:, 0:n], ps[:, b, 0:n], [("ps", b)], [("qnT", s2)])
            b2 = 6
            for cc in range(2):
                self.mm(ps[:, b2, cc * 128:cc * 128 + n], wukT[:, h, cc * 128:(cc + 1) * 128], qnT[s2][:, 0:n], True, True,
                        [("wukT",), ("qnT", s2)], [("ps", b2, "ql")])
            src = ps[:, b2, 0:256].rearrange("p (c q) -> p c q", c=2)[:, :, 0:n]
            if samp:
                self.cp("dve", self.qls[:, :, :, h], src, [("ps", b2, "ql")], [("qls", h)])
            else:
                self.cp("dve", qlT[:, h, :, 0:n], src, [("ps", b2, "ql")], [("qlT", h)])
            b3 = 7
            for k in range(3):
                self.mm(ps[:64, b3, 0:n], wuq[:, k, h * 192 + 128:h * 192 + 192], cqT[:, k, tc0:tc0 + n], k == 0, k == 2,
                        [("wuq",), ("cqT", t, k)], [("ps", b3)])
            for k in range(3):
                self.mm(ps[:64, b3, 128:128 + n], wqs[:, k, h, :], cqT[:, k, tc0:tc0 + n], k == 0, k == 2,
                        [("wqs", 0), ("wqs", 1), ("cqT", t, k)], [("ps", b3)])
            self.tt("dve", t12[s2][:64, 0, 0:n], ps[:64, b3, 0:n], csT[:64, 0, 0:n], ALU.mult,
                    [("ps", b3), ("csT",)], [("t12", s2, 0)])
            self.tt("dve", t12[s2][:64, 1, 0:n], ps[:64, b3, 128:128 + n], csT[:64, 1, 0:n], ALU.mult,
                    [("ps", b3), ("csT",)], [("t12", s2, 1)])
            if samp:
                self.tt("dve", self.qrs[:64, :, h], t12[s2][:64, 0, 0:n], t12[s2][:64, 1, 0:n], ALU.add,
                        [("t12", s2, 0), ("t12", s2, 1)], [("qrs", h)])
            else:
                self.tt("dve", qrT[:64, h, 0:n], t12[s2][:64, 0, 0:n], t12[s2][:64, 1, 0:n], ALU.add,
                        [("t12", s2, 0), ("t12", s2, 1)], [("qrT", h)])
        if samp:
            continue
        if stop == "B0":
            self.ln_tile((t, tc0, n), None, 0.0, gslot)
            self.t_phase((gi, tc0, n, [(t, tc0, n)]))
            continue
        nk = t + 1
        nkeys = nk * 128
        for h in range(HEADS):
            a2 = h % 2
            nch = (nkeys + 511) // 512
            for kc in range(nch):
                k0 = kc * 512
                w = min(512, nkeys - k0)
                diag = kc == nch - 1
                rk = [("ckvT", tt_) for tt_ in range(k0 // 128, (k0 + w) // 128)]
                rr = [("krT", tt_) for tt_ in range(k0 // 128, (k0 + w) // 128)]
                self.mm(ps[:, kc, 0:w], qlT[:, h, 0, :], ckvT[:, 0, k0:k0 + w], True, False, [("qlT", h)] + rk, [("ps", kc)])
                self.mm(ps[:, kc, 0:w], qlT[:, h, 1, :], ckvT[:, 1, k0:k0 + w], False, False, [("qlT", h)] + rk, [("ps", kc)])
                self.mm(ps[:, kc, 0:w], qrT[:64, h, :], krT[:64, k0:k0 + w], False, not diag, [("qrT", h)] + rr, [("ps", kc)])
                if diag:
                    self.mm(ps[:, kc, w - 128:w], self.identb, self.cmaskb, False, True, [("identb",), ("cmaskb",)],
                            [("ps", kc)])
            rps = [("ps", kc) for kc in range(nch)]
            self.red("dve", ats[:, a2, 0:1], sall[:, 0:nkeys], ALU.max, rps, [("ats", a2, "mx")])
            self.ts("dve", ats[:, a2, 1:2], ats[:, a2, 0:1], -SC, None, ALU.mult, None, [("ats", a2, "mx")], [("ats", a2, "nm")])
            self.memset("dve", ats[:, a2, 2:3], 0.0, [], [("ats", a2, "l")])
            self.af(pb[:, 0:nkeys], sall[:, 0:nkeys], AF.Exp, rps + [("ats", a2, "nm"), ("ats", a2, "l")],
                    [("pb",), ("ats", a2, "l")], bias=ats[:, a2, 1:2], scale=SC, accum_out=ats[:, a2, 2:3])
            self.recip(ats[:, a2, 3:4], ats[:, a2, 2:3], [("ats", a2, "l")], [("ats", a2, "ri")])
            for kt in range(nk):
                self.tp(pTps[:, kt * 128:(kt + 1) * 128], pb[:, kt * 128:(kt + 1) * 128], self.identb,
                        [("pb",), ("identb",)], [("ps", 4 + kt // 8)])
            hk = (nk + 1) // 2
            self.cp("act", pT[:, 0:hk * 128], pTps[:, 0:hk * 128], [("ps", 4), ("ps", 5)], [("pT", 0)])
            if nk > hk:
                self.cp("dve", pT[:, hk * 128:nkeys], pTps[:, hk * 128:nkeys], [("ps", 4), ("ps", 5)], [("pT", 1)])
            for kt in range(nk):
                self.mm(ps[:, 6, 0:KVR], pT[:, kt * 128:(kt + 1) * 128], ckv_tm[:, kt, :], kt == 0, kt == nk - 1,
                        [("pT", 0), ("pT", 1), ("ckv_tm", kt)], [("ps", 6, "pv")])
            self.af(ol[a2], ps[:, 6, 0:KVR], AF.Identity, [("ps", 6, "pv"), ("ats", a2, "ri")], [("ol", a2)],
                    scale=ats[:, a2, 3:4])
            for cc in range(2):
                self.tp(olTps[:, cc * 128:(cc + 1) * 128], ol[a2][:, cc * 128:(cc + 1) * 128], self.identb,
                        [("ol", a2), ("identb",)], [("ps", 7)])
            self.cp("dve", olT[a2], olTps, [("ps", 7)], [("olT", a2)])
            for cc in range(2):
                self.mm(ps[:, 6, 256:384], wuv[:, cc, h, :], olT[a2][:, cc * 128:(cc + 1) * 128], cc == 0, cc == 1,
                        [("wuv",), ("olT", a2)], [("ps", 6, "o")])
            self.cp("act", oT[os_][:, h, :], ps[:, 6, 256:384], [("ps", 6, "o")], [("oT", os_, h)])
        for hf in range(2):
            for h in range(HEADS):
                self.mm(ps[:, hf, :], oT[os_][:, h, :], wo[:, h, hf * 512:(hf + 1) * 512], h == 0, h == HEADS - 1,
                        [("oT", os_, h), ("wo",)], [("ps", hf)])
        self.ln_tile((t, tc0, n), (0, 1), coef, gslot)
        self.t_phase((gi, tc0, n, [(t, tc0, n)]))
    if getattr(cfg, "skip_c", False):
        self.ln_tile(self.tiles[NT], None, 0.0, gslot)
        self.t_phase(self.groups[self.NG])
        return
    P.fence()
    self.off = c_off
    NCH = NPG // 8
    NC = NCH + 1
    kvbk = [self.buf([8, KVR], BF16) for _ in range(3)]
    kvbr = [self.buf([8, ROPE], BF16) for _ in range(3)]
    idxt = self.buf([NSAMP, NCH])
    idx2 = self.buf([NSAMP, NCH], I32)
    self.ts("dve", idxt, self.idxf, float(j * cfg.NPOOL * 16), None, ALU.add, None, [("idxf",)], [("idxt",)])
    self.cp("dve", idx2, idxt, [("idxt",)], [("idx2",)])
    ckv_blk = d["cache_kv_latent"].rearrange("l n (g r) c -> (l n g) (r c)", r=8)
    kr_blk = d["cache_k_rope"].rearrange("l n (g r) c -> (l n g) (r c)", r=8)
    KT = [self.buf([2, 1024], BF16) for _ in range(2)]
    KrT = [self.buf([1024], BF16) for _ in range(2)]
    p_s = [self.buf([1024], BF16) for _ in range(2)]
    pT_s = [self.buf([64], BF16) for _ in range(2)]
    ssb = self.buf([NSAMP])
    mst = [self.buf([16]) for _ in range(2)]
    nmst = [self.buf([16]) for _ in range(2)]
    lst = [self.buf([16]) for _ in range(2)]
    wst = [self.buf([16]) for _ in range(2)]
    cst = [self.buf([8]) for _ in range(2)]
    ost1 = self.buf([NC, KVR])
    ost = [ost1, ost1]
    acc = [self.buf([KVR]) for _ in range(2)]
    ol_s = [self.buf([KVR], BF16) for _ in range(2)]
    olT_s = self.buf([2, HEADS, NSAMP], BF16)
    oT_s = self.buf([HEADS, NSAMP], BF16)
    ckv_d = d["cache_kv_latent"]
    kr_d = d["cache_k_rope"]
    ckv_fl = ckv_d.rearrange("l n p c -> (l n) p c")
    kr_fl = kr_d.rearrange("l n p c -> (l n) p c")
    dsem = lambda key: self.dsem[key]
    bkA, bkB, bkC = 0, 1, 2
    kA = ps[:, bkA, :].bitcast(BF16)
    kB = ps[:, bkB, :].bitcast(BF16)
    kC = ps[:, bkC, :].bitcast(BF16)
    pTs_ps = ps[:, 3, 0:32].bitcast(BF16)
    olTs_ps = ps[:, 3, 64:72].bitcast(BF16)
    s_ps = ps[:, 6:8, :].rearrange("p a b -> p (a b)")
    HS = HEADS
    cnt = 0
    smask3 = self.smask.rearrange("p (b k) -> p b k", b=NSAMP)
    for bsm in range(NSAMP):
        sb = bsm % 2
        self.memset("dve", lst[sb][:HS, :], 0.0, [], [("lst", sb, c) for c in range(NC)])
        for ci in range(NCH):
            slot = cnt % 3
            ks = cnt % 2
            cnt += 1
            self.gather(kvbk[slot].rearrange("p s c -> p (s c)"), ckv_blk, idx2[:, bsm, ci:ci + 1], ("kvk", slot))
            self.gather(kvbr[slot].rearrange("p s c -> p (s c)"), kr_blk, idx2[:, bsm, ci:ci + 1], ("kvr", slot))
            for pg in range(8):
                self.tp(kA[:, pg * 128:(pg + 1) * 128], kvbk[slot][:, pg, 0:128], self.identb, [("kvk", slot), ("identb",)],
                        [("ps", bkA)])
                self.tp(kB[:, pg * 128:(pg + 1) * 128], kvbk[slot][:, pg, 128:256], self.identb, [("kvk", slot), ("identb",)],
                        [("ps", bkB)])
                self.tp(kC[:64, pg * 128:(pg + 1) * 128], kvbr[slot][:, pg, :], self.identb,
                        [("kvr", slot), ("identb",)], [("ps", bkC)])
            self.cp("act", KT[ks][:, 0, :], kA, [("ps", bkA)], [("KT", ks, 0)])
            self.cp("dve", KT[ks][:, 1, :], kB, [("ps", bkB)], [("KT", ks, 1)])
            self.cp("act", KrT[ks][:64, :], kC[:64, :], [("ps", bkC)], [("KrT", ks)])
            for hh in range(2):
                bs_ = 6 + hh
                ksl = slice(hh * 512, (hh + 1) * 512)
                self.mm(ps[:HS, bs_, :], self.qls[:, 0, bsm, :], KT[ks][:, 0, ksl], True, False,
                        [("qls", h_) for h_ in range(HS)] + [("KT", ks, 0)], [("ps", bs_)])
                self.mm(ps[:HS, bs_, :], self.qls[:, 1, bsm, :], KT[ks][:, 1, ksl], False, False,
                        [("KT", ks, 1)], [("ps", bs_)])
                self.mm(ps[:HS, bs_, :], self.qrs[:64, bsm, :], KrT[ks][:64, ksl], False, True,
                        [("qrs", h_) for h_ in range(HS)] + [("KrT", ks)], [("ps", bs_)])
            self.red("dve", mst[sb][:HS, ci:ci + 1], s_ps[:HS, :], ALU.max, [("ps", 6), ("ps", 7)], [("mst", sb, ci)])
            self.ts("dve", nmst[sb][:HS, ci:ci + 1], mst[sb][:HS, ci:ci + 1], -SC, None, ALU.mult, None,
                    [("mst", sb, ci)], [("nmst", sb, ci)])
            self.af(p_s[ks][:HS, :], s_ps[:HS, :], AF.Exp, [("ps", 6), ("ps", 7), ("nmst", sb, ci), ("lst", sb, ci)],
                    [("p_s", ks), ("lst", sb, ci)], bias=nmst[sb][:HS, ci:ci + 1], scale=SC,
                    accum_out=lst[sb][:HS, ci:ci + 1])
            for pg in range(8):
                self.tp(pTs_ps[:, pg * 8:(pg + 1) * 8], p_s[ks][:HS, pg * 128:(pg + 1) * 128], self.identb[:HS, :HS],
                        [("p_s", ks), ("identb",)], [("ps", 3, "pT")])
            self.cp("dve", pT_s[ks], pTs_ps, [("ps", 3, "pT")], [("pT_s", ks)])
            for pg in range(8):
                self.mm(ps[:HS, 4, 0:KVR], pT_s[ks][:, pg * 8:(pg + 1) * 8], kvbk[slot][:, pg, :], pg == 0, pg == 7,
                        [("pT_s", ks), ("kvk", slot)], [("ps", 4)])
            self.cp("act", ost[sb][:HS, ci, :], ps[:HS, 4, 0:KVR], [("ps", 4)], [("ost", 0, ci)])
        ci = NCH
        ks = cnt % 2
        rself = [("ckvT", NT), ("krT", NT)]
        self.mm(ps[:HS, 5, 0:NSAMP], self.qls[:, 0, bsm, :], ckvT_s[:, 0, :], True, False,
                [("qls", h_) for h_ in range(HS)] + rself, [("ps", 5)])
        self.mm(ps[:HS, 5, 0:NSAMP], self.qls[:, 1, bsm, :], ckvT_s[:, 1, :], False, False, rself, [("ps", 5)])
        self.mm(ps[:HS, 5, 0:NSAMP], self.qrs[:64, bsm, :], krT_s[:64, :], False, True,
                [("qrs", h_) for h_ in range(HS)] + rself, [("ps", 5)])
        self.tt("dve", ssb[:HS, :], ps[:HS, 5, 0:NSAMP], smask3[:HS, bsm, :], ALU.add, [("ps", 5), ("smask",)], [("ssb",)])
        self.red("dve", mst[sb][:HS, ci:ci + 1], ssb[:HS, :], ALU.max, [("ssb",)], [("mst", sb, ci)])
        self.ts("dve", nmst[sb][:HS, ci:ci + 1], mst[sb][:HS, ci:ci + 1], -SC, None, ALU.mult, None,
                [("mst", sb, ci)], [("nmst", sb, ci)])
        self.af(p_s[ks][:HS, 0:NSAMP], ssb[:HS, :], AF.Exp, [("ssb",), ("nmst", sb, ci), ("lst", sb, ci)],
                [("p_s", ks), ("lst", sb, ci)], bias=nmst[sb][:HS, ci:ci + 1], scale=SC,
                accum_out=lst[sb][:HS, ci:ci + 1])
        self.tp(pTs_ps[:NSAMP, 0:8], p_s[ks][:HS, 0:NSAMP], self.identb[:HS, :HS], [("p_s", ks), ("identb",)],
                [("ps", 3, "pT")])
        self.cp("dve", pT_s[ks][:NSAMP, 0:8], pTs_ps[:NSAMP, 0:8], [("ps", 3, "pT")], [("pT_s", ks)])
        self.mm(ps[:HS, 4, 0:KVR], pT_s[ks][:NSAMP, 0:8], ckv_tm_s[:NSAMP, :], True, True,
                [("pT_s", ks), ("ckv_tm", NT)], [("ps", 4)])
        self.cp("act", ost[sb][:HS, ci, :], ps[:HS, 4, 0:KVR], [("ps", 4)], [("ost", 0, ci)])
        cnt += 1
        allm = [("mst", sb, c) for c in range(NC)]
        self.red("dve", cst[sb][:HS, 0:1], mst[sb][:HS, 0:NC], ALU.max, allm, [("cst", sb, 0)])
        self.ts("dve", cst[sb][:HS, 1:2], cst[sb][:HS, 0:1], -SC, None, ALU.mult, None, [("cst", sb, 0)], [("cst", sb, 1)])
        self.af(wst[sb][:HS, 0:NC], mst[sb][:HS, 0:NC], AF.Exp, allm + [("cst", sb, 1)], [("wst", sb)],
                bias=cst[sb][:HS, 1:2], scale=SC)
        self.tt("dve", nmst[sb][:HS, 0:NC], wst[sb][:HS, 0:NC], lst[sb][:HS, 0:NC], ALU.mult,
                [("wst", sb)] + [("lst", sb, c) for c in range(NC)], [("nmst", sb, c) for c in range(NC)])
        self.red("dve", cst[sb][:HS, 2:3], nmst[sb][:HS, 0:NC], ALU.add, [("nmst", sb, c) for c in range(NC)],
                 [("cst", sb, 2)])
        self.recip(cst[sb][:HS, 3:4], cst[sb][:HS, 2:3], [("cst", sb, 2)], [("cst", sb, 3)])
        self.ts("dve", acc[sb][:HS, :], ost[sb][:HS, 0, :], wst[sb][:HS, 0:1], None, ALU.mult, None,
                [("ost", 0, 0), ("wst", sb)], [("acc", sb)])
        for c in range(1, NC):
            self.stt("dve", acc[sb][:HS, :], ost[sb][:HS, c, :], wst[sb][:HS, c:c + 1], acc[sb][:HS, :], ALU.mult, ALU.add,
                     [("ost", 0, c), ("wst", sb), ("acc", sb)], [("acc", sb)])
        self.af(ol_s[sb][:HS, :], acc[sb][:HS, :], AF.Identity, [("acc", sb), ("cst", sb, 3)], [("ol_s", sb)],
                scale=cst[sb][:HS, 3:4])
        for cc in range(2):
            self.tp(olTs_ps[:, cc * 8:(cc + 1) * 8], ol_s[sb][:HS, cc * 128:(cc + 1) * 128], self.identb[:HS, :HS],
                    [("ol_s", sb), ("identb",)], [("ps", 3, "olT")])
        self.cp("dve", olT_s[:, :, :, bsm], olTs_ps.rearrange("p (c h) -> p c h", c=2), [("ps", 3, "olT")],
                [("olT_s", bsm)])
    allo = [("olT_s", b_) for b_ in range(NSAMP)]
    for h in range(HEADS):
        b = self.nb()
        for cc in range(2):
            self.mm(ps[:, b, 0:NSAMP], wuv[:, cc, h, :], olT_s[:, cc, h, :], cc == 0, cc == 1, [("wuv",)] + allo, [("ps", b)])
        self.cp("act", oT_s[:, h, :], ps[:, b, 0:NSAMP], [("ps", b)], [("oT_s", h)])
    b0, b1 = self.nb(), self.nb()
    for hf, b in ((0, b0), (1, b1)):
        for h in range(HEADS):
            self.mm(ps[:NSAMP, b, :], oT_s[:, h, :], wo[:, h, hf * 512:(hf + 1) * 512], h == 0, h == HEADS - 1,
                    [("oT_s", h), ("wo",)], [("ps", b)])
    st_ = self.tiles[NT]
    self.ln_tile(st_, (b0, b1), coef, gslot)
    self.t_phase(self.groups[self.NG])


def _gather(self, out, src, idx, key):
    self.P.add("pool", lambda e: e.indirect_dma_start(out=out, out_offset=None, in_=src,
                                                      in_offset=bass.IndirectOffsetOnAxis(ap=idx, axis=0)),
               [("idx2",)], [key], dkey=key)


K.gather = _gather
K.mla = _mla
```

```python
import numpy as np
import concourse.bass as bass
import concourse.mybir as mybir
from concourse.bass_utils import run_bass_kernel_spmd

F32 = mybir.dt.float32
BF16 = mybir.dt.bfloat16
I32 = mybir.dt.int32
AF = mybir.ActivationFunctionType
ALU = mybir.AluOpType
AX = mybir.AxisListType

D = 1024
KD = 8
DEPTH = 4
ALPHA = (2 * DEPTH) ** 0.25
LN_EPS = 1e-5
RMS_EPS = 1e-6
DFF = 2688
NFF = 21
HEADS = 8
QR = 384
KVR = 256
ROPE = 64
DC = 3072
NSAMP = 16
PAGE = 128
ATT_SCALE = (128 + 64) ** -0.5
NEG = -30000.0
SBUF_BYTES = 212800

ENGS = ("pe", "act", "dve", "pool", "sp")


class Op:
    __slots__ = ("eng", "fn", "deps", "sig", "idx", "pos", "dkey", "dcnt", "waits", "gid", "ph")

    def __init__(self, eng, fn):
        self.eng = eng
        self.fn = fn
        self.deps = ()
        self.sig = False
        self.idx = 0
        self.pos = 0
        self.dkey = None
        self.dcnt = 0
        self.waits = ()


class Prog:
    def __init__(self):
        self.ops = {e: [] for e in ENGS}
        self.lastw = {}
        self.readers = {}
        self.dma_tot = {}
        self.dma_eng = {}
        self.n = 0
        self.pending = {}
        self.last_dma = {}
        self.bank_rd = {}
        self.phase = "setup"

    def fence(self):
        deps = [self.ops[e][-1] for e in ENGS if self.ops[e]]
        deps += list(self.last_dma.values())
        for e in ENGS:
            self.pending[e] = list(deps)

    def add(self, eng, fn, reads=(), writes=(), dkey=None, dinc=16):
        op = Op(eng, fn)
        op.ph = self.phase
        op.gid = self.n
        self.n += 1
        deps = {}
        for r in reads:
            w = self.lastw.get(r)
            if w is not None:
                deps[id(w)] = w
        for wr in writes:
            w = self.lastw.get(wr)
            if w is not None:
                deps[id(w)] = w
            rd = self.readers.get(wr)
            if rd:
                for o in rd.values():
                    deps[id(o)] = o
        if eng in ("act", "dve"):
            other = "dve" if eng == "act" else "act"
            for r in reads:
                if r[0] == "ps":
                    o = self.bank_rd.get((r[1], other))
                    if o is not None:
                        deps[id(o)] = o
                    self.bank_rd[(r[1], eng)] = op
        if eng == "pe":
            for wr in writes:
                if wr[0] == "ps":
                    for other in ("act", "dve"):
                        o = self.bank_rd.get((wr[1], other))
                        if o is not None:
                            deps[id(o)] = o
        pend = self.pending.pop(eng, None)
        if pend:
            for o in pend:
                deps[id(o)] = o
        deps.pop(id(op), None)
        op.deps = tuple(deps.values())
        op.pos = len(self.ops[eng])
        self.ops[eng].append(op)
        if dkey is not None:
            assert self.dma_eng.setdefault(dkey, eng) == eng
            self.dma_tot[dkey] = self.dma_tot.get(dkey, 0) + dinc
            op.dkey = dkey
            op.dcnt = self.dma_tot[dkey]
            self.last_dma[dkey] = op
        for r in reads:
            d = self.readers.setdefault(r, {})
            d[eng if dkey is None else ("dma", id(op))] = op
        for wr in writes:
            self.lastw[wr] = op
            self.readers[wr] = {}
        return op

    def resolve(self):
        for e in ENGS:
            for op in self.ops[e]:
                for d in op.deps:
                    if d.dkey is not None:
                        continue
                    if d.eng == op.eng:
                        if op.eng == "pe" or op.dkey is not None:
                            if op.eng == "pe":
                                continue
                        if op.pos - d.pos > 3 and op.dkey is None:
                            continue
                    d.sig = True
        for e in ENGS:
            c = 0
            for op in self.ops[e]:
                if op.sig:
                    c += 1
                    op.idx = c
        for e in ENGS:
            known = {}
            for op in self.ops[e]:
                need = {}
                for d in op.deps:
                    if d.dkey is not None:
                        k = ("d", d.dkey)
                        v = d.dcnt
                    else:
                        if not d.sig:
                            continue
                        if d.eng == op.eng:
                            if op.eng == "pe":
                                continue
                            if op.pos - d.pos > 3 and op.dkey is None:
                                continue
                        k = ("e", d.eng)
                        v = d.idx
                    if v > need.get(k, 0):
                        need[k] = v
                w = []
                for k, v in need.items():
                    if v > known.get(k, 0):
                        known[k] = v
                        w.append((k, v))
                op.waits = tuple(w)


def _np_bf16():
    import ml_dtypes
    return ml_dtypes.bfloat16


class Cfg:
    def __init__(self, S=2048, NPG=64, NPOOL=10240, layers=(0, 1, 2, 3), mixers=True):
        self.S = S
        self.NT = S // 128
        self.NPG = NPG
        self.NPOOL = NPOOL
        self.TT = S + NSAMP
        self.layers = layers
        self.mixers = mixers


def rope_tables(cfg):
    half = ROPE // 2
    inv = (np.float32(10000.0) ** (-np.arange(half, dtype=np.float32) / np.float32(half))).astype(np.float32)
    pos = np.concatenate([np.arange(cfg.S), np.full(NSAMP, cfg.NPG * PAGE)]).astype(np.float32)
    ang = (pos[:, None] * inv[None, :]).astype(np.float32)
    cos = np.cos(ang).astype(np.float32)
    sin = np.sin(ang).astype(np.float32)
    tm = np.zeros((128, cfg.NT + 1, 64), np.float32)
    for t in range(cfg.NT):
        tm[:, t, :32] = cos[t * 128:(t + 1) * 128]
        tm[:, t, 32:] = sin[t * 128:(t + 1) * 128]
    tm[:NSAMP, cfg.NT, :32] = cos[cfg.S:]
    tm[:NSAMP, cfg.NT, 32:] = sin[cfg.S:]
    fm = np.zeros((64, 2, cfg.TT), np.float32)
    fm[:32, 0] = cos.T
    fm[32:, 0] = cos.T
    fm[:32, 1] = sin.T
    fm[32:, 1] = sin.T
    return tm, fm


def const_inputs(cfg):
    tm, fm = rope_tables(cfg)
    ident = np.eye(128, dtype=np.float32)
    cm = np.where(np.arange(128)[None, :] <= np.arange(128)[:, None], 0.0, NEG).astype(np.float32)
    tril = np.tril(np.ones((128, 128), np.float32))
    sm = np.full((HEADS, NSAMP, NSAMP), NEG, np.float32)
    for b in range(NSAMP):
        sm[:, b, b] = 0.0
    pc = np.ones((128, 8, 15), np.float32)
    for g, w in enumerate((2, 4, 8, 16)):
        for t in range(15):
            pc[:, 2 * g:2 * g + 2, t] = w / min(w, t + 1)
    return {"c_ident": ident, "c_cmask": cm, "c_tril": tril, "c_smask": sm.reshape(HEADS, NSAMP * NSAMP),
            "c_rope_tm": tm, "c_rope_fm": fm, "c_poolc": pc.reshape(128, 120),
            "c_pmod": (np.arange(128) % 16).astype(np.float32).reshape(128, 1)}


class K:
    def __init__(self, cfg):
        self.cfg = cfg
        self.nc = bass.Bass("TRN2", target_bir_lowering=False)
        self.P = Prog()
        self.dram = {}
        self.off = 0
        self.sems = {}

    def din(self, name, shape, dt=F32):
        self.dram[name] = self.nc.dram_tensor(name, list(shape), dt, kind="ExternalInput").ap()
        return self.dram[name]

    def dout(self, name, shape, dt=F32):
        self.dram[name] = self.nc.dram_tensor(name, list(shape), dt, kind="ExternalOutput").ap()
        return self.dram[name]

    def alloc(self, nbytes):
        o = self.off
        self.off += (nbytes + 63) // 64 * 64
        assert self.off <= self.arena_bytes, (self.off, self.arena_bytes, getattr(self, 'phase_kind', '?'), self.phase_off)
        return o

    def view(self, off, shape, dt=F32, parts=128):
        n = int(np.prod(shape))
        esz = 4 if dt in (F32, I32) else 2
        nb = n * esz
        assert off % 4 == 0 and nb % 4 == 0
        ap = self.arena[0:parts, off // 4:(off + nb) // 4]
        if dt != F32:
            ap = ap.bitcast(dt)
        if len(shape) == 2:
            ap = ap.rearrange("p (a b) -> p a b", a=shape[0])
        elif len(shape) == 3:
            ap = ap.rearrange("p (a b c) -> p a b c", a=shape[0], b=shape[1])
        elif len(shape) == 4:
            ap = ap.rearrange("p (a b c d) -> p a b c d", a=shape[0], b=shape[1], c=shape[2])
        return ap

    def buf(self, shape, dt=F32):
        n = int(np.prod(shape))
        esz = 4 if dt in (F32, I32) else 2
        off = self.alloc(n * esz)
        return self.view(off, shape, dt)

    def pe(self, fn, reads=(), writes=()):
        return self.P.add("pe", fn, reads, writes)

    def act(self, fn, reads=(), writes=()):
        return self.P.add("act", fn, reads, writes)

    def dve(self, fn, reads=(), writes=()):
        return self.P.add("dve", fn, reads, writes)

    def dma(self, q, out, in_, key, reads=(), writes=(), **kw):
        return self.P.add(q, lambda e: e.dma_start(out=out, in_=in_, **kw), reads, writes, dkey=key)

    def mm(self, out, lhsT, rhs, start, stop, reads=(), writes=()):
        return self.pe(lambda e: e.matmul(out, lhsT=lhsT, rhs=rhs, start=start, stop=stop), reads, writes)

    def tp(self, out, in_, ident, reads=(), writes=()):
        return self.pe(lambda e: e.transpose(out=out, in_=in_, identity=ident), reads, writes)

    def emit(self):
        P = self.P
        P.resolve()
        nc = self.nc
        esem = self.esem
        dsem = self.dsem
        engobj = {"pe": "tensor", "act": "scalar", "dve": "vector", "pool": "gpsimd", "sp": "sync"}

        def run(ename):
            def body(eng):
                for op in P.ops[ename]:
                    for (k, v) in op.waits:
                        if k[0] == "d":
                            eng.wait_ge(dsem[k[1]], v)
                        else:
                            eng.wait_ge(esem[k[1]], v)
                    ins = op.fn(eng)
                    if op.dkey is not None:
                        ins.then_inc(dsem[op.dkey], 16)
                    elif op.sig:
                        ins.then_inc(esem[ename], 1)
                if ename == "sp":
                    for key, tot in P.dma_tot.items():
                        eng.wait_ge(dsem[key], tot)
            return body

        with nc.Block() as block:
            for ename in ENGS:
                if P.ops[ename] or ename == "sp":
                    getattr(block, engobj[ename])(run(ename))


def _needs(op, d):
    if d.dkey is not None:
        return True
    if d.eng != op.eng:
        return True
    if op.dkey is not None:
        return True
    if op.eng == "pe":
        return False
    return (op.pos - d.pos) <= 3


def _resolve(self):
    for e in ENGS:
        for op in self.ops[e]:
            for d in op.deps:
                if d.dkey is None and _needs(op, d):
                    d.sig = True
    for e in ENGS:
        c = 0
        for op in self.ops[e]:
            if op.sig:
                c += 1
                op.idx = c
    for e in ENGS:
        known = {}
        for op in self.ops[e]:
            need = {}
            for d in op.deps:
                if not _needs(op, d):
                    continue
                if d.dkey is not None:
                    k = ("d", d.dkey)
                    v = d.dcnt
                else:
                    k = ("e", d.eng)
                    v = d.idx
                if v > need.get(k, 0):
                    need[k] = v
            w = []
            for k, v in need.items():
                if v > known.get(k, 0):
                    known[k] = v
                    w.append((k, v))
            op.waits = tuple(w)


Prog.resolve = _resolve


def _stt(k, eng, out, in0, scalar, in1, op0, op1, reads, writes):
    return k.P.add(eng, lambda e: e.scalar_tensor_tensor(out=out, in0=in0, scalar=scalar, in1=in1, op0=op0, op1=op1),
                   reads, writes)


def _tt(k, eng, out, in0, in1, op, reads, writes):
    return k.P.add(eng, lambda e: e.tensor_tensor(out=out, in0=in0, in1=in1, op=op), reads, writes)


def _ts(k, eng, out, in0, s1, s2, op0, op1, reads, writes):
    if s2 is None:
        return k.P.add(eng, lambda e: e.tensor_scalar(out=out, in0=in0, scalar1=s1, scalar2=None, op0=op0), reads, writes)
    return k.P.add(eng, lambda e: e.tensor_scalar(out=out, in0=in0, scalar1=s1, scalar2=s2, op0=op0, op1=op1),
                   reads, writes)


def _cp(k, eng, out, in_, reads, writes):
    if eng == "act":
        return k.P.add(eng, lambda e: e.activation(out=out, in_=in_, func=AF.Copy), reads, writes)
    return k.P.add(eng, lambda e: e.tensor_copy(out=out, in_=in_), reads, writes)


def _af(k, out, in_, func, reads, writes, **kw):
    return k.P.add("act", lambda e: e.activation(out=out, in_=in_, func=func, **kw), reads, writes)


def _red(k, eng, out, in_, op, reads, writes):
    return k.P.add(eng, lambda e: e.tensor_reduce(out=out, in_=in_, axis=AX.X, op=op), reads, writes)


def _recip(k, out, in_, reads, writes):
    return k.P.add("dve", lambda e: e.reciprocal(out=out, in_=in_), reads, writes)


def _memset(k, eng, ap, val, reads, writes):
    return k.P.add(eng, lambda e: e.memset(ap, val), reads, writes)


K.stt = _stt
K.tt = _tt
K.ts = _ts
K.cp = _cp
K.af = _af
K.red = _red
K.recip = _recip
K.memset = _memset


def _xr(self, gi, k, lo=0, hi=512):
    if gi == self.NG:
        return [("xT", gi, 0, k)]
    return [("xT", gi, hp, k) for hp in range(2) if lo < (hp + 1) * 256 and hi > hp * 256]


def _nb(self):
    b = self.bank
    self.bank = (self.bank + 1) % 8
    return b


def _setup(self):
    cfg = self.cfg
    S, NT, TT = cfg.S, cfg.NT, cfg.TT
    self.bank = 0
    self.tiles = [(t, t * 128, 128) for t in range(NT)] + [(NT, S, NSAMP)]
    ng = NT // 4
    self.groups = [(g, g * 512, 512, [self.tiles[4 * g + i] for i in range(4)]) for g in range(ng)]
    self.groups.append((ng, S, NSAMP, [self.tiles[NT]]))
    self.NG = ng
    self.passes = [[self.groups[g]] for g in range(ng)]
    self.passes[-1].append(self.groups[ng])
    self.x = self.buf([NT + 1, D])
    self.xT = self.buf([KD, TT], BF16)
    self.gb = [self.buf([2, D])]
    self.ident = self.buf([128])
    self.identb = self.buf([128], BF16)
    self.st = self.buf([4, 32])
    self.ones_row = self.buf([128], BF16)
    self.gbcnt = 0
    self.qls = self.buf([2, NSAMP, HEADS], BF16)
    self.qrs = self.buf([NSAMP, HEADS], BF16)
    nch = max(cfg.NPG // 8, 1)
    self.idxi = self.buf([NSAMP, nch], I32)
    self.idxf = self.buf([NSAMP, nch])
    self.pmod = self.buf([1])
    self.cmaskb = self.buf([128], BF16)
    self.cmaskf = self.buf([128])
    self.smask = self.buf([NSAMP * NSAMP])
    self.phase_off = self.off
    d = self.dram
    self.dma("sp", self.ident, d["c_ident"], "ident", writes=[("ident",)])
    self.cp("dve", self.identb, self.ident, [("ident",)], [("identb",)])
    self.memset("dve", self.ones_row, 1.0, [], [("ones",)])
    pt3 = d["page_table"].rearrange("b (c k) -> b c k", k=8)
    for k8 in range(8):
        self.dma("sp", self.idxi[16 * k8:16 * k8 + 16, :, :], pt3[:, :, k8].partition_broadcast(16), ("pt", k8),
                 writes=[("idxi", k8)], allow_slow_non_contiguous=True)
    self.dma("sp", self.pmod, d["c_pmod"], "pmod", writes=[("pmod",)])
    self.cp("dve", self.idxf, self.idxi, [("idxi", k8) for k8 in range(8)], [("idxf",)])
    self.ts("dve", self.idxf, self.idxf, 16.0, self.pmod[:, 0:1], ALU.mult, ALU.add, [("idxf",), ("pmod",)], [("idxf",)])
    self.dma("sp", self.cmaskf, d["c_cmask"], "cmask", writes=[("cmaskf",)])
    self.cp("dve", self.cmaskb, self.cmaskf, [("cmaskf",)], [("cmaskb",)])
    self.dma("sp", self.smask[:HEADS, :], d["c_smask"], "smask", writes=[("smask",)])
    xp = d["x_prompt"].rearrange("(t p) d -> p t d", p=128)
    for g in range(ng):
        self.dma("sp", self.x[:, 4 * g:4 * g + 4, :], xp[:, 4 * g:4 * g + 4, :], ("xin", g),
                 writes=[("x", t, h) for t in range(4 * g, 4 * g + 4) for h in range(2)])
    self.dma("sp", self.x[:NSAMP, NT, :], d["x_sample"], ("xin", ng), writes=[("x", NT, 0), ("x", NT, 1)])
    for grp in self.groups:
        self.t_phase(grp)


def _load_gb(self, li, w):
    slot = 0
    d = self.dram
    self.dma("sp", self.gb[slot][:, 0, :], d["ln_g"][li, w, :].partition_broadcast(128), ("gb", slot, 0),
             writes=[("gb", slot, 0)])
    self.dma("sp", self.gb[slot][:, 1, :], d["ln_b"][li, w, :].partition_broadcast(128), ("gb", slot, 1),
             writes=[("gb", slot, 1)])
    return slot


def _ln_tile(self, tile, banks, coef, gslot):
    t, c0, n = tile
    x, ps = self.x, self.ps
    sl = t % 4
    st = self.st
    rx = [("x", t, 0), ("x", t, 1)]
    rs = ("st", sl)
    if banks is not None:
        for hf in range(2):
            self.stt("dve", x[:n, t, hf * 512:(hf + 1) * 512], ps[:n, banks[hf], :], float(coef),
                     x[:n, t, hf * 512:(hf + 1) * 512], ALU.mult, ALU.add,
                     [("ps", banks[hf]), ("x", t, hf)], [("x", t, hf)])
    for hf in range(2):
        self.P.add("dve", (lambda e, hf=hf: e.bn_stats(out=st[:n, sl, hf * 6:hf * 6 + 6],
                                                       in_=x[:n, t, hf * 512:(hf + 1) * 512])),
                   [("x", t, hf)], [("st", sl, hf)])
    self.P.add("dve", lambda e: e.bn_aggr(out=st[:n, sl, 12:14],
                                          in_=st[:n, sl, 0:12].rearrange("p (a b) -> p a b", a=2)),
               [("st", sl, 0), ("st", sl, 1)], [("st", sl, "mv")])
    self.af(st[:n, sl, 14:15], st[:n, sl, 13:14], AF.Sqrt, [("st", sl, "mv")], [("st", sl, "sd")],
            bias=float(LN_EPS / ALPHA ** 2), scale=1.0)
    self.recip(st[:n, sl, 15:16], st[:n, sl, 14:15], [("st", sl, "sd")], [("st", sl, "rs")])
    g = self.gb[gslot]
    for hf in range(2):
        sli = slice(hf * 512, (hf + 1) * 512)
        self.stt("dve", x[:n, t, sli], x[:n, t, sli], st[:n, sl, 12:13], g[:n, 0, sli], ALU.subtract, ALU.mult,
                 [("x", t, hf), ("st", sl, "mv"), ("gb", gslot, 0)], [("x", t, hf)])
    for hf in range(2):
        sli = slice(hf * 512, (hf + 1) * 512)
        self.stt("dve", x[:n, t, sli], x[:n, t, sli], st[:n, sl, 15:16], g[:n, 1, sli], ALU.mult, ALU.add,
                 [("x", t, hf), ("st", sl, "rs"), ("gb", gslot, 1)], [("x", t, hf)])


def _t_phase(self, grp):
    gi, c0, ncols, tiles = grp
    x, ps, xT = self.x, self.ps, self.xT
    for k in range(KD):
        b = self.nb()
        for (t, tc0, n) in tiles:
            off = tc0 - c0
            self.tp(ps[:, b, off:off + n], x[:n, t, k * 128:(k + 1) * 128], self.ident[:n, :n],
                    [("x", t, k // 4), ("ident",)], [("ps", b)])
        g0 = (c0 // 512) * 512 if gi != self.NG else c0
        self.cp("act", xT[:, k, c0:c0 + ncols], ps[:, b, 0:ncols], [("ps", b)], self.xr(gi, k, c0 - g0, c0 - g0 + ncols))


def _ffn(self, li, w, lnw):
    cfg = self.cfg
    self.P.phase = "ffn%d_%d" % (li, w)
    d = self.dram
    ps, xT = self.ps, self.xT
    wg = d["ffn_w_gate"][li, w].rearrange("(k p) n -> p k n", p=128)
    wu = d["ffn_w_up"][li, w].rearrange("(k p) n -> p k n", p=128)
    wd = d["ffn_w_down"][li, w].rearrange("(c p) n -> p c n", p=128)
    gslot = self.load_gb(li, lnw)
    coef = 0.5 / ALPHA
    for pgroups in self.passes:
        pc0 = pgroups[0][1]
        for u in range(7):
            slot = self.cnt_gu % 2
            self.cnt_gu += 1
            wt = self.wgu[slot]
            self.dma("pool", wt[:, 0, :, :], wg[:, :, u * 384:(u + 1) * 384], ("wgu", slot, 0), writes=[("wgu", slot, 0)])
            self.dma("pool", wt[:, 1, :, :], wu[:, :, u * 384:(u + 1) * 384], ("wgu", slot, 1), writes=[("wgu", slot, 1)])
            for c in range(3):
                ff = u * 3 + c
                for gidx, (gi, c0, ncols, tiles) in enumerate(pgroups):
                    bA, bB = self.nb(), self.nb()
                    for which, b in ((0, bA), (1, bB)):
                        for k in range(KD):
                            self.mm(ps[:, b, 0:ncols], wt[:, which, k, c * 128:(c + 1) * 128], xT[:, k, c0:c0 + ncols],
                                    k == 0, k == KD - 1, [("wgu", slot, which)] + self.xr(gi, k), [("ps", b)])
                    ss = self.cnt_sg % 2
                    self.cnt_sg += 1
                    self.af(self.sg[ss][:, 0:ncols], ps[:, bA, 0:ncols], AF.Silu, [("ps", bA)], [("sg", ss)])
                    self.tt("dve", self.hT[:, ff, c0 - pc0:c0 - pc0 + ncols], self.sg[ss][:, 0:ncols], ps[:, bB, 0:ncols],
                            ALU.mult, [("sg", ss), ("ps", bB)], [("hT", ff, gidx)])
        for gidx, (gi, c0, ncols, tiles) in enumerate(pgroups):
            for u in range(7):
                slot = self.cnt_wd % 2
                self.cnt_wd += 1
                wt = self.wdb[slot]
                self.dma("pool", wt, wd[:, u * 3:(u + 1) * 3, :], ("wd", slot), writes=[("wd", slot)])
                for i, (t, tc0, n) in enumerate(tiles):
                    for hf in range(2):
                        for c in range(3):
                            ff = u * 3 + c
                            self.mm(ps[:n, 2 * i + hf, :], self.hT[:, ff, tc0 - pc0:tc0 - pc0 + n],
                                    wt[:, c, hf * 512:(hf + 1) * 512], u == 0 and c == 0, u == 6 and c == 2,
                                    [("hT", ff, gidx), ("wd", slot)], [("ps", 2 * i + hf)])
            for i, tile in enumerate(tiles):
                self.ln_tile(tile, (2 * i, 2 * i + 1), coef, gslot)
            self.t_phase((gi, c0, ncols, tiles))


def _alloc_ffn(self):
    if self.phase_kind != "ffn":
        self.P.fence()
    self.phase_kind = "ffn"
    self.off = self.phase_off
    pw = 512 + NSAMP
    self.hT = self.buf([NFF, pw], BF16)
    self.wgu = [self.buf([2, KD, 384], BF16) for _ in range(2)]
    self.wdb = [self.buf([3, D], BF16) for _ in range(2)]
    self.sg = [self.buf([512]) for _ in range(2)]
    self.ffn_end = self.off


def _ln_only(self, li, lnw):
    gslot = self.load_gb(li, lnw)
    for grp in self.groups:
        for tile in grp[3]:
            self.ln_tile(tile, None, 0.0, gslot)
        self.t_phase(grp)


def _finish(self):
    d = self.dram
    cfg = self.cfg
    yp = d["y_prompt"].rearrange("(t p) d -> p t d", p=128)
    for g in range(self.NG):
        self.dma("sp", yp[:, 4 * g:4 * g + 4, :], self.x[:, 4 * g:4 * g + 4, :], ("yout", g),
                 reads=[("x", t, h) for t in range(4 * g, 4 * g + 4) for h in range(2)])
    self.dma("sp", d["y_sample"], self.x[:NSAMP, cfg.NT, :], ("yout", self.NG),
             reads=[("x", cfg.NT, 0), ("x", cfg.NT, 1)])


K.nb = _nb
K.xr = _xr
K.setup = _setup
K.load_gb = _load_gb
K.ln_tile = _ln_tile
K.t_phase = _t_phase
K.ffn = _ffn
K.alloc_ffn = _alloc_ffn
K.ln_only = _ln_only
K.finish = _finish


IN_SPECS = [
    ("x_prompt", lambda c: (c.S, D), F32), ("x_sample", lambda c: (NSAMP, D), F32),
    ("cache_kv_latent", lambda c: (2, c.NPOOL, PAGE, KVR), F32), ("cache_k_rope", lambda c: (2, c.NPOOL, PAGE, ROPE), F32),
    ("state_pool", lambda c: (NSAMP * 15, D), F32), ("page_table", lambda c: (NSAMP, c.NPG), I32),
    ("ln_g", lambda c: (DEPTH, 3, D), F32), ("ln_b", lambda c: (DEPTH, 3, D), F32),
    ("ffn_w_gate", lambda c: (DEPTH, 2, D, DFF), F32), ("ffn_w_up", lambda c: (DEPTH, 2, D, DFF), F32),
    ("ffn_w_down", lambda c: (DEPTH, 2, DFF, D), F32),
    ("a_w_in", lambda c: (2, D, 704), F32), ("a_q_norm", lambda c: (2, QR), F32), ("a_kv_norm", lambda c: (2, KVR), F32),
    ("a_w_uq", lambda c: (2, QR, 1536), F32), ("a_w_uk", lambda c: (2, KVR, HEADS, 128), F32),
    ("a_w_uv", lambda c: (2, KVR, HEADS, 128), F32), ("a_w_o", lambda c: (2, D, D), F32),
    ("b_w_in", lambda c: (1, D, 2 * DC), F32), ("b_b_in", lambda c: (1, 2 * DC), F32),
    ("b_v_norm_g", lambda c: (1, DC), F32), ("b_v_norm_b", lambda c: (1, DC), F32),
    ("b_w_s", lambda c: (1, HEADS, 128, 128), F32), ("b_b_s", lambda c: (1, HEADS, 128), F32),
    ("b_w_out", lambda c: (1, DC, D), F32),
    ("c_w_in", lambda c: (1, D, D), F32), ("c_w_grp", lambda c: (1, 4, 256, 256), F32),
    ("c_scale", lambda c: (1, D), F32), ("c_w_out", lambda c: (1, D, D), F32),
    ("c_ident", lambda c: (128, 128), F32), ("c_cmask", lambda c: (128, 128), F32), ("c_tril", lambda c: (128, 128), F32),
    ("c_smask", lambda c: (HEADS, NSAMP * NSAMP), F32), ("c_rope_tm", lambda c: (128, c.NT + 1, 64), F32),
    ("c_rope_fm", lambda c: (64, 2, c.TT), F32), ("c_poolc", lambda c: (128, 120), F32),
    ("c_pmod", lambda c: (128, 1), F32),
]
OUT_SPECS = [
    ("y_prompt", lambda c: (c.S, D)), ("y_sample", lambda c: (NSAMP, D)),
    ("o_ckv_p", lambda c: (2, c.S, KVR)), ("o_kr_p", lambda c: (2, c.S, ROPE)),
    ("o_ckv_s", lambda c: (2, NSAMP, KVR)), ("o_kr_s", lambda c: (2, NSAMP, ROPE)),
    ("o_v_s", lambda c: (NSAMP, DC)), ("o_pool_p", lambda c: (15, D)), ("o_pool_s", lambda c: (NSAMP, 15, D)),
]
DEBUG = False


def build(cfg):
    from contextlib import ExitStack
    k = K(cfg)
    nc = k.nc
    for name, shp, dt in IN_SPECS:
        k.din(name, shp(cfg), dt)
    for name, shp in OUT_SPECS:
        k.dout(name, shp(cfg))
    if DEBUG:
        k.dout("o_dbg", (128, 8192))
    k.dbgcol = 0
    k.arena_bytes = SBUF_BYTES
    k.cnt_gu = k.cnt_wd = k.cnt_sg = 0
    k.cnt_wv = k.cnt_vg = k.cnt_wu = k.cnt_wo = k.cnt_uc = 0
    k.phase_kind = "ffn"
    with ExitStack() as es:
        k.arena = es.enter_context(nc.sbuf_tensor("arena", [128, SBUF_BYTES // 4], F32))
        k.ps = es.enter_context(nc.psum_tensor("ps", [128, 8, 512], F32))
        k.setup()
        for li in cfg.layers:
            kind = li % 3
            j = li // 3
            k.alloc_ffn()
            k.ffn(li, 0, 0)
            if not cfg.mixers:
                k.ln_only(li, 1)
            elif kind == 0:
                k.mla(li, j)
            elif kind == 1:
                k.cmlp(li)
            else:
                k.poolmix(li)
            k.alloc_ffn()
            k.ffn(li, 1, 2)
        k.finish()
        k.esem = {e: es.enter_context(nc.semaphore("e_" + e)) for e in ENGS}
        k.dsem = {}
        for i, key in enumerate(k.P.dma_tot):
            k.dsem[key] = es.enter_context(nc.semaphore("d%d" % i))
        k.emit()
    return nc, k


def make_in_maps(cfg, inputs, ncores=8):
    consts = const_inputs(cfg)
    maps = []
    for c in range(ncores):
        m = {}
        m["x_prompt"] = np.ascontiguousarray(inputs["x_prompt"][c])
        m["x_sample"] = np.ascontiguousarray(inputs["x_sample"][c * NSAMP:(c + 1) * NSAMP, 0])
        m["cache_kv_latent"] = inputs["cache_kv_latent"]
        m["cache_k_rope"] = inputs["cache_k_rope"]
        m["state_pool"] = np.ascontiguousarray(inputs["state_pool"][0, c * NSAMP:(c + 1) * NSAMP]).reshape(NSAMP * 15, D)
        m["page_table"] = np.ascontiguousarray(inputs["page_table"][c * NSAMP:(c + 1) * NSAMP]).astype(np.int32)
        for name, _, _ in IN_SPECS:
            if name in m:
                continue
            if name in consts:
                m[name] = consts[name]
            else:
                m[name] = np.asarray(inputs[name])
        maps.append(m)
    return maps


def gather_outputs(cfg, results, ncores=8):
    S = cfg.S
    R = results
    y_p = np.stack([R[c]["y_prompt"] for c in range(ncores)])
    y_s = np.concatenate([R[c]["y_sample"] for c in range(ncores)])[:, None, :]
    ckv_p = np.stack([R[c]["o_ckv_p"] for c in range(ncores)], axis=1)
    kr_p = np.stack([R[c]["o_kr_p"] for c in range(ncores)], axis=1)
    ckv_s = np.concatenate([R[c]["o_ckv_s"] for c in range(ncores)], axis=1)[:, :, None, :]
    kr_s = np.concatenate([R[c]["o_kr_s"] for c in range(ncores)], axis=1)[:, :, None, :]
    v_s = np.concatenate([R[c]["o_v_s"] for c in range(ncores)])[None, :, None, :]
    pool_p = np.stack([R[c]["o_pool_p"] for c in range(ncores)])[None]
    pool_s = np.concatenate([R[c]["o_pool_s"] for c in range(ncores)])[None]
    outs = (y_p, y_s, ckv_p, kr_p, ckv_s, kr_s, v_s, pool_p, pool_s)
    return tuple(np.ascontiguousarray(o, dtype=np.float32) for o in outs)


_CACHE = {}


def kernel(**inputs):
    cfg = Cfg()
    inputs = {k_: np.asarray(v) for k_, v in inputs.items()}
    if "nc" not in _CACHE:
        _CACHE["nc"] = build(cfg)[0]
    nc = _CACHE["nc"]
    maps = make_in_maps(cfg, inputs)
    res = run_bass_kernel_spmd(nc, maps, core_ids=list(range(8)))
    return gather_outputs(cfg, res.results)


def _load_colvec(self, dram1d, nk, dst, key, rname):
    tmp = self.buf([128])
    self.dma("sp", tmp[:nk, :], dram1d.rearrange("(k p) -> k p", p=128), key, writes=[(rname, "tmp")])
    b = self.nb()
    self.tp(self.ps[:, b, 0:nk], tmp[:nk, :], self.ident[:nk, :nk], [(rname, "tmp"), ("ident",)], [("ps", b)])
    self.cp("dve", dst, self.ps[:, b, 0:nk], [("ps", b)], [(rname,)])


def _proj_out_ln(self, grp, actT, kc_n, w_sb, wres, act_res, coef, gslot):
    gi, c0, ncols, tiles = grp
    ps = self.ps
    for (t, tc0, n) in tiles:
        b0, b1 = self.nb(), self.nb()
        for hf, b in ((0, b0), (1, b1)):
            for kc in range(kc_n):
                self.mm(ps[:n, b, :], actT[:, kc, tc0 - c0:tc0 - c0 + n], w_sb[:, kc, hf * 512:(hf + 1) * 512],
                        kc == 0, kc == kc_n - 1, [act_res(kc), wres], [("ps", b)])
        self.ln_tile((t, tc0, n), (b0, b1), coef, gslot)
    self.t_phase(grp)


def _poolmix(self, li):
    cfg = self.cfg
    d = self.dram
    S, NT = cfg.S, cfg.NT
    ps, xT = self.ps, self.xT
    self.P.phase = "pool"
    self.P.fence()
    self.phase_kind = "pool"
    self.off = self.phase_off
    cwin = self.buf([KD, D], BF16)
    cwout = self.buf([KD, D], BF16)
    wgrp = self.buf([4, 2, 256], BF16)
    cscale = self.buf([8])
    poolc = self.buf([8, 15])
    hb_off = self.off
    hb = [self.buf([8, 527]) for _ in range(2)]
    tmpA = self.buf([2, 527])
    tmpB = self.buf([2, 527])
    plT = self.buf([8, 512], BF16)
    zT = self.buf([8, 512], BF16)
    hrow = self.view(hb_off, [D])
    hsrow = self.view(hb_off + 4096, [D])
    stt_tm = self.view(hb_off + 8192, [2, D])
    hsT = self.view(hb_off + 16384, [8, NSAMP, 16])
    sw = self.view(hb_off + 16384 + 8192, [8, NSAMP])
    assert 16384 + 8192 + 512 <= 2 * 8 * 527 * 4
    gslot = self.load_gb(li, 1)
    coef = 1.0 / ALPHA
    WINS = (2, 4, 8, 16)
    self.dma("pool", cwin, d["c_w_in"][0].rearrange("(k p) n -> p k n", p=128), "cwin", writes=[("cwin",)])
    self.dma("pool", cwout, d["c_w_out"][0].rearrange("(k p) n -> p k n", p=128), "cwout", writes=[("cwout",)])
    self.dma("pool", wgrp, d["c_w_grp"][0].rearrange("g (c p) e -> p g c e", p=128), "wgrp", writes=[("wgrp",)])
    self.dma("sp", poolc, d["c_poolc"].rearrange("p (a b) -> p a b", a=8), "poolc", writes=[("poolc",)])
    self.load_colvec(d["c_scale"][0], 8, cscale, "cscale", "cscale")
    lt = self.tiles[NT - 1]
    for (tile, stage, n) in ((lt, hrow, 128), (self.tiles[NT], hsrow, NSAMP)):
        t, tc0, _ = tile
        gi = min(t // 4, self.NG)
        for hf in range(2):
            b = self.nb()
            for k in range(KD):
                self.mm(ps[:n, b, :], xT[:, k, tc0:tc0 + n], cwin[:, k, hf * 512:(hf + 1) * 512], k == 0, k == KD - 1,
                        self.xr(gi, k) + [("cwin",)], [("ps", b)])
            self.cp("act", stage[:n, hf * 512:(hf + 1) * 512], ps[:n, b, :], [("ps", b)], [("hrow", t, hf)])
    self.dma("sp", d["o_pool_p"], hrow[113:128, :], "o_pool_p", reads=[("hrow", NT - 1, 0), ("hrow", NT - 1, 1)])
    self.dma("sp", d["o_pool_s"][:, 14, :], hsrow[:NSAMP, :], "o_pool_s", reads=[("hrow", NT, 0), ("hrow", NT, 1)])
    sp3 = d["state_pool"].rearrange("(b r) d -> b r d", r=15)
    self.dma("sp", d["o_pool_s"][:, 0:14, :], sp3[:, 1:15, :], "o_pool_s2")
    P_ = self.P
    P_.fence()
    prev = None
    for grp in self.groups:
        gi, c0, ncols, tiles = grp
        samp = gi == self.NG
        cur = gi % 2
        H = hb[cur]
        if samp:
            P_.fence()
            self.dma("sp", stt_tm[:120, :, :], d["state_pool"].rearrange("(a p) d -> p a d", p=120), "stt", writes=[("stt",)])
            for a in range(2):
                for c in range(8):
                    b = self.nb()
                    self.tp(ps[:, b, 0:120], stt_tm[:120, a, c * 128:(c + 1) * 128], self.ident[:120, :120],
                            [("stt",), ("ident",)], [("ps", b)])
                    self.cp("dve", hsT[:, c, 8 * a:8 * a + 8, 0:15], ps[:, b, 0:120].rearrange("p (s r) -> p s r", r=15),
                            [("ps", b)], [("hsT", c, a)])
        if not samp:
            if gi == 0:
                self.memset("dve", H[:, :, 0:15], 0.0, [], [("hb", cur, "halo")])
            else:
                self.cp("act", H[:, :, 0:15], hb[prev][:, :, 512:527], [("hb", prev, c) for c in range(8)],
                        [("hb", cur, "halo")])
        for c in range(8):
            b = self.nb()
            for k in range(KD):
                self.mm(ps[:, b, 0:ncols], cwin[:, k, c * 128:(c + 1) * 128], xT[:, k, c0:c0 + ncols], k == 0, k == KD - 1,
                        [("cwin",)] + self.xr(gi, k), [("ps", b)])
            if samp:
                self.cp("act", hsT[:, c, :, 15], ps[:, b, 0:ncols], [("ps", b)], [("hsT", c, 2)])
            else:
                self.cp("act", H[:, c, 15:527], ps[:, b, 0:512], [("ps", b)], [("hb", cur, c)])
        for g, w in enumerate(WINS):
            cs = slice(2 * g, 2 * g + 2)
            rin = [("hb", cur, 2 * g), ("hb", cur, 2 * g + 1), ("hb", cur, "halo")]
            if samp:
                for c in (2 * g, 2 * g + 1):
                    self.red("dve", sw[:, c, :], hsT[:, c, :, 16 - w:16], ALU.add,
                             [("hsT", c, 0), ("hsT", c, 1), ("hsT", c, 2)], [("sw", c)])
                    self.stt("dve", plT[:, c, 0:NSAMP], sw[:, c, :], 1.0 / w, hsT[:, c, :, 15], ALU.mult, ALU.subtract,
                             [("sw", c), ("hsT", c, 2)], [("plT", c)])
                continue
            src = H[:, cs, :]
            lo = 0
            bufs = (tmpA, tmpB)
            for si in range(g + 1):
                sh = 1 << si
                dst = bufs[si % 2]
                nlo = lo + sh
                self.tt("dve", dst[:, :, nlo:527], src[:, :, nlo:527], src[:, :, nlo - sh:527 - sh], ALU.add,
                        rin + [("ptmp", (si + 1) % 2)], [("ptmp", si % 2)])
                src = dst
                lo = nlo
            last = ("ptmp", g % 2)
            if gi == 0:
                self.tt("dve", src[:, :, 15:30], src[:, :, 15:30], poolc[:, cs, :], ALU.mult, [last, ("poolc",)], [last])
            self.stt("dve", plT[:, cs, :], src[:, :, 15:527], 1.0 / w, H[:, cs, 15:527], ALU.mult, ALU.subtract,
                     [last] + rin, [("plT", 2 * g), ("plT", 2 * g + 1)])
        for g in range(4):
            for ec in range(2):
                b = self.nb()
                for dc in range(2):
                    self.mm(ps[:, b, 0:ncols], wgrp[:, g, dc, ec * 128:(ec + 1) * 128], plT[:, 2 * g + dc, 0:ncols],
                            dc == 0, dc == 1, [("wgrp",), ("plT", 2 * g + dc)], [("ps", b)])
                self.af(zT[:, 2 * g + ec, 0:ncols], ps[:, b, 0:ncols], AF.Identity, [("ps", b), ("cscale",)],
                        [("zT", 2 * g + ec)], scale=cscale[:, 2 * g + ec:2 * g + ec + 1])
        if DEBUG and gi == 0:
            self.dbgmap = {}
            self.dbgmap["cscale"] = self.dbg(cscale, 128, 8, [("cscale",)])
            self.dbgmap["hb0"] = self.dbg(H[:, 0, :], 128, 527, [("hb", cur, 0), ("hb", cur, "halo")])
            self.dbgmap["hb7"] = self.dbg(H[:, 7, :], 128, 527, [("hb", cur, 7), ("hb", cur, "halo")])
            self.dbgmap["pl0"] = self.dbg(plT[:, 0, :], 128, 512, [("plT", 0)])
            self.dbgmap["pl7"] = self.dbg(plT[:, 7, :], 128, 512, [("plT", 7)])
            self.dbgmap["z0"] = self.dbg(zT[:, 0, :], 128, 512, [("zT", 0)])
            self.dbgmap["z7"] = self.dbg(zT[:, 7, :], 128, 512, [("zT", 7)])
            self.dbgmap["cwout"] = self.dbg(cwout[:, 0, :], 128, 1024, [("cwout",)])
        self.proj_out_ln(grp, zT, 8, cwout, ("cwout",), lambda kc: ("zT", kc), coef, gslot)
        prev = cur


def _dbg(self, ap, parts, n, reads):
    col = self.dbgcol
    self.dbgcol += n
    q = "sp" if ap.dtype == F32 else "pool"
    self.dma(q, self.dram["o_dbg"][0:parts, col:col + n], ap, ("dbg", col), reads=reads)
    return col


K.dbg = _dbg
K.load_colvec = _load_colvec
K.proj_out_ln = _proj_out_ln
K.poolmix = _poolmix


def _cmlp(self, li):
    cfg = self.cfg
    d = self.dram
    S, NT = cfg.S, cfg.NT
    ps, xT = self.ps, self.xT
    self.P.phase = "cmlp"
    self.P.fence()
    self.phase_kind = "cmlp"
    self.off = self.phase_off
    winu = [self.buf([KD, 256], BF16) for _ in range(2)]
    winv = [self.buf([KD, 512], BF16) for _ in range(2)]
    binv = [self.buf([512], BF16) for _ in range(2)]
    wout = [self.buf([2, D], BF16) for _ in range(2)]
    vgb = [self.buf([2, 512]) for _ in range(2)]
    vf_off = self.off
    v_f = self.buf([2, DC])
    vn = self.buf([2, DC], BF16)
    uc = [self.buf([256], BF16) for _ in range(2)]
    huc = [self.buf([256], BF16) for _ in range(2)]
    WsT = self.buf([HEADS, 128], BF16)
    wsS = self.buf([HEADS, NSAMP], BF16)
    bsrow = self.buf([HEADS, 128], BF16)
    bsS = self.buf([HEADS, NSAMP], BF16)
    tril = self.buf([128])
    w00 = self.buf([HEADS])
    bs0 = self.buf([HEADS])
    binu = self.buf([24])
    vst = self.buf([2, 48])
    stage = self.view(vf_off, [HEADS, 128])
    gslot = self.load_gb(li, 1)
    coef = 1.0 / ALPHA
    w_in = d["b_w_in"][0].rearrange("(k p) n -> p k n", p=128)
    w_out = d["b_w_out"][0].rearrange("(c p) n -> p c n", p=128)
    b_in = d["b_b_in"][0]
    vfall = [("vf", ti, b) for ti in range(2) for b in range(6)]
    self.dma("sp", stage, d["b_w_s"][0].rearrange("h i j -> i h j"), "ws", writes=vfall)
    self.dma("sp", tril, d["c_tril"], "tril", writes=[("tril",)])
    for h in range(HEADS):
        self.tt("dve", stage[:, h, :], stage[:, h, :], tril, ALU.mult, vfall + [("tril",)], [("wsm", h)])
    for h in range(HEADS):
        b = self.nb()
        self.tp(ps[:, b, 0:128], stage[:, h, :], self.ident, [("wsm", h), ("ident",)], [("ps", b)])
        self.cp("act", WsT[:, h, :], ps[:, b, 0:128], [("ps", b)], [("WsT",)])
    self.P.add("dve", lambda e: e.memset(vst[:, 0, 0:1], 0.0), [("wsm", h) for h in range(HEADS)], vfall)
    self.dma("pool", bsrow[0:1, :, :], d["b_b_s"][0:1], "bsrow", writes=[("bsrow",)])
    self.dma("sp", w00[:NSAMP, :], d["b_w_s"][0][:, 0, 0].partition_broadcast(NSAMP), "w00", writes=[("w00",)],
             allow_slow_non_contiguous=True)
    self.dma("sp", bs0[0:1, :], d["b_b_s"][0:1, :, 0], "bs0", writes=[("bs0",)], allow_slow_non_contiguous=True)
    for h in range(HEADS):
        self.ts("dve", wsS[:NSAMP, h, :], self.ident[:NSAMP, :NSAMP], w00[:NSAMP, h:h + 1], None, ALU.mult, None,
                [("w00",), ("ident",)], [("wsS",)])
        self.ts("dve", bsS[0:1, h, :], self.ones_row[0:1, 0:NSAMP], bs0[0:1, h:h + 1], None, ALU.mult, None,
                [("bs0",), ("ones",)], [("bsS",)])
    self.load_colvec(b_in[0:DC], 24, binu, "binu", "binu")
    plist = []
    for g in range(self.NG):
        for hp in range(2):
            tl = [self.tiles[4 * g + 2 * hp], self.tiles[4 * g + 2 * hp + 1]]
            plist.append((g, g * 512 + hp * 256, 256, tl))
    plist.append((self.NG, S, NSAMP, [self.tiles[NT]]))
    for (gi, c0, ncols, tiles) in plist:
        samp = gi == self.NG
        nt = len(tiles)
        lo = c0 - (gi * 512 if not samp else c0)
        for blk in range(6):
            sl = self.cnt_wv % 2
            self.cnt_wv += 1
            self.dma("pool", winv[sl], w_in[:, :, DC + blk * 512:DC + (blk + 1) * 512], ("winv", sl), writes=[("winv", sl)])
            self.dma("pool", binv[sl][0:1, :], b_in[DC + blk * 512:DC + (blk + 1) * 512].rearrange("(o n) -> o n", o=1),
                     ("binv", sl), writes=[("binv", sl)])
            for ti, (t, tc0, n) in enumerate(tiles):
                b = self.nb()
                for k in range(KD):
                    self.mm(ps[:n, b, :], xT[:, k, tc0:tc0 + n], winv[sl][:, k, :], k == 0, False,
                            self.xr(gi, k, lo, lo + ncols) + [("winv", sl)], [("ps", b)])
                self.mm(ps[:n, b, :], self.ones_row[0:1, 0:n], binv[sl][0:1, :], False, True,
                        [("ones",), ("binv", sl)], [("ps", b)])
                self.af(v_f[:n, ti, blk * 512:(blk + 1) * 512], ps[:n, b, :], AF.Gelu_apprx_tanh, [("ps", b)],
                        [("vf", ti, blk)])
                self.P.add("dve", (lambda e, n=n, ti=ti, blk=blk: e.bn_stats(
                    out=vst[:n, ti, blk * 6:blk * 6 + 6], in_=v_f[:n, ti, blk * 512:(blk + 1) * 512])),
                    [("vf", ti, blk)], [("vst", ti, blk)])
        for ti, (t, tc0, n) in enumerate(tiles):
            self.P.add("dve", (lambda e, n=n, ti=ti: e.bn_aggr(
                out=vst[:n, ti, 36:38], in_=vst[:n, ti, 0:36].rearrange("p (a b) -> p a b", a=6))),
                [("vst", ti, b_) for b_ in range(6)], [("vst", ti, "mv")])
            self.af(vst[:n, ti, 38:39], vst[:n, ti, 37:38], AF.Sqrt, [("vst", ti, "mv")], [("vst", ti, "sd")],
                    bias=float(LN_EPS), scale=1.0)
            self.recip(vst[:n, ti, 39:40], vst[:n, ti, 38:39], [("vst", ti, "sd")], [("vst", ti, "rs")])
        for blk in range(6):
            sl = self.cnt_vg % 2
            self.cnt_vg += 1
            bsl = slice(blk * 512, (blk + 1) * 512)
            self.dma("sp", vgb[sl][:, 0, :], d["b_v_norm_g"][0, bsl].partition_broadcast(128), ("vgb", sl, 0),
                     writes=[("vgb", sl, 0)])
            self.dma("sp", vgb[sl][:, 1, :], d["b_v_norm_b"][0, bsl].partition_broadcast(128), ("vgb", sl, 1),
                     writes=[("vgb", sl, 1)])
            for ti, (t, tc0, n) in enumerate(tiles):
                self.stt("dve", v_f[:n, ti, bsl], v_f[:n, ti, bsl], vst[:n, ti, 36:37], vgb[sl][:n, 0, :],
                         ALU.subtract, ALU.mult, [("vf", ti, blk), ("vst", ti, "mv"), ("vgb", sl, 0)], [("vf", ti, blk)])
                if samp:
                    self.stt("dve", v_f[:n, ti, bsl], v_f[:n, ti, bsl], vst[:n, ti, 39:40], vgb[sl][:n, 1, :],
                             ALU.mult, ALU.add, [("vf", ti, blk), ("vst", ti, "rs"), ("vgb", sl, 1)], [("vf", ti, blk)])
                    self.cp("act", vn[:n, ti, bsl], v_f[:n, ti, bsl], [("vf", ti, blk)], [("vn", ti, blk)])
                else:
                    self.stt("dve", vn[:n, ti, bsl], v_f[:n, ti, bsl], vst[:n, ti, 39:40], vgb[sl][:n, 1, :],
                             ALU.mult, ALU.add, [("vf", ti, blk), ("vst", ti, "rs"), ("vgb", sl, 1)], [("vn", ti, blk)])
        if samp:
            self.dma("sp", d["o_v_s"], v_f[:NSAMP, 0, :], "o_v_s", reads=[("vf", 0, b_) for b_ in range(6)])
        accb = [[self.nb(), self.nb()] for _ in range(nt)]
        used = set(b for pr in accb for b in pr)
        for ch in range(24):
            h = ch // 3
            if ch % 2 == 0:
                su = self.cnt_wu % 2
                self.cnt_wu += 1
                self.dma("pool", winu[su], w_in[:, :, ch * 128:(ch + 2) * 128], ("winu", su), writes=[("winu", su)])
                so = self.cnt_wo % 2
                self.cnt_wo += 1
                self.dma("pool", wout[so], w_out[:, ch:ch + 2, :], ("wout", so), writes=[("wout", so)])
            b = self.nb()
            while b in used:
                b = self.nb()
            for k in range(KD):
                self.mm(ps[:, b, 0:ncols], winu[su][:, k, (ch % 2) * 128:(ch % 2 + 1) * 128], xT[:, k, c0:c0 + ncols],
                        k == 0, k == KD - 1, [("winu", su)] + self.xr(gi, k, lo, lo + ncols), [("ps", b)])
            us = self.cnt_uc % 2
            self.cnt_uc += 1
            self.af(uc[us][:, 0:ncols], ps[:, b, 0:ncols], AF.Gelu_apprx_tanh, [("ps", b), ("binu",)], [("uc", us)],
                    bias=binu[:, ch:ch + 1], scale=1.0)
            b2 = self.nb()
            while b2 in used:
                b2 = self.nb()
            for ti, (t, tc0, n) in enumerate(tiles):
                o = tc0 - c0
                blk = ch // 4
                if samp:
                    self.mm(ps[:, b2, o:o + n], vn[:n, ti, ch * 128:(ch + 1) * 128], wsS[:n, h, :], True, False,
                            [("vn", ti, blk), ("wsS",)], [("ps", b2)])
                    self.mm(ps[:, b2, o:o + n], self.ones_row[0:1, 0:128], bsS[0:1, h, :], False, True,
                            [("ones",), ("bsS",)], [("ps", b2)])
                else:
                    self.mm(ps[:, b2, o:o + n], vn[:n, ti, ch * 128:(ch + 1) * 128], WsT[:, h, :], True, False,
                            [("vn", ti, blk), ("WsT",)], [("ps", b2)])
                    self.mm(ps[:, b2, o:o + n], self.ones_row[0:1, 0:128], bsrow[0:1, h, :], False, True,
                            [("ones",), ("bsrow",)], [("ps", b2)])
            self.tt("dve", huc[us][:, 0:ncols], uc[us][:, 0:ncols], ps[:, b2, 0:ncols], ALU.mult,
                    [("uc", us), ("ps", b2)], [("huc", us)])
            for ti, (t, tc0, n) in enumerate(tiles):
                o = tc0 - c0
                for hf in range(2):
                    self.mm(ps[:n, accb[ti][hf], :], huc[us][:, o:o + n], wout[so][:, ch % 2, hf * 512:(hf + 1) * 512],
                            ch == 0, ch == 23, [("huc", us), ("wout", so)], [("ps", accb[ti][hf])])
        for ti, tile in enumerate(tiles):
            self.ln_tile(tile, accb[ti], coef, gslot)
        self.t_phase((gi, c0, ncols, tiles))


K.cmlp = _cmlp


def _mla(self, li, j):
    cfg = self.cfg
    d = self.dram
    S, NT, TT, NPG = cfg.S, cfg.NT, cfg.TT, cfg.NPG
    ps, xT = self.ps, self.xT
    P = self.P
    P.phase = "mlaA%d" % li
    P.fence()
    self.phase_kind = "mla"
    self.off = self.phase_off
    gslot = self.load_gb(li, 1)
    coef = 1.0 / ALPHA
    SC = float(ATT_SCALE)
    wuv = self.buf([2, HEADS, 128], BF16)
    wo = self.buf([HEADS, D], BF16)
    qn_c = self.buf([3])
    kvn_bc = self.buf([KVR])
    ast = self.buf([2, 8])
    ckvT_s = self.buf([2, NSAMP], BF16)
    krT_s = self.buf([NSAMP], BF16)
    ckv_tm_s = self.buf([KVR], BF16)
    c_off = self.off
    cqT = self.buf([3, TT], BF16)
    ckvT = self.buf([2, TT], BF16)
    krT = self.buf([TT], BF16)
    ckv_tm = self.buf([NT, KVR], BF16)
    mid_off = self.off
    cs_tm = self.buf([NT + 1, 64])
    self.dma("pool", wuv, d["a_w_uv"][j].rearrange("(c p) h v -> p c h v", p=128), "wuv", writes=[("wuv",)])
    self.dma("pool", wo, d["a_w_o"][j].rearrange("(k p) n -> p k n", p=128), "wo", writes=[("wo",)])
    self.dma("sp", kvn_bc, d["a_kv_norm"][j].partition_broadcast(128), "kvn", writes=[("kvn",)])
    self.dma("sp", cs_tm, d["c_rope_tm"], "cs_tm", writes=[("cs_tm",)])
    self.load_colvec(d["a_q_norm"][j], 3, qn_c, "qn", "qn")
    win = self.buf([KD, 704], BF16)
    cqn = [self.buf([QR]) for _ in range(2)]
    ckvf = [self.buf([KVR]) for _ in range(2)]
    krf = [self.buf([128]) for _ in range(2)]
    for kk in range(2):
        self.memset("dve", krf[kk], 0.0, [], [("krf", kk, 0), ("krf", kk, 1)])
    ktmp = [self.buf([128]) for _ in range(2)]
    junk = self.buf([QR])
    self.dma("pool", win, d["a_w_in"][j].rearrange("(k p) n -> p k n", p=128), "win", writes=[("win",)])
    for (t, tc0, n) in self.tiles:
        samp = t == NT
        gi = min(t // 4, self.NG)
        lo = tc0 - gi * 512 if not samp else 0
        s2 = t % 2
        bq, bk = self.nb(), self.nb()
        for k in range(KD):
            self.mm(ps[:n, bq, 0:QR], xT[:, k, tc0:tc0 + n], win[:, k, 0:QR], k == 0, k == KD - 1,
                    self.xr(gi, k, lo, lo + n) + [("win",)], [("ps", bq)])
        for k in range(KD):
            self.mm(ps[:n, bk, 0:320], xT[:, k, tc0:tc0 + n], win[:, k, QR:704], k == 0, k == KD - 1,
                    self.xr(gi, k, lo, lo + n) + [("win",)], [("ps", bk)])
        self.memset("dve", ast[:n, s2, 0:2], 0.0, [], [("ast", s2, "ss")])
        self.af(junk[:n, :], ps[:n, bq, 0:QR], AF.Square, [("ps", bq), ("ast", s2, "ss")], [("junk",), ("ast", s2, "ssq")],
                accum_out=ast[:n, s2, 0:1])
        self.af(junk[:n, 0:KVR], ps[:n, bk, 0:KVR], AF.Square, [("ps", bk), ("ast", s2, "ss")],
                [("junk",), ("ast", s2, "ssk")], accum_out=ast[:n, s2, 1:2])
        self.af(ast[:n, s2, 2:3], ast[:n, s2, 0:1], AF.Sqrt, [("ast", s2, "ssq")], [("ast", s2, "sdq")],
                bias=float(RMS_EPS), scale=1.0 / QR)
        self.af(ast[:n, s2, 3:4], ast[:n, s2, 1:2], AF.Sqrt, [("ast", s2, "ssk")], [("ast", s2, "sdk")],
                bias=float(RMS_EPS), scale=1.0 / KVR)
        self.recip(ast[:n, s2, 4:6], ast[:n, s2, 2:4], [("ast", s2, "sdq"), ("ast", s2, "sdk")], [("ast", s2, "rs")])
        self.af(cqn[s2][:n, :], ps[:n, bq, 0:QR], AF.Identity, [("ps", bq), ("ast", s2, "rs")], [("cqn", s2)],
                scale=ast[:n, s2, 4:5])
        self.stt("dve", ckvf[s2][:n, :], ps[:n, bk, 0:KVR], ast[:n, s2, 5:6], kvn_bc[:n, :], ALU.mult, ALU.mult,
                 [("ps", bk), ("ast", s2, "rs"), ("kvn",)], [("ckvf", s2)])
        if samp:
            self.dma("sp", d["o_ckv_s"][j], ckvf[s2][:n, :], ("o_ckv", s2), reads=[("ckvf", s2)])
        else:
            self.dma("sp", d["o_ckv_p"][j, tc0:tc0 + n, :], ckvf[s2][:n, :], ("o_ckv", s2), reads=[("ckvf", s2)])
        if samp:
            self.cp("act", ckv_tm_s[:n, :], ckvf[s2][:n, :], [("ckvf", s2)], [("ckv_tm", t)])
        else:
            self.cp("act", ckv_tm[:n, t, :], ckvf[s2][:n, :], [("ckvf", s2)], [("ckv_tm", t)])
        sk = getattr(cfg, "skip", ())
        if "all" in sk:
            continue
        x1 = ps[:n, bk, 256:288]
        x2 = ps[:n, bk, 288:320]
        cosv = cs_tm[:n, t, 0:32]
        sinv = cs_tm[:n, t, 32:64]
        kt_ = ktmp[s2]
        for (o_, a_, b_) in ((0, x1, cosv), (32, x2, sinv), (64, x1, sinv), (96, x2, cosv)):
            self.tt("dve", kt_[:n, o_:o_ + 32], a_, b_, ALU.mult, [("ps", bk), ("cs_tm",)], [("ktmp", s2, o_)])
        self.tt("dve", krf[s2][:n, 0:32], kt_[:n, 0:32], kt_[:n, 32:64], ALU.subtract,
                [("ktmp", s2, 0), ("ktmp", s2, 32)], [("krf", s2, 0)])
        self.tt("dve", krf[s2][:n, 32:64], kt_[:n, 64:96], kt_[:n, 96:128], ALU.add,
                [("ktmp", s2, 64), ("ktmp", s2, 96)], [("krf", s2, 1)])
        if samp:
            self.dma("sp", d["o_kr_s"][j], krf[s2][:n, 0:ROPE], ("o_kr", s2), reads=[("krf", s2, 0), ("krf", s2, 1)])
        else:
            self.dma("sp", d["o_kr_p"][j, tc0:tc0 + n, :], krf[s2][:n, 0:ROPE], ("o_kr", s2),
                     reads=[("krf", s2, 0), ("krf", s2, 1)])
        if "tr" in sk:
            continue
        bX, bY = self.nb(), self.nb()
        for k in range(3):
            self.tp(ps[:, bX, k * 128:k * 128 + n], cqn[s2][:n, k * 128:(k + 1) * 128], self.ident[:n, :n],
                    [("cqn", s2), ("ident",)], [("ps", bX)])
        for c in range(2):
            self.tp(ps[:, bY, c * 128:c * 128 + n], ckvf[s2][:n, c * 128:(c + 1) * 128], self.ident[:n, :n],
                    [("ckvf", s2), ("ident",)], [("ps", bY)])
        self.tp(ps[:, bY, 256:256 + n], krf[s2][:n, :], self.ident[:n, :n],
                [("krf", s2, 0), ("krf", s2, 1), ("ident",)], [("ps", bY)])
        for k in range(3):
            if "e1" in sk:
                continue
            self.af(cqT[:, k, tc0:tc0 + n], ps[:, bX, k * 128:k * 128 + n], AF.Identity, [("ps", bX), ("qn",)],
                    [("cqT", t, k)], scale=qn_c[:, k:k + 1])
        kdst = ckvT_s[:, :, 0:n] if samp else ckvT[:, :, tc0:tc0 + n]
        rdst = krT_s[:64, 0:n] if samp else krT[:64, tc0:tc0 + n]
        if "e2" not in sk:
            self.cp("dve", kdst, ps[:, bY, 0:256].rearrange("p (c q) -> p c q", c=2)[:, :, 0:n],
                    [("ps", bY)], [("ckvT", t)])
        if "e3" not in sk:
            self.cp("act", rdst, ps[:64, bY, 256:256 + n], [("ps", bY)], [("krT", t)])
    stop = getattr(cfg, "stop", "")
    if stop == "A":
        for grp in self.groups:
            for tile in grp[3]:
                self.ln_tile(tile, None, 0.0, gslot)
            self.t_phase(grp)
        return
    P.phase = "mlaB%d" % li
    P.fence()
    self.off = mid_off
    wuq = self.buf([3, 1536], BF16)
    wqs = self.buf([3, HEADS, ROPE], BF16)
    wukT = self.buf([HEADS, KVR], BF16)
    csT = self.buf([2, 128])
    qnT = [self.buf([128], BF16)]
    qlT = self.buf([HEADS, 2, 128], BF16)
    qrT = self.buf([HEADS, 128], BF16)
    t12 = [self.buf([2, 128])]
    pb_off = self.off
    pb = self.buf([max(NT * 128, 2048)], BF16)
    wuk = self.view(pb_off, [2, HEADS, 128], BF16)
    pb2 = [pb, self.buf([NT * 128], BF16)]
    pT = self.buf([NT * 128], BF16)
    ol = [self.buf([KVR], BF16) for _ in range(2)]
    olT = [self.buf([KVR], BF16) for _ in range(2)]
    oT = [self.buf([HEADS, 128], BF16)]
    ats = self.buf([2, 8])
    self.dma("pool", wuq, d["a_w_uq"][j].rearrange("(k p) n -> p k n", p=128), "wuq", writes=[("wuq",)])
    self.dma("pool", wuk, d["a_w_uk"][j].rearrange("(c p) h d -> p c h d", p=128), "wuk", writes=[("pb", 0)])
    wr = wuq.rearrange("p k (h e) -> p k h e", h=HEADS)
    self.ts("dve", wqs[:, :, :, 0:32], wr[:, :, :, 160:192], -1.0, None, ALU.mult, None, [("wuq",)], [("wqs", 0)])
    self.cp("dve", wqs[:, :, :, 32:64], wr[:, :, :, 128:160], [("wuq",)], [("wqs", 1)])
    for h in range(HEADS):
        b = 7
        pv = ps[:, b, 0:128].bitcast(BF16)
        for cc in range(2):
            self.tp(pv[:, cc * 128:(cc + 1) * 128], wuk[:, cc, h, :], self.identb, [("pb", 0), ("identb",)], [("ps", b)])
        self.cp("dve", wukT[:, h, :], pv, [("ps", b)], [("wukT",)])
    pTps = ps[:, 4:6, :].rearrange("p a b -> p (a b)").bitcast(BF16)
    olTps = ps[:, 7, 0:128].bitcast(BF16)
    sall = ps[:, 0:4, :].rearrange("p a b -> p (a b)")
    for (t, tc0, n) in self.tiles:
        samp = t == NT
        gi = min(t // 4, self.NG)
        self.dma("sp", csT[:64, :, 0:n], d["c_rope_fm"][:, :, tc0:tc0 + n], "csT", writes=[("csT",)])
        os_ = 0
        for h in range(HEADS):
            s2 = h % 2
            b = 4 + (h % 2)
            for k in range(3):
                self.mm(ps[:, b, 0:n], wuq[:, k, h * 192:h * 192 + 128], cqT[:, k, tc0:tc0 + n], k == 0, k == 2,
                        [("wuq",), ("cqT", t, k)], [("ps", b)])
            self.cp("act", qnT[0][:, 0:n], ps[:, b, 0:n], [("ps", b)], [("qnT", 0)])
            b2 = 6
            for cc in range(2):
                self.mm(ps[:, b2, cc * 128:cc * 128 + n], wukT[:, h, cc * 128:(cc + 1) * 128], qnT[0][:, 0:n], True, True,
                        [("wukT",), ("qnT", 0)], [("ps", b2, "ql")])
            src = ps[:, b2, 0:256].rearrange("p (c q) -> p c q", c=2)[:, :, 0:n]
            if samp:
                self.cp("dve", self.qls[:, :, :, h], src, [("ps", b2, "ql")], [("qls", h)])
            else:
                self.cp("dve", qlT[:, h, :, 0:n], src, [("ps", b2, "ql")], [("qlT", h)])
            b3 = 7
            for k in range(3):
                self.mm(ps[:64, b3, 0:n], wuq[:, k, h * 192 + 128:h * 192 + 192], cqT[:, k, tc0:tc0 + n], k == 0, k == 2,
                        [("wuq",), ("cqT", t, k)], [("ps", b3)])
            for k in range(3):
                self.mm(ps[:64, b3, 128:128 + n], wqs[:, k, h, :], cqT[:, k, tc0:tc0 + n], k == 0, k == 2,
                        [("wqs", 0), ("wqs", 1), ("cqT", t, k)], [("ps", b3)])
            self.tt("dve", t12[0][:64, 0, 0:n], ps[:64, b3, 0:n], csT[:64, 0, 0:n], ALU.mult,
                    [("ps", b3), ("csT",)], [("t12", 0, 0)])
            self.tt("dve", t12[0][:64, 1, 0:n], ps[:64, b3, 128:128 + n], csT[:64, 1, 0:n], ALU.mult,
                    [("ps", b3), ("csT",)], [("t12", 0, 1)])
            if samp:
                self.tt("dve", self.qrs[:64, :, h], t12[0][:64, 0, 0:n], t12[0][:64, 1, 0:n], ALU.add,
                        [("t12", 0, 0), ("t12", 0, 1)], [("qrs", h)])
            else:
                self.tt("dve", qrT[:64, h, 0:n], t12[0][:64, 0, 0:n], t12[0][:64, 1, 0:n], ALU.add,
                        [("t12", 0, 0), ("t12", 0, 1)], [("qrT", h)])
        if samp:
            continue
        if stop == "B0":
            self.ln_tile((t, tc0, n), None, 0.0, gslot)
            self.t_phase((gi, tc0, n, [(t, tc0, n)]))
            continue
        nk = t + 1
        nkeys = nk * 128
        nch = (nkeys + 511) // 512
        rps = [("ps", kc) for kc in range(nch)]

        def stS(h):
            a2 = h % 2
            pbh = pb2[a2]
            for kc in range(nch):
                k0 = kc * 512
                w = min(512, nkeys - k0)
                diag = kc == nch - 1
                rk = [("ckvT", tt_) for tt_ in range(k0 // 128, (k0 + w) // 128)]
                rr = [("krT", tt_) for tt_ in range(k0 // 128, (k0 + w) // 128)]
                self.mm(ps[:, kc, 0:w], qlT[:, h, 0, :], ckvT[:, 0, k0:k0 + w], True, False, [("qlT", h)] + rk, [("ps", kc)])
                self.mm(ps[:, kc, 0:w], qlT[:, h, 1, :], ckvT[:, 1, k0:k0 + w], False, False, [("qlT", h)] + rk, [("ps", kc)])
                self.mm(ps[:, kc, 0:w], qrT[:64, h, :], krT[:64, k0:k0 + w], False, not diag, [("qrT", h)] + rr, [("ps", kc)])
                if diag:
                    self.mm(ps[:, kc, w - 128:w], self.identb, self.cmaskb, False, True, [("identb",), ("cmaskb",)],
                            [("ps", kc)])
            self.red("dve", ats[:, a2, 0:1], sall[:, 0:nkeys], ALU.max, rps, [("ats", a2, "mx")])
            self.ts("dve", ats[:, a2, 1:2], ats[:, a2, 0:1], -SC, None, ALU.mult, None, [("ats", a2, "mx")], [("ats", a2, "nm")])
            self.memset("dve", ats[:, a2, 2:3], 0.0, [], [("ats", a2, "l")])
            self.af(pbh[:, 0:nkeys], sall[:, 0:nkeys], AF.Exp, rps + [("ats", a2, "nm"), ("ats", a2, "l")],
                    [("pb", a2), ("ats", a2, "l")], bias=ats[:, a2, 1:2], scale=SC, accum_out=ats[:, a2, 2:3])
            self.recip(ats[:, a2, 3:4], ats[:, a2, 2:3], [("ats", a2, "l")], [("ats", a2, "ri")])

        def stTVO(h):
            a2 = h % 2
            pbh = pb2[a2]
            for kt in range(nk):
                self.tp(pTps[:, kt * 128:(kt + 1) * 128], pbh[:, kt * 128:(kt + 1) * 128], self.identb,
                        [("pb", a2), ("identb",)], [("ps", 4 + kt // 8)])
            hk = min(nk, 8)
            self.cp("act", pT[:, 0:hk * 128], pTps[:, 0:hk * 128], [("ps", 4)], [("pT", 0)])
            if nk > hk:
                self.cp("dve", pT[:, hk * 128:nkeys], pTps[:, hk * 128:nkeys], [("ps", 5)], [("pT", 1)])
            for kt in range(nk):
                self.mm(ps[:, 6, 0:KVR], pT[:, kt * 128:(kt + 1) * 128], ckv_tm[:, kt, :], kt == 0, kt == nk - 1,
                        [("pT", 0), ("pT", 1), ("ckv_tm", kt)], [("ps", 6, "pv")])
            self.af(ol[a2], ps[:, 6, 0:KVR], AF.Identity, [("ps", 6, "pv"), ("ats", a2, "ri")], [("ol", a2)],
                    scale=ats[:, a2, 3:4])
            for cc in range(2):
                self.tp(olTps[:, cc * 128:(cc + 1) * 128], ol[a2][:, cc * 128:(cc + 1) * 128], self.identb,
                        [("ol", a2), ("identb",)], [("ps", 7)])
            self.cp("dve", olT[a2], olTps, [("ps", 7)], [("olT", a2)])
            for cc in range(2):
                self.mm(ps[:, 6, 256:384], wuv[:, cc, h, :], olT[a2][:, cc * 128:(cc + 1) * 128], cc == 0, cc == 1,
                        [("wuv",), ("olT", a2)], [("ps", 6, "o")])
            self.cp("act", oT[os_][:, h, :], ps[:, 6, 256:384], [("ps", 6, "o")], [("oT", os_, h)])

        stS(0)
        for h in range(HEADS):
            if h + 1 < HEADS:
                stS(h + 1)
            stTVO(h)
        for hf in range(2):
            for h in range(HEADS):
                self.mm(ps[:, hf, :], oT[os_][:, h, :], wo[:, h, hf * 512:(hf + 1) * 512], h == 0, h == HEADS - 1,
                        [("oT", os_, h), ("wo",)], [("ps", hf)])
        self.ln_tile((t, tc0, n), (0, 1), coef, gslot)
        self.t_phase((gi, tc0, n, [(t, tc0, n)]))
    if getattr(cfg, "skip_c", False):
        self.ln_tile(self.tiles[NT], None, 0.0, gslot)
        self.t_phase(self.groups[self.NG])
        return
    P.phase = "mlaC%d" % li
    P.fence()
    self.off = c_off
    NCH = NPG // 8
    NC = NCH + 1
    kvbk = [self.buf([8, KVR], BF16) for _ in range(5)]
    kvbr = [self.buf([8, ROPE], BF16) for _ in range(5)]
    idxt = self.buf([NSAMP, NCH])
    idx2 = self.buf([NSAMP, NCH], I32)
    self.ts("dve", idxt, self.idxf, float(j * cfg.NPOOL * 16), None, ALU.add, None, [("idxf",)], [("idxt",)])
    self.cp("dve", idx2, idxt, [("idxt",)], [("idx2",)])
    ckv_blk = d["cache_kv_latent"].rearrange("l n (g r) c -> (l n g) (r c)", r=8)
    kr_blk = d["cache_k_rope"].rearrange("l n (g r) c -> (l n g) (r c)", r=8)
    KT = [self.buf([2, 1024], BF16) for _ in range(2)]
    KrT = [self.buf([1024], BF16) for _ in range(2)]
    p_s = [self.buf([1024], BF16) for _ in range(2)]
    pT_s = [self.buf([64], BF16) for _ in range(2)]
    ssb = self.buf([NSAMP])
    mst = [self.buf([16]) for _ in range(2)]
    nmst = [self.buf([16]) for _ in range(2)]
    lst = [self.buf([16]) for _ in range(2)]
    wst = [self.buf([16]) for _ in range(2)]
    cst = [self.buf([8]) for _ in range(2)]
    ost = [self.buf([NC, KVR]) for _ in range(2)]
    acc = [self.buf([KVR]) for _ in range(2)]
    ol_s = [self.buf([KVR], BF16) for _ in range(2)]
    olT_s = self.buf([2, HEADS, NSAMP], BF16)
    oT_s = self.buf([HEADS, NSAMP], BF16)
    ckv_d = d["cache_kv_latent"]
    kr_d = d["cache_k_rope"]
    ckv_fl = ckv_d.rearrange("l n p c -> (l n) p c")
    kr_fl = kr_d.rearrange("l n p c -> (l n) p c")
    dsem = lambda key: self.dsem[key]
    bkA, bkB, bkC = 0, 1, 2
    kA = ps[:, bkA, :].bitcast(BF16)
    kB = ps[:, bkB, :].bitcast(BF16)
    kC = ps[:, bkC, :].bitcast(BF16)
    pTs_ps = ps[:, 3, 0:32].bitcast(BF16)
    olTs_ps = ps[:, 3, 64:72].bitcast(BF16)
    s_ps = ps[:, 6:8, :].rearrange("p a b -> p (a b)")
    HS = HEADS
    smask3 = self.smask.rearrange("p (b k) -> p b k", b=NSAMP)
    items = []
    for bsm in range(NSAMP):
        for ci in range(NCH):
            items.append((bsm, ci))
        items.append((bsm, NCH))
    NS = len(kvbk)

    def stageA(it):
        bsm, ci = items[it]
        sb = bsm % 2
        if ci == 0:
            self.memset("dve", lst[sb][:HS, :], 0.0, [], [("lst", sb, c) for c in range(NC)])
        if ci == NCH:
            return
        slot = it % NS
        ks = it % 2
        self.gather(kvbk[slot].rearrange("p s c -> p (s c)"), ckv_blk, idx2[:, bsm, ci:ci + 1], ("kvk", slot))
        self.gather(kvbr[slot].rearrange("p s c -> p (s c)"), kr_blk, idx2[:, bsm, ci:ci + 1], ("kvr", slot))
        for pg in range(8):
            self.tp(kA[:, pg * 128:(pg + 1) * 128], kvbk[slot][:, pg, 0:128], self.identb, [("kvk", slot), ("identb",)],
                    [("ps", bkA)])
            self.tp(kB[:, pg * 128:(pg + 1) * 128], kvbk[slot][:, pg, 128:256], self.identb, [("kvk", slot), ("identb",)],
                    [("ps", bkB)])
            self.tp(kC[:64, pg * 128:(pg + 1) * 128], kvbr[slot][:, pg, :], self.identb,
                    [("kvr", slot), ("identb",)], [("ps", bkC)])
        self.cp("act", KT[ks][:, 0, :], kA, [("ps", bkA)], [("KT", ks, 0)])
        self.cp("dve", KT[ks][:, 1, :], kB, [("ps", bkB)], [("KT", ks, 1)])
        self.cp("act", KrT[ks][:64, :], kC[:64, :], [("ps", bkC)], [("KrT", ks)])

    def stageB(it):
        bsm, ci = items[it]
        sb = bsm % 2
        ks = it % 2
        rq = [("qls", h_) for h_ in range(HS)]
        rr = [("qrs", h_) for h_ in range(HS)]
        if ci < NCH:
            for hh in range(2):
                bs_ = 6 + hh
                ksl = slice(hh * 512, (hh + 1) * 512)
                self.mm(ps[:HS, bs_, :], self.qls[:, 0, bsm, :], KT[ks][:, 0, ksl], True, False,
                        rq + [("KT", ks, 0)], [("ps", bs_)])
                self.mm(ps[:HS, bs_, :], self.qls[:, 1, bsm, :], KT[ks][:, 1, ksl], False, False,
                        [("KT", ks, 1)], [("ps", bs_)])
                self.mm(ps[:HS, bs_, :], self.qrs[:64, bsm, :], KrT[ks][:64, ksl], False, True,
                        rr + [("KrT", ks)], [("ps", bs_)])
            src, nkk, rsrc = s_ps[:HS, :], 1024, [("ps", 6), ("ps", 7)]
        else:
            rself = [("ckvT", NT), ("krT", NT)]
            self.mm(ps[:HS, 5, 0:NSAMP], self.qls[:, 0, bsm, :], ckvT_s[:, 0, :], True, False, rq + rself, [("ps", 5)])
            self.mm(ps[:HS, 5, 0:NSAMP], self.qls[:, 1, bsm, :], ckvT_s[:, 1, :], False, False, rself, [("ps", 5)])
            self.mm(ps[:HS, 5, 0:NSAMP], self.qrs[:64, bsm, :], krT_s[:64, :], False, True, rr + rself, [("ps", 5)])
            self.tt("dve", ssb[:HS, :], ps[:HS, 5, 0:NSAMP], smask3[:HS, bsm, :], ALU.add, [("ps", 5), ("smask",)],
                    [("ssb",)])
            src, nkk, rsrc = ssb[:HS, :], NSAMP, [("ssb",)]
        self.red("dve", mst[sb][:HS, ci:ci + 1], src, ALU.max, rsrc, [("mst", sb, ci)])
        self.ts("dve", nmst[sb][:HS, ci:ci + 1], mst[sb][:HS, ci:ci + 1], -SC, None, ALU.mult, None,
                [("mst", sb, ci)], [("nmst", sb, ci)])
        self.af(p_s[ks][:HS, 0:nkk], src, AF.Exp, rsrc + [("nmst", sb, ci), ("lst", sb, ci)],
                [("p_s", ks), ("lst", sb, ci)], bias=nmst[sb][:HS, ci:ci + 1], scale=SC,
                accum_out=lst[sb][:HS, ci:ci + 1])

    def stageC(it):
        bsm, ci = items[it]
        sb = bsm % 2
        ks = it % 2
        slot = it % NS
        if ci < NCH:
            for pg in range(8):
                self.tp(pTs_ps[:, pg * 8:(pg + 1) * 8], p_s[ks][:HS, pg * 128:(pg + 1) * 128], self.identb[:HS, :HS],
                        [("p_s", ks), ("identb",)], [("ps", 3, "pT")])
            self.cp("dve", pT_s[ks], pTs_ps, [("ps", 3, "pT")], [("pT_s", ks)])
            for pg in range(8):
                self.mm(ps[:HS, 4, 0:KVR], pT_s[ks][:, pg * 8:(pg + 1) * 8], kvbk[slot][:, pg, :], pg == 0, pg == 7,
                        [("pT_s", ks), ("kvk", slot)], [("ps", 4)])
        else:
            self.tp(pTs_ps[:NSAMP, 0:8], p_s[ks][:HS, 0:NSAMP], self.identb[:HS, :HS], [("p_s", ks), ("identb",)],
                    [("ps", 3, "pT")])
            self.cp("dve", pT_s[ks][:NSAMP, 0:8], pTs_ps[:NSAMP, 0:8], [("ps", 3, "pT")], [("pT_s", ks)])
            self.mm(ps[:HS, 4, 0:KVR], pT_s[ks][:NSAMP, 0:8], ckv_tm_s[:NSAMP, :], True, True,
                    [("pT_s", ks), ("ckv_tm", NT)], [("ps", 4)])
        self.cp("act", ost[sb][:HS, ci, :], ps[:HS, 4, 0:KVR], [("ps", 4)], [("ost", sb, ci)])
        if ci < NCH:
            return
        allm = [("mst", sb, c) for c in range(NC)]
        self.red("dve", cst[sb][:HS, 0:1], mst[sb][:HS, 0:NC], ALU.max, allm, [("cst", sb, 0)])
        self.ts("dve", cst[sb][:HS, 1:2], cst[sb][:HS, 0:1], -SC, None, ALU.mult, None, [("cst", sb, 0)], [("cst", sb, 1)])
        self.af(wst[sb][:HS, 0:NC], mst[sb][:HS, 0:NC], AF.Exp, allm + [("cst", sb, 1)], [("wst", sb)],
                bias=cst[sb][:HS, 1:2], scale=SC)
        self.tt("dve", nmst[sb][:HS, 0:NC], wst[sb][:HS, 0:NC], lst[sb][:HS, 0:NC], ALU.mult,
                [("wst", sb)] + [("lst", sb, c) for c in range(NC)], [("nmst", sb, c) for c in range(NC)])
        self.red("dve", cst[sb][:HS, 2:3], nmst[sb][:HS, 0:NC], ALU.add, [("nmst", sb, c) for c in range(NC)],
                 [("cst", sb, 2)])
        self.recip(cst[sb][:HS, 3:4], cst[sb][:HS, 2:3], [("cst", sb, 2)], [("cst", sb, 3)])
        self.ts("dve", acc[sb][:HS, :], ost[sb][:HS, 0, :], wst[sb][:HS, 0:1], None, ALU.mult, None,
                [("ost", sb, 0), ("wst", sb)], [("acc", sb)])
        for c in range(1, NC):
            self.stt("dve", acc[sb][:HS, :], ost[sb][:HS, c, :], wst[sb][:HS, c:c + 1], acc[sb][:HS, :], ALU.mult, ALU.add,
                     [("ost", sb, c), ("wst", sb), ("acc", sb)], [("acc", sb)])
        self.af(ol_s[sb][:HS, :], acc[sb][:HS, :], AF.Identity, [("acc", sb), ("cst", sb, 3)], [("ol_s", sb)],
                scale=cst[sb][:HS, 3:4])
        for cc in range(2):
            self.tp(olTs_ps[:, cc * 8:(cc + 1) * 8], ol_s[sb][:HS, cc * 128:(cc + 1) * 128], self.identb[:HS, :HS],
                    [("ol_s", sb), ("identb",)], [("ps", 3, "olT")])
        self.cp("dve", olT_s[:, :, :, bsm], olTs_ps.rearrange("p (c h) -> p c h", c=2), [("ps", 3, "olT")],
                [("olT_s", bsm)])

    NI = len(items)
    for step in range(NI + 2):
        if step < NI:
            stageA(step)
        if 0 <= step - 1 < NI:
            stageB(step - 1)
        if 0 <= step - 2 < NI:
            stageC(step - 2)
    allo = [("olT_s", b_) for b_ in range(NSAMP)]
    for h in range(HEADS):
        b = self.nb()
        for cc in range(2):
            self.mm(ps[:, b, 0:NSAMP], wuv[:, cc, h, :], olT_s[:, cc, h, :], cc == 0, cc == 1, [("wuv",)] + allo, [("ps", b)])
        self.cp("act", oT_s[:, h, :], ps[:, b, 0:NSAMP], [("ps", b)], [("oT_s", h)])
    b0, b1 = self.nb(), self.nb()
    for hf, b in ((0, b0), (1, b1)):
        for h in range(HEADS):
            self.mm(ps[:NSAMP, b, :], oT_s[:, h, :], wo[:, h, hf * 512:(hf + 1) * 512], h == 0, h == HEADS - 1,
                    [("oT_s", h), ("wo",)], [("ps", b)])
    st_ = self.tiles[NT]
    self.ln_tile(st_, (b0, b1), coef, gslot)
    self.t_phase(self.groups[self.NG])


def _gather(self, out, src, idx, key):
    self.P.add("pool", lambda e: e.indirect_dma_start(out=out, out_offset=None, in_=src,
                                                      in_offset=bass.IndirectOffsetOnAxis(ap=idx, axis=0)),
               [("idx2",)], [key], dkey=key)


K.gather = _gather
K.mla = _mla
```

```python
import numpy as np
import concourse.bass as bass
import concourse.mybir as mybir
from concourse.bass_utils import run_bass_kernel_spmd

F32 = mybir.dt.float32
BF16 = mybir.dt.bfloat16
I32 = mybir.dt.int32
AF = mybir.ActivationFunctionType
ALU = mybir.AluOpType
AX = mybir.AxisListType

D = 1024
KD = 8
DEPTH = 4
ALPHA = (2 * DEPTH) ** 0.25
LN_EPS = 1e-5
RMS_EPS = 1e-6
DFF = 2688
NFF = 21
HEADS = 8
QR = 384
KVR = 256
ROPE = 64
DC = 3072
NSAMP = 16
PAGE = 128
ATT_SCALE = (128 + 64) ** -0.5
NEG = -30000.0
SBUF_BYTES = 212800

ENGS = ("pe", "act", "dve", "pool", "sp")


class Op:
    __slots__ = ("eng", "fn", "deps", "sig", "idx", "pos", "dkey", "dcnt", "waits", "gid", "ph")

    def __init__(self, eng, fn):
        self.eng = eng
        self.fn = fn
        self.deps = ()
        self.sig = False
        self.idx = 0
        self.pos = 0
        self.dkey = None
        self.dcnt = 0
        self.waits = ()


class Prog:
    def __init__(self):
        self.ops = {e: [] for e in ENGS}
        self.lastw = {}
        self.readers = {}
        self.dma_tot = {}
        self.dma_eng = {}
        self.n = 0
        self.pending = {}
        self.last_dma = {}
        self.bank_rd = {}
        self.phase = "setup"

    def fence(self):
        deps = [self.ops[e][-1] for e in ENGS if self.ops[e]]
        deps += list(self.last_dma.values())
        for e in ENGS:
            self.pending[e] = list(deps)

    def add(self, eng, fn, reads=(), writes=(), dkey=None, dinc=16):
        op = Op(eng, fn)
        op.ph = self.phase
        op.gid = self.n
        self.n += 1
        deps = {}
        for r in reads:
            w = self.lastw.get(r)
            if w is not None:
                deps[id(w)] = w
        for wr in writes:
            w = self.lastw.get(wr)
            if w is not None:
                deps[id(w)] = w
            rd = self.readers.get(wr)
            if rd:
                for o in rd.values():
                    deps[id(o)] = o
        if eng in ("act", "dve"):
            other = "dve" if eng == "act" else "act"
            for r in reads:
                if r[0] == "ps":
                    o = self.bank_rd.get((r[1], other))
                    if o is not None:
                        deps[id(o)] = o
                    self.bank_rd[(r[1], eng)] = op
        if eng == "pe":
            for wr in writes:
                if wr[0] == "ps":
                    for other in ("act", "dve"):
                        o = self.bank_rd.get((wr[1], other))
                        if o is not None:
                            deps[id(o)] = o
        pend = self.pending.pop(eng, None)
        if pend:
            for o in pend:
                deps[id(o)] = o
        deps.pop(id(op), None)
        op.deps = tuple(deps.values())
        op.pos = len(self.ops[eng])
        self.ops[eng].append(op)
        if dkey is not None:
            assert self.dma_eng.setdefault(dkey, eng) == eng
            self.dma_tot[dkey] = self.dma_tot.get(dkey, 0) + dinc
            op.dkey = dkey
            op.dcnt = self.dma_tot[dkey]
            self.last_dma[dkey] = op
        for r in reads:
            d = self.readers.setdefault(r, {})
            d[eng if dkey is None else ("dma", id(op))] = op
        for wr in writes:
            self.lastw[wr] = op
            self.readers[wr] = {}
        return op

    def resolve(self):
        for e in ENGS:
            for op in self.ops[e]:
                for d in op.deps:
                    if d.dkey is not None:
                        continue
                    if d.eng == op.eng:
                        if op.eng == "pe" or op.dkey is not None:
                            if op.eng == "pe":
                                continue
                        if op.pos - d.pos > 3 and op.dkey is None:
                            continue
                    d.sig = True
        for e in ENGS:
            c = 0
            for op in self.ops[e]:
                if op.sig:
                    c += 1
                    op.idx = c
        for e in ENGS:
            known = {}
            for op in self.ops[e]:
                need = {}
                for d in op.deps:
                    if d.dkey is not None:
                        k = ("d", d.dkey)
                        v = d.dcnt
                    else:
                        if not d.sig:
                            continue
                        if d.eng == op.eng:
                            if op.eng == "pe":
                                continue
                            if op.pos - d.pos > 3 and op.dkey is None:
                                continue
                        k = ("e", d.eng)
                        v = d.idx
                    if v > need.get(k, 0):
                        need[k] = v
                w = []
                for k, v in need.items():
                    if v > known.get(k, 0):
                        known[k] = v
                        w.append((k, v))
                op.waits = tuple(w)


def _np_bf16():
    import ml_dtypes
    return ml_dtypes.bfloat16


class Cfg:
    def __init__(self, S=2048, NPG=64, NPOOL=10240, layers=(0, 1, 2, 3), mixers=True):
        self.S = S
        self.NT = S // 128
        self.NPG = NPG
        self.NPOOL = NPOOL
        self.TT = S + NSAMP
        self.layers = layers
        self.mixers = mixers


def rope_tables(cfg):
    half = ROPE // 2
    inv = (np.float32(10000.0) ** (-np.arange(half, dtype=np.float32) / np.float32(half))).astype(np.float32)
    pos = np.concatenate([np.arange(cfg.S), np.full(NSAMP, cfg.NPG * PAGE)]).astype(np.float32)
    ang = (pos[:, None] * inv[None, :]).astype(np.float32)
    cos = np.cos(ang).astype(np.float32)
    sin = np.sin(ang).astype(np.float32)
    tm = np.zeros((128, cfg.NT + 1, 64), np.float32)
    for t in range(cfg.NT):
        tm[:, t, :32] = cos[t * 128:(t + 1) * 128]
        tm[:, t, 32:] = sin[t * 128:(t + 1) * 128]
    tm[:NSAMP, cfg.NT, :32] = cos[cfg.S:]
    tm[:NSAMP, cfg.NT, 32:] = sin[cfg.S:]
    fm = np.zeros((64, 2, cfg.TT), np.float32)
    fm[:32, 0] = cos.T
    fm[32:, 0] = cos.T
    fm[:32, 1] = sin.T
    fm[32:, 1] = sin.T
    return tm, fm


def const_inputs(cfg):
    tm, fm = rope_tables(cfg)
    ident = np.eye(128, dtype=np.float32)
    cm = np.where(np.arange(128)[None, :] <= np.arange(128)[:, None], 0.0, NEG).astype(np.float32)
    tril = np.tril(np.ones((128, 128), np.float32))
    sm = np.full((HEADS, NSAMP, NSAMP), NEG, np.float32)
    for b in range(NSAMP):
        sm[:, b, b] = 0.0
    pc = np.ones((128, 8, 15), np.float32)
    for g, w in enumerate((2, 4, 8, 16)):
        for t in range(15):
            pc[:, 2 * g:2 * g + 2, t] = w / min(w, t + 1)
    return {"c_ident": ident, "c_cmask": cm, "c_tril": tril, "c_smask": sm.reshape(HEADS, NSAMP * NSAMP),
            "c_rope_tm": tm, "c_rope_fm": fm, "c_poolc": pc.reshape(128, 120),
            "c_pmod": (np.arange(128) % 16).astype(np.float32).reshape(128, 1)}


class K:
    def __init__(self, cfg):
        self.cfg = cfg
        self.nc = bass.Bass("TRN2", target_bir_lowering=False)
        self.P = Prog()
        self.dram = {}
        self.off = 0
        self.sems = {}

    def din(self, name, shape, dt=F32):
        self.dram[name] = self.nc.dram_tensor(name, list(shape), dt, kind="ExternalInput").ap()
        return self.dram[name]

    def dout(self, name, shape, dt=F32):
        self.dram[name] = self.nc.dram_tensor(name, list(shape), dt, kind="ExternalOutput").ap()
        return self.dram[name]

    def alloc(self, nbytes):
        o = self.off
        self.off += (nbytes + 63) // 64 * 64
        assert self.off <= self.arena_bytes, (self.off, self.arena_bytes, getattr(self, 'phase_kind', '?'), self.phase_off)
        return o

    def view(self, off, shape, dt=F32, parts=128):
        n = int(np.prod(shape))
        esz = 4 if dt in (F32, I32) else 2
        nb = n * esz
        assert off % 4 == 0 and nb % 4 == 0
        ap = self.arena[0:parts, off // 4:(off + nb) // 4]
        if dt != F32:
            ap = ap.bitcast(dt)
        if len(shape) == 2:
            ap = ap.rearrange("p (a b) -> p a b", a=shape[0])
        elif len(shape) == 3:
            ap = ap.rearrange("p (a b c) -> p a b c", a=shape[0], b=shape[1])
        elif len(shape) == 4:
            ap = ap.rearrange("p (a b c d) -> p a b c d", a=shape[0], b=shape[1], c=shape[2])
        return ap

    def buf(self, shape, dt=F32):
        n = int(np.prod(shape))
        esz = 4 if dt in (F32, I32) else 2
        off = self.alloc(n * esz)
        return self.view(off, shape, dt)

    def pe(self, fn, reads=(), writes=()):
        return self.P.add("pe", fn, reads, writes)

    def act(self, fn, reads=(), writes=()):
        return self.P.add("act", fn, reads, writes)

    def dve(self, fn, reads=(), writes=()):
        return self.P.add("dve", fn, reads, writes)

    def dma(self, q, out, in_, key, reads=(), writes=(), **kw):
        return self.P.add(q, lambda e: e.dma_start(out=out, in_=in_, **kw), reads, writes, dkey=key)

    def mm(self, out, lhsT, rhs, start, stop, reads=(), writes=()):
        return self.pe(lambda e: e.matmul(out, lhsT=lhsT, rhs=rhs, start=start, stop=stop), reads, writes)

    def tp(self, out, in_, ident, reads=(), writes=()):
        return self.pe(lambda e: e.transpose(out=out, in_=in_, identity=ident), reads, writes)

    def emit(self):
        P = self.P
        P.resolve()
        nc = self.nc
        esem = self.esem
        dsem = self.dsem
        engobj = {"pe": "tensor", "act": "scalar", "dve": "vector", "pool": "gpsimd", "sp": "sync"}

        def run(ename):
            def body(eng):
                for op in P.ops[ename]:
                    for (k, v) in op.waits:
                        if k[0] == "d":
                            eng.wait_ge(dsem[k[1]], v)
                        else:
                            eng.wait_ge(esem[k[1]], v)
                    ins = op.fn(eng)
                    if op.dkey is not None:
                        ins.then_inc(dsem[op.dkey], 16)
                    elif op.sig:
                        ins.then_inc(esem[ename], 1)
                if ename == "sp":
                    for key, tot in P.dma_tot.items():
                        eng.wait_ge(dsem[key], tot)
            return body

        with nc.Block() as block:
            for ename in ENGS:
                if P.ops[ename] or ename == "sp":
                    getattr(block, engobj[ename])(run(ename))


def _needs(op, d):
    if d.dkey is not None:
        return True
    if d.eng != op.eng:
        return True
    if op.dkey is not None:
        return True
    if op.eng == "pe":
        return False
    return (op.pos - d.pos) <= 3


def _resolve(self):
    for e in ENGS:
        for op in self.ops[e]:
            for d in op.deps:
                if d.dkey is None and _needs(op, d):
                    d.sig = True
    for e in ENGS:
        c = 0
        for op in self.ops[e]:
            if op.sig:
                c += 1
                op.idx = c
    for e in ENGS:
        known = {}
        for op in self.ops[e]:
            need = {}
            for d in op.deps:
                if not _needs(op, d):
                    continue
                if d.dkey is not None:
                    k = ("d", d.dkey)
                    v = d.dcnt
                else:
                    k = ("e", d.eng)
                    v = d.idx
                if v > need.get(k, 0):
                    need[k] = v
            w = []
            for k, v in need.items():
                if v > known.get(k, 0):
                    known[k] = v
                    w.append((k, v))
            op.waits = tuple(w)


Prog.resolve = _resolve


def _stt(k, eng, out, in0, scalar, in1, op0, op1, reads, writes):
    return k.P.add(eng, lambda e: e.scalar_tensor_tensor(out=out, in0=in0, scalar=scalar, in1=in1, op0=op0, op1=op1),
                   reads, writes)


def _tt(k, eng, out, in0, in1, op, reads, writes):
    return k.P.add(eng, lambda e: e.tensor_tensor(out=out, in0=in0, in1=in1, op=op), reads, writes)


def _ts(k, eng, out, in0, s1, s2, op0, op1, reads, writes):
    if s2 is None:
        return k.P.add(eng, lambda e: e.tensor_scalar(out=out, in0=in0, scalar1=s1, scalar2=None, op0=op0), reads, writes)
    return k.P.add(eng, lambda e: e.tensor_scalar(out=out, in0=in0, scalar1=s1, scalar2=s2, op0=op0, op1=op1),
                   reads, writes)


def _cp(k, eng, out, in_, reads, writes):
    if eng == "act":
        return k.P.add(eng, lambda e: e.activation(out=out, in_=in_, func=AF.Copy), reads, writes)
    return k.P.add(eng, lambda e: e.tensor_copy(out=out, in_=in_), reads, writes)


def _af(k, out, in_, func, reads, writes, **kw):
    return k.P.add("act", lambda e: e.activation(out=out, in_=in_, func=func, **kw), reads, writes)


def _red(k, eng, out, in_, op, reads, writes):
    return k.P.add(eng, lambda e: e.tensor_reduce(out=out, in_=in_, axis=AX.X, op=op), reads, writes)


def _recip(k, out, in_, reads, writes):
    return k.P.add("dve", lambda e: e.reciprocal(out=out, in_=in_), reads, writes)


def _memset(k, eng, ap, val, reads, writes):
    return k.P.add(eng, lambda e: e.memset(ap, val), reads, writes)


K.stt = _stt
K.tt = _tt
K.ts = _ts
K.cp = _cp
K.af = _af
K.red = _red
K.recip = _recip
K.memset = _memset


def _xr(self, gi, k, lo=0, hi=512):
    if gi == self.NG:
        return [("xT", gi, 0, k)]
    return [("xT", gi, hp, k) for hp in range(2) if lo < (hp + 1) * 256 and hi > hp * 256]


def _nb(self):
    b = self.bank
    self.bank = (self.bank + 1) % 8
    return b


def _setup(self):
    cfg = self.cfg
    S, NT, TT = cfg.S, cfg.NT, cfg.TT
    self.bank = 0
    self.tiles = [(t, t * 128, 128) for t in range(NT)] + [(NT, S, NSAMP)]
    ng = NT // 4
    self.groups = [(g, g * 512, 512, [self.tiles[4 * g + i] for i in range(4)]) for g in range(ng)]
    self.groups.append((ng, S, NSAMP, [self.tiles[NT]]))
    self.NG = ng
    self.passes = [[self.groups[g]] for g in range(ng)]
    self.passes[-1].append(self.groups[ng])
    self.x = self.buf([NT + 1, D])
    self.xT = self.buf([KD, TT], BF16)
    self.gb = [self.buf([2, D])]
    self.ident = self.buf([128])
    self.identb = self.buf([128], BF16)
    self.st = self.buf([4, 32])
    self.ones_row = self.buf([128], BF16)
    self.gbcnt = 0
    self.qls = self.buf([2, NSAMP, HEADS], BF16)
    self.qrs = self.buf([NSAMP, HEADS], BF16)
    nch = max(cfg.NPG // 8, 1)
    self.idxi = self.buf([NSAMP, nch], I32)
    self.idxf = self.buf([NSAMP, nch])
    self.pmod = self.buf([1])
    self.cmaskb = self.buf([128], BF16)
    self.cmaskf = self.buf([128])
    self.smask = self.buf([NSAMP * NSAMP])
    self.phase_off = self.off
    d = self.dram
    self.dma("sp", self.ident, d["c_ident"], "ident", writes=[("ident",)])
    self.cp("dve", self.identb, self.ident, [("ident",)], [("identb",)])
    self.memset("dve", self.ones_row, 1.0, [], [("ones",)])
    pt3 = d["page_table"].rearrange("b (c k) -> b c k", k=8)
    for k8 in range(8):
        self.dma("sp", self.idxi[16 * k8:16 * k8 + 16, :, :], pt3[:, :, k8].partition_broadcast(16), ("pt", k8),
                 writes=[("idxi", k8)], allow_slow_non_contiguous=True)
    self.dma("sp", self.pmod, d["c_pmod"], "pmod", writes=[("pmod",)])
    self.cp("dve", self.idxf, self.idxi, [("idxi", k8) for k8 in range(8)], [("idxf",)])
    self.ts("dve", self.idxf, self.idxf, 16.0, self.pmod[:, 0:1], ALU.mult, ALU.add, [("idxf",), ("pmod",)], [("idxf",)])
    self.dma("sp", self.cmaskf, d["c_cmask"], "cmask", writes=[("cmaskf",)])
    self.cp("dve", self.cmaskb, self.cmaskf, [("cmaskf",)], [("cmaskb",)])
    self.dma("sp", self.smask[:HEADS, :], d["c_smask"], "smask", writes=[("smask",)])
    xp = d["x_prompt"].rearrange("(t p) d -> p t d", p=128)
    for g in range(ng):
        self.dma("sp", self.x[:, 4 * g:4 * g + 4, :], xp[:, 4 * g:4 * g + 4, :], ("xin", g),
                 writes=[("x", t, h) for t in range(4 * g, 4 * g + 4) for h in range(2)])
    self.dma("sp", self.x[:NSAMP, NT, :], d["x_sample"], ("xin", ng), writes=[("x", NT, 0), ("x", NT, 1)])
    for grp in self.groups:
        self.t_phase(grp)


def _load_gb(self, li, w):
    slot = 0
    d = self.dram
    self.dma("sp", self.gb[slot][:, 0, :], d["ln_g"][li, w, :].partition_broadcast(128), ("gb", slot, 0),
             writes=[("gb", slot, 0)])
    self.dma("sp", self.gb[slot][:, 1, :], d["ln_b"][li, w, :].partition_broadcast(128), ("gb", slot, 1),
             writes=[("gb", slot, 1)])
    return slot


def _ln_tile(self, tile, banks, coef, gslot):
    t, c0, n = tile
    x, ps = self.x, self.ps
    sl = t % 4
    st = self.st
    rx = [("x", t, 0), ("x", t, 1)]
    rs = ("st", sl)
    if banks is not None:
        for hf in range(2):
            self.stt("dve", x[:n, t, hf * 512:(hf + 1) * 512], ps[:n, banks[hf], :], float(coef),
                     x[:n, t, hf * 512:(hf + 1) * 512], ALU.mult, ALU.add,
                     [("ps", banks[hf]), ("x", t, hf)], [("x", t, hf)])
    for hf in range(2):
        self.P.add("dve", (lambda e, hf=hf: e.bn_stats(out=st[:n, sl, hf * 6:hf * 6 + 6],
                                                       in_=x[:n, t, hf * 512:(hf + 1) * 512])),
                   [("x", t, hf)], [("st", sl, hf)])
    self.P.add("dve", lambda e: e.bn_aggr(out=st[:n, sl, 12:14],
                                          in_=st[:n, sl, 0:12].rearrange("p (a b) -> p a b", a=2)),
               [("st", sl, 0), ("st", sl, 1)], [("st", sl, "mv")])
    self.af(st[:n, sl, 14:15], st[:n, sl, 13:14], AF.Sqrt, [("st", sl, "mv")], [("st", sl, "sd")],
            bias=float(LN_EPS / ALPHA ** 2), scale=1.0)
    self.recip(st[:n, sl, 15:16], st[:n, sl, 14:15], [("st", sl, "sd")], [("st", sl, "rs")])
    g = self.gb[gslot]
    for hf in range(2):
        sli = slice(hf * 512, (hf + 1) * 512)
        self.stt("dve", x[:n, t, sli], x[:n, t, sli], st[:n, sl, 12:13], g[:n, 0, sli], ALU.subtract, ALU.mult,
                 [("x", t, hf), ("st", sl, "mv"), ("gb", gslot, 0)], [("x", t, hf)])
    for hf in range(2):
        sli = slice(hf * 512, (hf + 1) * 512)
        self.stt("dve", x[:n, t, sli], x[:n, t, sli], st[:n, sl, 15:16], g[:n, 1, sli], ALU.mult, ALU.add,
                 [("x", t, hf), ("st", sl, "rs"), ("gb", gslot, 1)], [("x", t, hf)])


def _t_phase(self, grp):
    gi, c0, ncols, tiles = grp
    x, ps, xT = self.x, self.ps, self.xT
    for k in range(KD):
        b = self.nb()
        for (t, tc0, n) in tiles:
            off = tc0 - c0
            self.tp(ps[:, b, off:off + n], x[:n, t, k * 128:(k + 1) * 128], self.ident[:n, :n],
                    [("x", t, k // 4), ("ident",)], [("ps", b)])
        g0 = (c0 // 512) * 512 if gi != self.NG else c0
        self.cp("act", xT[:, k, c0:c0 + ncols], ps[:, b, 0:ncols], [("ps", b)], self.xr(gi, k, c0 - g0, c0 - g0 + ncols))


def _ffn(self, li, w, lnw):
    cfg = self.cfg
    self.P.phase = "ffn%d_%d" % (li, w)
    d = self.dram
    ps, xT = self.ps, self.xT
    wg = d["ffn_w_gate"][li, w].rearrange("(k p) n -> p k n", p=128)
    wu = d["ffn_w_up"][li, w].rearrange("(k p) n -> p k n", p=128)
    wd = d["ffn_w_down"][li, w].rearrange("(c p) n -> p c n", p=128)
    gslot = self.load_gb(li, lnw)
    coef = 0.5 / ALPHA
    for pgroups in self.passes:
        pc0 = pgroups[0][1]
        for u in range(7):
            slot = self.cnt_gu % 2
            self.cnt_gu += 1
            wt = self.wgu[slot]
            self.dma("pool", wt[:, 0, :, :], wg[:, :, u * 384:(u + 1) * 384], ("wgu", slot, 0), writes=[("wgu", slot, 0)])
            self.dma("pool", wt[:, 1, :, :], wu[:, :, u * 384:(u + 1) * 384], ("wgu", slot, 1), writes=[("wgu", slot, 1)])
            for c in range(3):
                ff = u * 3 + c
                for gidx, (gi, c0, ncols, tiles) in enumerate(pgroups):
                    bA, bB = self.nb(), self.nb()
                    for which, b in ((0, bA), (1, bB)):
                        for k in range(KD):
                            self.mm(ps[:, b, 0:ncols], wt[:, which, k, c * 128:(c + 1) * 128], xT[:, k, c0:c0 + ncols],
                                    k == 0, k == KD - 1, [("wgu", slot, which)] + self.xr(gi, k), [("ps", b)])
                    ss = self.cnt_sg % 2
                    self.cnt_sg += 1
                    self.af(self.sg[ss][:, 0:ncols], ps[:, bA, 0:ncols], AF.Silu, [("ps", bA)], [("sg", ss)])
                    self.tt("dve", self.hT[:, ff, c0 - pc0:c0 - pc0 + ncols], self.sg[ss][:, 0:ncols], ps[:, bB, 0:ncols],
                            ALU.mult, [("sg", ss), ("ps", bB)], [("hT", ff, gidx)])
        for gidx, (gi, c0, ncols, tiles) in enumerate(pgroups):
            for u in range(7):
                slot = self.cnt_wd % 2
                self.cnt_wd += 1
                wt = self.wdb[slot]
                self.dma("pool", wt, wd[:, u * 3:(u + 1) * 3, :], ("wd", slot), writes=[("wd", slot)])
                for i, (t, tc0, n) in enumerate(tiles):
                    for hf in range(2):
                        for c in range(3):
                            ff = u * 3 + c
                            self.mm(ps[:n, 2 * i + hf, :], self.hT[:, ff, tc0 - pc0:tc0 - pc0 + n],
                                    wt[:, c, hf * 512:(hf + 1) * 512], u == 0 and c == 0, u == 6 and c == 2,
                                    [("hT", ff, gidx), ("wd", slot)], [("ps", 2 * i + hf)])
            for i, tile in enumerate(tiles):
                self.ln_tile(tile, (2 * i, 2 * i + 1), coef, gslot)
            self.t_phase((gi, c0, ncols, tiles))


def _alloc_ffn(self):
    if self.phase_kind != "ffn":
        self.P.fence()
    self.phase_kind = "ffn"
    self.off = self.phase_off
    pw = 512 + NSAMP
    self.hT = self.buf([NFF, pw], BF16)
    self.wgu = [self.buf([2, KD, 384], BF16) for _ in range(2)]
    self.wdb = [self.buf([3, D], BF16) for _ in range(2)]
    self.sg = [self.buf([512]) for _ in range(2)]
    self.ffn_end = self.off


def _ln_only(self, li, lnw):
    gslot = self.load_gb(li, lnw)
    for grp in self.groups:
        for tile in grp[3]:
            self.ln_tile(tile, None, 0.0, gslot)
        self.t_phase(grp)


def _finish(self):
    d = self.dram
    cfg = self.cfg
    yp = d["y_prompt"].rearrange("(t p) d -> p t d", p=128)
    for g in range(self.NG):
        self.dma("sp", yp[:, 4 * g:4 * g + 4, :], self.x[:, 4 * g:4 * g + 4, :], ("yout", g),
                 reads=[("x", t, h) for t in range(4 * g, 4 * g + 4) for h in range(2)])
    self.dma("sp", d["y_sample"], self.x[:NSAMP, cfg.NT, :], ("yout", self.NG),
             reads=[("x", cfg.NT, 0), ("x", cfg.NT, 1)])


K.nb = _nb
K.xr = _xr
K.setup = _setup
K.load_gb = _load_gb
K.ln_tile = _ln_tile
K.t_phase = _t_phase
K.ffn = _ffn
K.alloc_ffn = _alloc_ffn
K.ln_only = _ln_only
K.finish = _finish


IN_SPECS = [
    ("x_prompt", lambda c: (c.S, D), F32), ("x_sample", lambda c: (NSAMP, D), F32),
    ("cache_kv_latent", lambda c: (2, c.NPOOL, PAGE, KVR), F32), ("cache_k_rope", lambda c: (2, c.NPOOL, PAGE, ROPE), F32),
    ("state_pool", lambda c: (NSAMP * 15, D), F32), ("page_table", lambda c: (NSAMP, c.NPG), I32),
    ("ln_g", lambda c: (DEPTH, 3, D), F32), ("ln_b", lambda c: (DEPTH, 3, D), F32),
    ("ffn_w_gate", lambda c: (DEPTH, 2, D, DFF), F32), ("ffn_w_up", lambda c: (DEPTH, 2, D, DFF), F32),
    ("ffn_w_down", lambda c: (DEPTH, 2, DFF, D), F32),
    ("a_w_in", lambda c: (2, D, 704), F32), ("a_q_norm", lambda c: (2, QR), F32), ("a_kv_norm", lambda c: (2, KVR), F32),
    ("a_w_uq", lambda c: (2, QR, 1536), F32), ("a_w_uk", lambda c: (2, KVR, HEADS, 128), F32),
    ("a_w_uv", lambda c: (2, KVR, HEADS, 128), F32), ("a_w_o", lambda c: (2, D, D), F32),
    ("b_w_in", lambda c: (1, D, 2 * DC), F32), ("b_b_in", lambda c: (1, 2 * DC), F32),
    ("b_v_norm_g", lambda c: (1, DC), F32), ("b_v_norm_b", lambda c: (1, DC), F32),
    ("b_w_s", lambda c: (1, HEADS, 128, 128), F32), ("b_b_s", lambda c: (1, HEADS, 128), F32),
    ("b_w_out", lambda c: (1, DC, D), F32),
    ("c_w_in", lambda c: (1, D, D), F32), ("c_w_grp", lambda c: (1, 4, 256, 256), F32),
    ("c_scale", lambda c: (1, D), F32), ("c_w_out", lambda c: (1, D, D), F32),
    ("c_ident", lambda c: (128, 128), F32), ("c_cmask", lambda c: (128, 128), F32), ("c_tril", lambda c: (128, 128), F32),
    ("c_smask", lambda c: (HEADS, NSAMP * NSAMP), F32), ("c_rope_tm", lambda c: (128, c.NT + 1, 64), F32),
    ("c_rope_fm", lambda c: (64, 2, c.TT), F32), ("c_poolc", lambda c: (128, 120), F32),
    ("c_pmod", lambda c: (128, 1), F32),
]
OUT_SPECS = [
    ("y_prompt", lambda c: (c.S, D)), ("y_sample", lambda c: (NSAMP, D)),
    ("o_ckv_p", lambda c: (2, c.S, KVR)), ("o_kr_p", lambda c: (2, c.S, ROPE)),
    ("o_ckv_s", lambda c: (2, NSAMP, KVR)), ("o_kr_s", lambda c: (2, NSAMP, ROPE)),
    ("o_v_s", lambda c: (NSAMP, DC)), ("o_pool_p", lambda c: (15, D)), ("o_pool_s", lambda c: (NSAMP, 15, D)),
]
DEBUG = False


def build(cfg):
    from contextlib import ExitStack
    k = K(cfg)
    nc = k.nc
    for name, shp, dt in IN_SPECS:
        k.din(name, shp(cfg), dt)
    for name, shp in OUT_SPECS:
        k.dout(name, shp(cfg))
    if DEBUG:
        k.dout("o_dbg", (128, 8192))
    k.dbgcol = 0
    k.arena_bytes = SBUF_BYTES
    k.cnt_gu = k.cnt_wd = k.cnt_sg = 0
    k.cnt_wv = k.cnt_vg = k.cnt_wu = k.cnt_wo = k.cnt_uc = 0
    k.phase_kind = "ffn"
    with ExitStack() as es:
        k.arena = es.enter_context(nc.sbuf_tensor("arena", [128, SBUF_BYTES // 4], F32))
        k.ps = es.enter_context(nc.psum_tensor("ps", [128, 8, 512], F32))
        k.setup()
        for li in cfg.layers:
            kind = li % 3
            j = li // 3
            k.alloc_ffn()
            k.ffn(li, 0, 0)
            if not cfg.mixers:
                k.ln_only(li, 1)
            elif kind == 0:
                k.mla(li, j)
            elif kind == 1:
                k.cmlp(li)
            else:
                k.poolmix(li)
            k.alloc_ffn()
            k.ffn(li, 1, 2)
        k.finish()
        k.esem = {e: es.enter_context(nc.semaphore("e_" + e)) for e in ENGS}
        k.dsem = {}
        for i, key in enumerate(k.P.dma_tot):
            k.dsem[key] = es.enter_context(nc.semaphore("d%d" % i))
        k.emit()
    return nc, k


def make_in_maps(cfg, inputs, ncores=8):
    consts = const_inputs(cfg)
    maps = []
    for c in range(ncores):
        m = {}
        m["x_prompt"] = np.ascontiguousarray(inputs["x_prompt"][c])
        m["x_sample"] = np.ascontiguousarray(inputs["x_sample"][c * NSAMP:(c + 1) * NSAMP, 0])
        m["cache_kv_latent"] = inputs["cache_kv_latent"]
        m["cache_k_rope"] = inputs["cache_k_rope"]
        m["state_pool"] = np.ascontiguousarray(inputs["state_pool"][0, c * NSAMP:(c + 1) * NSAMP]).reshape(NSAMP * 15, D)
        m["page_table"] = np.ascontiguousarray(inputs["page_table"][c * NSAMP:(c + 1) * NSAMP]).astype(np.int32)
        for name, _, _ in IN_SPECS:
            if name in m:
                continue
            if name in consts:
                m[name] = consts[name]
            else:
                m[name] = np.asarray(inputs[name])
        maps.append(m)
    return maps


def gather_outputs(cfg, results, ncores=8):
    S = cfg.S
    R = results
    y_p = np.stack([R[c]["y_prompt"] for c in range(ncores)])
    y_s = np.concatenate([R[c]["y_sample"] for c in range(ncores)])[:, None, :]
    ckv_p = np.stack([R[c]["o_ckv_p"] for c in range(ncores)], axis=1)
    kr_p = np.stack([R[c]["o_kr_p"] for c in range(ncores)], axis=1)
    ckv_s = np.concatenate([R[c]["o_ckv_s"] for c in range(ncores)], axis=1)[:, :, None, :]
    kr_s = np.concatenate([R[c]["o_kr_s"] for c in range(ncores)], axis=1)[:, :, None, :]
    v_s = np.concatenate([R[c]["o_v_s"] for c in range(ncores)])[None, :, None, :]
    pool_p = np.stack([R[c]["o_pool_p"] for c in range(ncores)])[None]
    pool_s = np.concatenate([R[c]["o_pool_s"] for c in range(ncores)])[None]
    outs = (y_p, y_s, ckv_p, kr_p, ckv_s, kr_s, v_s, pool_p, pool_s)
    return tuple(np.ascontiguousarray(o, dtype=np.float32) for o in outs)


_CACHE = {}


def kernel(**inputs):
    cfg = Cfg()
    inputs = {k_: np.asarray(v) for k_, v in inputs.items()}
    if "nc" not in _CACHE:
        _CACHE["nc"] = build(cfg)[0]
    nc = _CACHE["nc"]
    maps = make_in_maps(cfg, inputs)
    res = run_bass_kernel_spmd(nc, maps, core_ids=list(range(8)))
    return gather_outputs(cfg, res.results)


def _load_colvec(self, dram1d, nk, dst, key, rname):
    tmp = self.buf([128])
    self.dma("sp", tmp[:nk, :], dram1d.rearrange("(k p) -> k p", p=128), key, writes=[(rname, "tmp")])
    b = self.nb()
    self.tp(self.ps[:, b, 0:nk], tmp[:nk, :], self.ident[:nk, :nk], [(rname, "tmp"), ("ident",)], [("ps", b)])
    self.cp("dve", dst, self.ps[:, b, 0:nk], [("ps", b)], [(rname,)])


def _proj_out_ln(self, grp, actT, kc_n, w_sb, wres, act_res, coef, gslot):
    gi, c0, ncols, tiles = grp
    ps = self.ps
    for (t, tc0, n) in tiles:
        b0, b1 = self.nb(), self.nb()
        for hf, b in ((0, b0), (1, b1)):
            for kc in range(kc_n):
                self.mm(ps[:n, b, :], actT[:, kc, tc0 - c0:tc0 - c0 + n], w_sb[:, kc, hf * 512:(hf + 1) * 512],
                        kc == 0, kc == kc_n - 1, [act_res(kc), wres], [("ps", b)])
        self.ln_tile((t, tc0, n), (b0, b1), coef, gslot)
    self.t_phase(grp)


def _poolmix(self, li):
    cfg = self.cfg
    d = self.dram
    S, NT = cfg.S, cfg.NT
    ps, xT = self.ps, self.xT
    self.P.phase = "pool"
    self.P.fence()
    self.phase_kind = "pool"
    self.off = self.phase_off
    cwin = self.buf([KD, D], BF16)
    cwout = self.buf([KD, D], BF16)
    wgrp = self.buf([4, 2, 256], BF16)
    cscale = self.buf([8])
    poolc = self.buf([8, 15])
    hb_off = self.off
    hb = [self.buf([8, 527]) for _ in range(2)]
    tmpA = self.buf([2, 527])
    tmpB = self.buf([2, 527])
    plT = self.buf([8, 512], BF16)
    zT = self.buf([8, 512], BF16)
    hrow = self.view(hb_off, [D])
    hsrow = self.view(hb_off + 4096, [D])
    stt_tm = self.view(hb_off + 8192, [2, D])
    hsT = self.view(hb_off + 16384, [8, NSAMP, 16])
    sw = self.view(hb_off + 16384 + 8192, [8, NSAMP])
    assert 16384 + 8192 + 512 <= 2 * 8 * 527 * 4
    gslot = self.load_gb(li, 1)
    coef = 1.0 / ALPHA
    WINS = (2, 4, 8, 16)
    self.dma("pool", cwin, d["c_w_in"][0].rearrange("(k p) n -> p k n", p=128), "cwin", writes=[("cwin",)])
    self.dma("pool", cwout, d["c_w_out"][0].rearrange("(k p) n -> p k n", p=128), "cwout", writes=[("cwout",)])
    self.dma("pool", wgrp, d["c_w_grp"][0].rearrange("g (c p) e -> p g c e", p=128), "wgrp", writes=[("wgrp",)])
    self.dma("sp", poolc, d["c_poolc"].rearrange("p (a b) -> p a b", a=8), "poolc", writes=[("poolc",)])
    self.load_colvec(d["c_scale"][0], 8, cscale, "cscale", "cscale")
    lt = self.tiles[NT - 1]
    for (tile, stage, n) in ((lt, hrow, 128), (self.tiles[NT], hsrow, NSAMP)):
        t, tc0, _ = tile
        gi = min(t // 4, self.NG)
        for hf in range(2):
            b = self.nb()
            for k in range(KD):
                self.mm(ps[:n, b, :], xT[:, k, tc0:tc0 + n], cwin[:, k, hf * 512:(hf + 1) * 512], k == 0, k == KD - 1,
                        self.xr(gi, k) + [("cwin",)], [("ps", b)])
            self.cp("act", stage[:n, hf * 512:(hf + 1) * 512], ps[:n, b, :], [("ps", b)], [("hrow", t, hf)])
    self.dma("sp", d["o_pool_p"], hrow[113:128, :], "o_pool_p", reads=[("hrow", NT - 1, 0), ("hrow", NT - 1, 1)])
    self.dma("sp", d["o_pool_s"][:, 14, :], hsrow[:NSAMP, :], "o_pool_s", reads=[("hrow", NT, 0), ("hrow", NT, 1)])
    sp3 = d["state_pool"].rearrange("(b r) d -> b r d", r=15)
    self.dma("sp", d["o_pool_s"][:, 0:14, :], sp3[:, 1:15, :], "o_pool_s2")
    P_ = self.P
    P_.fence()
    prev = None
    for grp in self.groups:
        gi, c0, ncols, tiles = grp
        samp = gi == self.NG
        cur = gi % 2
        H = hb[cur]
        if samp:
            P_.fence()
            self.dma("sp", stt_tm[:120, :, :], d["state_pool"].rearrange("(a p) d -> p a d", p=120), "stt", writes=[("stt",)])
            for a in range(2):
                for c in range(8):
                    b = self.nb()
                    self.tp(ps[:, b, 0:120], stt_tm[:120, a, c * 128:(c + 1) * 128], self.ident[:120, :120],
                            [("stt",), ("ident",)], [("ps", b)])
                    self.cp("dve", hsT[:, c, 8 * a:8 * a + 8, 0:15], ps[:, b, 0:120].rearrange("p (s r) -> p s r", r=15),
                            [("ps", b)], [("hsT", c, a)])
        if not samp:
            if gi == 0:
                self.memset("dve", H[:, :, 0:15], 0.0, [], [("hb", cur, "halo")])
            else:
                self.cp("act", H[:, :, 0:15], hb[prev][:, :, 512:527], [("hb", prev, c) for c in range(8)],
                        [("hb", cur, "halo")])
        for c in range(8):
            b = self.nb()
            for k in range(KD):
                self.mm(ps[:, b, 0:ncols], cwin[:, k, c * 128:(c + 1) * 128], xT[:, k, c0:c0 + ncols], k == 0, k == KD - 1,
                        [("cwin",)] + self.xr(gi, k), [("ps", b)])
            if samp:
                self.cp("act", hsT[:, c, :, 15], ps[:, b, 0:ncols], [("ps", b)], [("hsT", c, 2)])
            else:
                self.cp("act", H[:, c, 15:527], ps[:, b, 0:512], [("ps", b)], [("hb", cur, c)])
        for g, w in enumerate(WINS):
            cs = slice(2 * g, 2 * g + 2)
            rin = [("hb", cur, 2 * g), ("hb", cur, 2 * g + 1), ("hb", cur, "halo")]
            if samp:
                for c in (2 * g, 2 * g + 1):
                    self.red("dve", sw[:, c, :], hsT[:, c, :, 16 - w:16], ALU.add,
                             [("hsT", c, 0), ("hsT", c, 1), ("hsT", c, 2)], [("sw", c)])
                    self.stt("dve", plT[:, c, 0:NSAMP], sw[:, c, :], 1.0 / w, hsT[:, c, :, 15], ALU.mult, ALU.subtract,
                             [("sw", c), ("hsT", c, 2)], [("plT", c)])
                continue
            src = H[:, cs, :]
            lo = 0
            bufs = (tmpA, tmpB)
            for si in range(g + 1):
                sh = 1 << si
                dst = bufs[si % 2]
                nlo = lo + sh
                self.tt("dve", dst[:, :, nlo:527], src[:, :, nlo:527], src[:, :, nlo - sh:527 - sh], ALU.add,
                        rin + [("ptmp", (si + 1) % 2)], [("ptmp", si % 2)])
                src = dst
                lo = nlo
            last = ("ptmp", g % 2)
            if gi == 0:
                self.tt("dve", src[:, :, 15:30], src[:, :, 15:30], poolc[:, cs, :], ALU.mult, [last, ("poolc",)], [last])
            self.stt("dve", plT[:, cs, :], src[:, :, 15:527], 1.0 / w, H[:, cs, 15:527], ALU.mult, ALU.subtract,
                     [last] + rin, [("plT", 2 * g), ("plT", 2 * g + 1)])
        for g in range(4):
            for ec in range(2):
                b = self.nb()
                for dc in range(2):
                    self.mm(ps[:, b, 0:ncols], wgrp[:, g, dc, ec * 128:(ec + 1) * 128], plT[:, 2 * g + dc, 0:ncols],
                            dc == 0, dc == 1, [("wgrp",), ("plT", 2 * g + dc)], [("ps", b)])
                self.af(zT[:, 2 * g + ec, 0:ncols], ps[:, b, 0:ncols], AF.Identity, [("ps", b), ("cscale",)],
                        [("zT", 2 * g + ec)], scale=cscale[:, 2 * g + ec:2 * g + ec + 1])
        if DEBUG and gi == 0:
            self.dbgmap = {}
            self.dbgmap["cscale"] = self.dbg(cscale, 128, 8, [("cscale",)])
            self.dbgmap["hb0"] = self.dbg(H[:, 0, :], 128, 527, [("hb", cur, 0), ("hb", cur, "halo")])
            self.dbgmap["hb7"] = self.dbg(H[:, 7, :], 128, 527, [("hb", cur, 7), ("hb", cur, "halo")])
            self.dbgmap["pl0"] = self.dbg(plT[:, 0, :], 128, 512, [("plT", 0)])
            self.dbgmap["pl7"] = self.dbg(plT[:, 7, :], 128, 512, [("plT", 7)])
            self.dbgmap["z0"] = self.dbg(zT[:, 0, :], 128, 512, [("zT", 0)])
            self.dbgmap["z7"] = self.dbg(zT[:, 7, :], 128, 512, [("zT", 7)])
            self.dbgmap["cwout"] = self.dbg(cwout[:, 0, :], 128, 1024, [("cwout",)])
        self.proj_out_ln(grp, zT, 8, cwout, ("cwout",), lambda kc: ("zT", kc), coef, gslot)
        prev = cur


def _dbg(self, ap, parts, n, reads):
    col = self.dbgcol
    self.dbgcol += n
    q = "sp" if ap.dtype == F32 else "pool"
    self.dma(q, self.dram["o_dbg"][0:parts, col:col + n], ap, ("dbg", col), reads=reads)
    return col


K.dbg = _dbg
K.load_colvec = _load_colvec
K.proj_out_ln = _proj_out_ln
K.poolmix = _poolmix


def _cmlp(self, li):
    cfg = self.cfg
    d = self.dram
    S, NT = cfg.S, cfg.NT
    ps, xT = self.ps, self.xT
    self.P.phase = "cmlp"
    self.P.fence()
    self.phase_kind = "cmlp"
    self.off = self.phase_off
    winu = [self.buf([KD, 256], BF16) for _ in range(2)]
    winv = [self.buf([KD, 512], BF16) for _ in range(2)]
    binv = [self.buf([512], BF16) for _ in range(2)]
    wout = [self.buf([2, D], BF16) for _ in range(2)]
    vgb = [self.buf([2, 512]) for _ in range(2)]
    vf_off = self.off
    v_f = self.buf([2, DC])
    vn = self.buf([2, DC], BF16)
    uc = [self.buf([256], BF16) for _ in range(2)]
    huc = [self.buf([256], BF16) for _ in range(2)]
    WsT = self.buf([HEADS, 128], BF16)
    wsS = self.buf([HEADS, NSAMP], BF16)
    bsrow = self.buf([HEADS, 128], BF16)
    bsS = self.buf([HEADS, NSAMP], BF16)
    tril = self.buf([128])
    w00 = self.buf([HEADS])
    bs0 = self.buf([HEADS])
    binu = self.buf([24])
    vst = self.buf([2, 48])
    stage = self.view(vf_off, [HEADS, 128])
    gslot = self.load_gb(li, 1)
    coef = 1.0 / ALPHA
    w_in = d["b_w_in"][0].rearrange("(k p) n -> p k n", p=128)
    w_out = d["b_w_out"][0].rearrange("(c p) n -> p c n", p=128)
    b_in = d["b_b_in"][0]
    vfall = [("vf", ti, b) for ti in range(2) for b in range(6)]
    self.dma("sp", stage, d["b_w_s"][0].rearrange("h i j -> i h j"), "ws", writes=vfall)
    self.dma("sp", tril, d["c_tril"], "tril", writes=[("tril",)])
    for h in range(HEADS):
        self.tt("dve", stage[:, h, :], stage[:, h, :], tril, ALU.mult, vfall + [("tril",)], [("wsm", h)])
    for h in range(HEADS):
        b = self.nb()
        self.tp(ps[:, b, 0:128], stage[:, h, :], self.ident, [("wsm", h), ("ident",)], [("ps", b)])
        self.cp("act", WsT[:, h, :], ps[:, b, 0:128], [("ps", b)], [("WsT",)])
    self.P.add("dve", lambda e: e.memset(vst[:, 0, 0:1], 0.0), [("wsm", h) for h in range(HEADS)], vfall)
    self.dma("pool", bsrow[0:1, :, :], d["b_b_s"][0:1], "bsrow", writes=[("bsrow",)])
    self.dma("sp", w00[:NSAMP, :], d["b_w_s"][0][:, 0, 0].partition_broadcast(NSAMP), "w00", writes=[("w00",)],
             allow_slow_non_contiguous=True)
    self.dma("sp", bs0[0:1, :], d["b_b_s"][0:1, :, 0], "bs0", writes=[("bs0",)], allow_slow_non_contiguous=True)
    for h in range(HEADS):
        self.ts("dve", wsS[:NSAMP, h, :], self.ident[:NSAMP, :NSAMP], w00[:NSAMP, h:h + 1], None, ALU.mult, None,
                [("w00",), ("ident",)], [("wsS",)])
        self.ts("dve", bsS[0:1, h, :], self.ones_row[0:1, 0:NSAMP], bs0[0:1, h:h + 1], None, ALU.mult, None,
                [("bs0",), ("ones",)], [("bsS",)])
    self.load_colvec(b_in[0:DC], 24, binu, "binu", "binu")
    plist = []
    for g in range(self.NG):
        for hp in range(2):
            tl = [self.tiles[4 * g + 2 * hp], self.tiles[4 * g + 2 * hp + 1]]
            plist.append((g, g * 512 + hp * 256, 256, tl))
    plist.append((self.NG, S, NSAMP, [self.tiles[NT]]))
    for (gi, c0, ncols, tiles) in plist:
        samp = gi == self.NG
        nt = len(tiles)
        lo = c0 - (gi * 512 if not samp else c0)
        for blk in range(6):
            sl = self.cnt_wv % 2
            self.cnt_wv += 1
            self.dma("pool", winv[sl], w_in[:, :, DC + blk * 512:DC + (blk + 1) * 512], ("winv", sl), writes=[("winv", sl)])
            self.dma("pool", binv[sl][0:1, :], b_in[DC + blk * 512:DC + (blk + 1) * 512].rearrange("(o n) -> o n", o=1),
                     ("binv", sl), writes=[("binv", sl)])
            for ti, (t, tc0, n) in enumerate(tiles):
                b = self.nb()
                for k in range(KD):
                    self.mm(ps[:n, b, :], xT[:, k, tc0:tc0 + n], winv[sl][:, k, :], k == 0, False,
                            self.xr(gi, k, lo, lo + ncols) + [("winv", sl)], [("ps", b)])
                self.mm(ps[:n, b, :], self.ones_row[0:1, 0:n], binv[sl][0:1, :], False, True,
                        [("ones",), ("binv", sl)], [("ps", b)])
                self.af(v_f[:n, ti, blk * 512:(blk + 1) * 512], ps[:n, b, :], AF.Gelu_apprx_tanh, [("ps", b)],
                        [("vf", ti, blk)])
                self.P.add("dve", (lambda e, n=n, ti=ti, blk=blk: e.bn_stats(
                    out=vst[:n, ti, blk * 6:blk * 6 + 6], in_=v_f[:n, ti, blk * 512:(blk + 1) * 512])),
                    [("vf", ti, blk)], [("vst", ti, blk)])
        for ti, (t, tc0, n) in enumerate(tiles):
            self.P.add("dve", (lambda e, n=n, ti=ti: e.bn_aggr(
                out=vst[:n, ti, 36:38], in_=vst[:n, ti, 0:36].rearrange("p (a b) -> p a b", a=6))),
                [("vst", ti, b_) for b_ in range(6)], [("vst", ti, "mv")])
            self.af(vst[:n, ti, 38:39], vst[:n, ti, 37:38], AF.Sqrt, [("vst", ti, "mv")], [("vst", ti, "sd")],
                    bias=float(LN_EPS), scale=1.0)
            self.recip(vst[:n, ti, 39:40], vst[:n, ti, 38:39], [("vst", ti, "sd")], [("vst", ti, "rs")])
        for blk in range(6):
            sl = self.cnt_vg % 2
            self.cnt_vg += 1
            bsl = slice(blk * 512, (blk + 1) * 512)
            self.dma("sp", vgb[sl][:, 0, :], d["b_v_norm_g"][0, bsl].partition_broadcast(128), ("vgb", sl, 0),
                     writes=[("vgb", sl, 0)])
            self.dma("sp", vgb[sl][:, 1, :], d["b_v_norm_b"][0, bsl].partition_broadcast(128), ("vgb", sl, 1),
                     writes=[("vgb", sl, 1)])
            for ti, (t, tc0, n) in enumerate(tiles):
                self.stt("dve", v_f[:n, ti, bsl], v_f[:n, ti, bsl], vst[:n, ti, 36:37], vgb[sl][:n, 0, :],
                         ALU.subtract, ALU.mult, [("vf", ti, blk), ("vst", ti, "mv"), ("vgb", sl, 0)], [("vf", ti, blk)])
                if samp:
                    self.stt("dve", v_f[:n, ti, bsl], v_f[:n, ti, bsl], vst[:n, ti, 39:40], vgb[sl][:n, 1, :],
                             ALU.mult, ALU.add, [("vf", ti, blk), ("vst", ti, "rs"), ("vgb", sl, 1)], [("vf", ti, blk)])
                    self.cp("act", vn[:n, ti, bsl], v_f[:n, ti, bsl], [("vf", ti, blk)], [("vn", ti, blk)])
                else:
                    self.stt("dve", vn[:n, ti, bsl], v_f[:n, ti, bsl], vst[:n, ti, 39:40], vgb[sl][:n, 1, :],
                             ALU.mult, ALU.add, [("vf", ti, blk), ("vst", ti, "rs"), ("vgb", sl, 1)], [("vn", ti, blk)])
        if samp:
            self.dma("sp", d["o_v_s"], v_f[:NSAMP, 0, :], "o_v_s", reads=[("vf", 0, b_) for b_ in range(6)])
        accb = [[self.nb(), self.nb()] for _ in range(nt)]
        used = set(b for pr in accb for b in pr)
        for ch in range(24):
            h = ch // 3
            if ch % 2 == 0:
                su = self.cnt_wu % 2
                self.cnt_wu += 1
                self.dma("pool", winu[su], w_in[:, :, ch * 128:(ch + 2) * 128], ("winu", su), writes=[("winu", su)])
                so = self.cnt_wo % 2
                self.cnt_wo += 1
                self.dma("pool", wout[so], w_out[:, ch:ch + 2, :], ("wout", so), writes=[("wout", so)])
            b = self.nb()
            while b in used:
                b = self.nb()
            for k in range(KD):
                self.mm(ps[:, b, 0:ncols], winu[su][:, k, (ch % 2) * 128:(ch % 2 + 1) * 128], xT[:, k, c0:c0 + ncols],
                        k == 0, k == KD - 1, [("winu", su)] + self.xr(gi, k, lo, lo + ncols), [("ps", b)])
            us = self.cnt_uc % 2
            self.cnt_uc += 1
            self.af(uc[us][:, 0:ncols], ps[:, b, 0:ncols], AF.Gelu_apprx_tanh, [("ps", b), ("binu",)], [("uc", us)],
                    bias=binu[:, ch:ch + 1], scale=1.0)
            b2 = self.nb()
            while b2 in used:
                b2 = self.nb()
            for ti, (t, tc0, n) in enumerate(tiles):
                o = tc0 - c0
                blk = ch // 4
                if samp:
                    self.mm(ps[:, b2, o:o + n], vn[:n, ti, ch * 128:(ch + 1) * 128], wsS[:n, h, :], True, False,
                            [("vn", ti, blk), ("wsS",)], [("ps", b2)])
                    self.mm(ps[:, b2, o:o + n], self.ones_row[0:1, 0:128], bsS[0:1, h, :], False, True,
                            [("ones",), ("bsS",)], [("ps", b2)])
                else:
                    self.mm(ps[:, b2, o:o + n], vn[:n, ti, ch * 128:(ch + 1) * 128], WsT[:, h, :], True, False,
                            [("vn", ti, blk), ("WsT",)], [("ps", b2)])
                    self.mm(ps[:, b2, o:o + n], self.ones_row[0:1, 0:128], bsrow[0:1, h, :], False, True,
                            [("ones",), ("bsrow",)], [("ps", b2)])
            self.tt("dve", huc[us][:, 0:ncols], uc[us][:, 0:ncols], ps[:, b2, 0:ncols], ALU.mult,
                    [("uc", us), ("ps", b2)], [("huc", us)])
            for ti, (t, tc0, n) in enumerate(tiles):
                o = tc0 - c0
                for hf in range(2):
                    self.mm(ps[:n, accb[ti][hf], :], huc[us][:, o:o + n], wout[so][:, ch % 2, hf * 512:(hf + 1) * 512],
                            ch == 0, ch == 23, [("huc", us), ("wout", so)], [("ps", accb[ti][hf])])
        for ti, tile in enumerate(tiles):
            self.ln_tile(tile, accb[ti], coef, gslot)
        self.t_phase((gi, c0, ncols, tiles))


K.cmlp = _cmlp


def _mla(self, li, j):
    cfg = self.cfg
    d = self.dram
    S, NT, TT, NPG = cfg.S, cfg.NT, cfg.TT, cfg.NPG
    ps, xT = self.ps, self.xT
    P = self.P
    P.phase = "mlaA%d" % li
    P.fence()
    self.phase_kind = "mla"
    self.off = self.phase_off
    gslot = self.load_gb(li, 1)
    coef = 1.0 / ALPHA
    SC = float(ATT_SCALE)
    wuv = self.buf([2, HEADS, 128], BF16)
    wo = self.buf([HEADS, D], BF16)
    qn_c = self.buf([3])
    kvn_bc = self.buf([KVR])
    ast = self.buf([2, 8])
    ckvT_s = self.buf([2, NSAMP], BF16)
    krT_s = self.buf([NSAMP], BF16)
    ckv_tm_s = self.buf([KVR], BF16)
    c_off = self.off
    cqT = self.buf([3, TT], BF16)
    ckvT = self.buf([2, TT], BF16)
    krT = self.buf([TT], BF16)
    ckv_tm = self.buf([NT, KVR], BF16)
    mid_off = self.off
    cs_tm = self.buf([NT + 1, 64])
    self.dma("pool", wuv, d["a_w_uv"][j].rearrange("(c p) h v -> p c h v", p=128), "wuv", writes=[("wuv",)])
    self.dma("pool", wo, d["a_w_o"][j].rearrange("(k p) n -> p k n", p=128), "wo", writes=[("wo",)])
    self.dma("sp", kvn_bc, d["a_kv_norm"][j].partition_broadcast(128), "kvn", writes=[("kvn",)])
    self.dma("sp", cs_tm, d["c_rope_tm"], "cs_tm", writes=[("cs_tm",)])
    self.load_colvec(d["a_q_norm"][j], 3, qn_c, "qn", "qn")
    win = self.buf([KD, 704], BF16)
    cqn = [self.buf([QR]) for _ in range(2)]
    ckvf = [self.buf([KVR]) for _ in range(2)]
    krf = [self.buf([128]) for _ in range(2)]
    for kk in range(2):
        self.memset("dve", krf[kk], 0.0, [], [("krf", kk, 0), ("krf", kk, 1)])
    ktmp = [self.buf([128]) for _ in range(2)]
    junk = self.buf([QR])
    self.dma("pool", win, d["a_w_in"][j].rearrange("(k p) n -> p k n", p=128), "win", writes=[("win",)])
    for (t, tc0, n) in self.tiles:
        samp = t == NT
        gi = min(t // 4, self.NG)
        lo = tc0 - gi * 512 if not samp else 0
        s2 = t % 2
        bq, bk = self.nb(), self.nb()
        for k in range(KD):
            self.mm(ps[:n, bq, 0:QR], xT[:, k, tc0:tc0 + n], win[:, k, 0:QR], k == 0, k == KD - 1,
                    self.xr(gi, k, lo, lo + n) + [("win",)], [("ps", bq)])
        for k in range(KD):
            self.mm(ps[:n, bk, 0:320], xT[:, k, tc0:tc0 + n], win[:, k, QR:704], k == 0, k == KD - 1,
                    self.xr(gi, k, lo, lo + n) + [("win",)], [("ps", bk)])
        self.memset("dve", ast[:n, s2, 0:2], 0.0, [], [("ast", s2, "ss")])
        self.af(junk[:n, :], ps[:n, bq, 0:QR], AF.Square, [("ps", bq), ("ast", s2, "ss")], [("junk",), ("ast", s2, "ssq")],
                accum_out=ast[:n, s2, 0:1])
        self.af(junk[:n, 0:KVR], ps[:n, bk, 0:KVR], AF.Square, [("ps", bk), ("ast", s2, "ss")],
                [("junk",), ("ast", s2, "ssk")], accum_out=ast[:n, s2, 1:2])
        self.af(ast[:n, s2, 2:3], ast[:n, s2, 0:1], AF.Sqrt, [("ast", s2, "ssq")], [("ast", s2, "sdq")],
                bias=float(RMS_EPS), scale=1.0 / QR)
        self.af(ast[:n, s2, 3:4], ast[:n, s2, 1:2], AF.Sqrt, [("ast", s2, "ssk")], [("ast", s2, "sdk")],
                bias=float(RMS_EPS), scale=1.0 / KVR)
        self.recip(ast[:n, s2, 4:6], ast[:n, s2, 2:4], [("ast", s2, "sdq"), ("ast", s2, "sdk")], [("ast", s2, "rs")])
        self.af(cqn[s2][:n, :], ps[:n, bq, 0:QR], AF.Identity, [("ps", bq), ("ast", s2, "rs")], [("cqn", s2)],
                scale=ast[:n, s2, 4:5])
        self.stt("dve", ckvf[s2][:n, :], ps[:n, bk, 0:KVR], ast[:n, s2, 5:6], kvn_bc[:n, :], ALU.mult, ALU.mult,
                 [("ps", bk), ("ast", s2, "rs"), ("kvn",)], [("ckvf", s2)])
        if samp:
            self.dma("sp", d["o_ckv_s"][j], ckvf[s2][:n, :], ("o_ckv", s2), reads=[("ckvf", s2)])
        else:
            self.dma("sp", d["o_ckv_p"][j, tc0:tc0 + n, :], ckvf[s2][:n, :], ("o_ckv", s2), reads=[("ckvf", s2)])
        if samp:
            self.cp("act", ckv_tm_s[:n, :], ckvf[s2][:n, :], [("ckvf", s2)], [("ckv_tm", t)])
        else:
            self.cp("act", ckv_tm[:n, t, :], ckvf[s2][:n, :], [("ckvf", s2)], [("ckv_tm", t)])
        sk = getattr(cfg, "skip", ())
        if "all" in sk:
            continue
        x1 = ps[:n, bk, 256:288]
        x2 = ps[:n, bk, 288:320]
        cosv = cs_tm[:n, t, 0:32]
        sinv = cs_tm[:n, t, 32:64]
        kt_ = ktmp[s2]
        for (o_, a_, b_) in ((0, x1, cosv), (32, x2, sinv), (64, x1, sinv), (96, x2, cosv)):
            self.tt("dve", kt_[:n, o_:o_ + 32], a_, b_, ALU.mult, [("ps", bk), ("cs_tm",)], [("ktmp", s2, o_)])
        self.tt("dve", krf[s2][:n, 0:32], kt_[:n, 0:32], kt_[:n, 32:64], ALU.subtract,
                [("ktmp", s2, 0), ("ktmp", s2, 32)], [("krf", s2, 0)])
        self.tt("dve", krf[s2][:n, 32:64], kt_[:n, 64:96], kt_[:n, 96:128], ALU.add,
                [("ktmp", s2, 64), ("ktmp", s2, 96)], [("krf", s2, 1)])
        if samp:
            self.dma("sp", d["o_kr_s"][j], krf[s2][:n, 0:ROPE], ("o_kr", s2), reads=[("krf", s2, 0), ("krf", s2, 1)])
        else:
            self.dma("sp", d["o_kr_p"][j, tc0:tc0 + n, :], krf[s2][:n, 0:ROPE], ("o_kr", s2),
                     reads=[("krf", s2, 0), ("krf", s2, 1)])
        if "tr" in sk:
            continue
        bX, bY = self.nb(), self.nb()
        for k in range(3):
            self.tp(ps[:, bX, k * 128:k * 128 + n], cqn[s2][:n, k * 128:(k + 1) * 128], self.ident[:n, :n],
                    [("cqn", s2), ("ident",)], [("ps", bX)])
        for c in range(2):
            self.tp(ps[:, bY, c * 128:c * 128 + n], ckvf[s2][:n, c * 128:(c + 1) * 128], self.ident[:n, :n],
                    [("ckvf", s2), ("ident",)], [("ps", bY)])
        self.tp(ps[:, bY, 256:256 + n], krf[s2][:n, :], self.ident[:n, :n],
                [("krf", s2, 0), ("krf", s2, 1), ("ident",)], [("ps", bY)])
        for k in range(3):
            if "e1" in sk:
                continue
            self.af(cqT[:, k, tc0:tc0 + n], ps[:, bX, k * 128:k * 128 + n], AF.Identity, [("ps", bX), ("qn",)],
                    [("cqT", t, k)], scale=qn_c[:, k:k + 1])
        kdst = ckvT_s[:, :, 0:n] if samp else ckvT[:, :, tc0:tc0 + n]
        rdst = krT_s[:64, 0:n] if samp else krT[:64, tc0:tc0 + n]
        if "e2" not in sk:
            self.cp("dve", kdst, ps[:, bY, 0:256].rearrange("p (c q) -> p c q", c=2)[:, :, 0:n],
                    [("ps", bY)], [("ckvT", t)])
        if "e3" not in sk:
            self.cp("act", rdst, ps[:64, bY, 256:256 + n], [("ps", bY)], [("krT", t)])
    stop = getattr(cfg, "stop", "")
    if stop == "A":
        for grp in self.groups:
            for tile in grp[3]:
                self.ln_tile(tile, None, 0.0, gslot)
            self.t_phase(grp)
        return
    P.phase = "mlaB%d" % li
    P.fence()
    self.off = mid_off
    wuq = self.buf([3, 1536], BF16)
    wqs = self.buf([3, HEADS, ROPE], BF16)
    wukT = self.buf([HEADS, KVR], BF16)
    csT = self.buf([2, 128])
    qlT = self.buf([HEADS, 2, 128], BF16)
    qrT = self.buf([HEADS, 128], BF16)
    t12 = [self.buf([2, 128])]
    pb_off = self.off
    pb = self.buf([max(NT * 128, 2048)], BF16)
    wuk = self.view(pb_off, [2, HEADS, 128], BF16)
    pb2 = [pb, self.buf([NT * 128], BF16)]
    pT_off = self.off
    pT = self.buf([max(NT * 128, 1024)], BF16)
    qnA = self.view(pT_off, [HEADS, 128], BF16)
    ol = [self.buf([KVR], BF16) for _ in range(2)]
    olT = [self.buf([KVR], BF16) for _ in range(2)]
    oT = [self.buf([HEADS, 128], BF16)]
    ats = self.buf([2, 8])
    self.dma("pool", wuq, d["a_w_uq"][j].rearrange("(k p) n -> p k n", p=128), "wuq", writes=[("wuq",)])
    self.dma("pool", wuk, d["a_w_uk"][j].rearrange("(c p) h d -> p c h d", p=128), "wuk", writes=[("pb", 0)])
    wr = wuq.rearrange("p k (h e) -> p k h e", h=HEADS)
    self.ts("dve", wqs[:, :, :, 0:32], wr[:, :, :, 160:192], -1.0, None, ALU.mult, None, [("wuq",)], [("wqs", 0)])
    self.cp("dve", wqs[:, :, :, 32:64], wr[:, :, :, 128:160], [("wuq",)], [("wqs", 1)])
    for h in range(HEADS):
        b = 7
        pv = ps[:, b, 0:128].bitcast(BF16)
        for cc in range(2):
            self.tp(pv[:, cc * 128:(cc + 1) * 128], wuk[:, cc, h, :], self.identb, [("pb", 0), ("identb",)], [("ps", b)])
        self.cp("dve", wukT[:, h, :], pv, [("ps", b)], [("wukT",)])
    pTps = ps[:, 4:6, :].rearrange("p a b -> p (a b)").bitcast(BF16)
    olTps = ps[:, 7, 0:128].bitcast(BF16)
    sall = ps[:, 0:4, :].rearrange("p a b -> p (a b)")
    for (t, tc0, n) in self.tiles:
        samp = t == NT
        gi = min(t // 4, self.NG)
        self.dma("sp", csT[:64, :, 0:n], d["c_rope_fm"][:, :, tc0:tc0 + n], "csT", writes=[("csT",)])
        os_ = 0
        rcq = [("cqT", t, k) for k in range(3)]
        for hb_ in range(2):
            b = 4 + hb_
            for hh in range(4):
                h = hb_ * 4 + hh
                for k in range(3):
                    self.mm(ps[:, b, hh * 128:hh * 128 + n], wuq[:, k, h * 192:h * 192 + 128], cqT[:, k, tc0:tc0 + n],
                            k == 0, k == 2, [("wuq",)] + rcq, [("ps", b)])
            self.cp("act" if hb_ == 0 else "dve", qnA[:, hb_ * 4:hb_ * 4 + 4, 0:n],
                    ps[:, b, :].rearrange("p (h q) -> p h q", h=4)[:, :, 0:n], [("ps", b)], [("pT", hb_)])
        for bi in range(4):
            for hh in range(2):
                h = bi * 2 + hh
                for cc in range(2):
                    o_ = (hh * 2 + cc) * 128
                    self.mm(ps[:, bi, o_:o_ + n], wukT[:, h, cc * 128:(cc + 1) * 128], qnA[:, h, 0:n], True, True,
                            [("wukT",), ("pT", h // 4)], [("ps", bi)])
            src = ps[:, bi, :].rearrange("p (h c q) -> p h c q", h=2, c=2)[:, :, :, 0:n]
            eng_ = "act" if bi % 2 == 0 else "dve"
            if samp:
                self.cp(eng_, self.qls[:, :, :, bi * 2:bi * 2 + 2].rearrange("p c q h -> p h c q"), src, [("ps", bi)],
                        [("qls", bi * 2), ("qls", bi * 2 + 1)])
            else:
                self.cp(eng_, qlT[:, bi * 2:bi * 2 + 2, :, 0:n], src, [("ps", bi)], [("qlT", bi * 2), ("qlT", bi * 2 + 1)])
        for hb_ in range(2):
            for hh in range(4):
                h = hb_ * 4 + hh
                for k in range(3):
                    self.mm(ps[:64, 4 + hb_, hh * 128:hh * 128 + n], wuq[:, k, h * 192 + 128:h * 192 + 192],
                            cqT[:, k, tc0:tc0 + n], k == 0, k == 2, [("wuq",)] + rcq, [("ps", 4 + hb_)])
                for k in range(3):
                    self.mm(ps[:64, 6 + hb_, hh * 128:hh * 128 + n], wqs[:, k, h, :], cqT[:, k, tc0:tc0 + n],
                            k == 0, k == 2, [("wqs", 0), ("wqs", 1)] + rcq, [("ps", 6 + hb_)])
        for h in range(HEADS):
            hb_, hh = h // 4, h % 4
            self.tt("dve", t12[0][:64, 0, 0:n], ps[:64, 4 + hb_, hh * 128:hh * 128 + n], csT[:64, 0, 0:n], ALU.mult,
                    [("ps", 4 + hb_), ("csT",)], [("t12", 0, 0)])
            self.tt("dve", t12[0][:64, 1, 0:n], ps[:64, 6 + hb_, hh * 128:hh * 128 + n], csT[:64, 1, 0:n], ALU.mult,
                    [("ps", 6 + hb_), ("csT",)], [("t12", 0, 1)])
            if samp:
                self.tt("dve", self.qrs[:64, :, h], t12[0][:64, 0, 0:n], t12[0][:64, 1, 0:n], ALU.add,
                        [("t12", 0, 0), ("t12", 0, 1)], [("qrs", h)])
            else:
                self.tt("dve", qrT[:64, h, 0:n], t12[0][:64, 0, 0:n], t12[0][:64, 1, 0:n], ALU.add,
                        [("t12", 0, 0), ("t12", 0, 1)], [("qrT", h)])
        if samp:
            continue
        if stop == "B0":
            self.ln_tile((t, tc0, n), None, 0.0, gslot)
            self.t_phase((gi, tc0, n, [(t, tc0, n)]))
            continue
        nk = t + 1
        nkeys = nk * 128
        nch = (nkeys + 511) // 512
        rps = [("ps", kc) for kc in range(nch)]

        def stS(h):
            a2 = h % 2
            pbh = pb2[a2]
            for kc in range(nch):
                k0 = kc * 512
                w = min(512, nkeys - k0)
                diag = kc == nch - 1
                rk = [("ckvT", tt_) for tt_ in range(k0 // 128, (k0 + w) // 128)]
                rr = [("krT", tt_) for tt_ in range(k0 // 128, (k0 + w) // 128)]
                self.mm(ps[:, kc, 0:w], qlT[:, h, 0, :], ckvT[:, 0, k0:k0 + w], True, False, [("qlT", h)] + rk, [("ps", kc)])
                self.mm(ps[:, kc, 0:w], qlT[:, h, 1, :], ckvT[:, 1, k0:k0 + w], False, False, [("qlT", h)] + rk, [("ps", kc)])
                self.mm(ps[:, kc, 0:w], qrT[:64, h, :], krT[:64, k0:k0 + w], False, not diag, [("qrT", h)] + rr, [("ps", kc)])
                if diag:
                    self.mm(ps[:, kc, w - 128:w], self.identb, self.cmaskb, False, True, [("identb",), ("cmaskb",)],
                            [("ps", kc)])
            self.red("dve", ats[:, a2, 0:1], sall[:, 0:nkeys], ALU.max, rps, [("ats", a2, "mx")])
            self.ts("dve", ats[:, a2, 1:2], ats[:, a2, 0:1], -SC, None, ALU.mult, None, [("ats", a2, "mx")], [("ats", a2, "nm")])
            self.memset("dve", ats[:, a2, 2:3], 0.0, [], [("ats", a2, "l")])
            self.af(pbh[:, 0:nkeys], sall[:, 0:nkeys], AF.Exp, rps + [("ats", a2, "nm"), ("ats", a2, "l")],
                    [("pb", a2), ("ats", a2, "l")], bias=ats[:, a2, 1:2], scale=SC, accum_out=ats[:, a2, 2:3])
            self.recip(ats[:, a2, 3:4], ats[:, a2, 2:3], [("ats", a2, "l")], [("ats", a2, "ri")])

        def stTVO(h):
            a2 = h % 2
            pbh = pb2[a2]
            for kt in range(nk):
                self.tp(pTps[:, kt * 128:(kt + 1) * 128], pbh[:, kt * 128:(kt + 1) * 128], self.identb,
                        [("pb", a2), ("identb",)], [("ps", 4 + kt // 8)])
            hk = min(nk, 8)
            self.cp("act", pT[:, 0:hk * 128], pTps[:, 0:hk * 128], [("ps", 4)], [("pT", 0)])
            if nk > hk:
                self.cp("dve", pT[:, hk * 128:nkeys], pTps[:, hk * 128:nkeys], [("ps", 5)], [("pT", 1)])
            for kt in range(nk):
                self.mm(ps[:, 6, 0:KVR], pT[:, kt * 128:(kt + 1) * 128], ckv_tm[:, kt, :], kt == 0, kt == nk - 1,
                        [("pT", 0), ("pT", 1), ("ckv_tm", kt)], [("ps", 6, "pv")])
            self.af(ol[a2], ps[:, 6, 0:KVR], AF.Identity, [("ps", 6, "pv"), ("ats", a2, "ri")], [("ol", a2)],
                    scale=ats[:, a2, 3:4])
            for cc in range(2):
                self.tp(olTps[:, cc * 128:(cc + 1) * 128], ol[a2][:, cc * 128:(cc + 1) * 128], self.identb,
                        [("ol", a2), ("identb",)], [("ps", 7)])
            self.cp("dve", olT[a2], olTps, [("ps", 7)], [("olT", a2)])
            for cc in range(2):
                self.mm(ps[:, 6, 256:384], wuv[:, cc, h, :], olT[a2][:, cc * 128:(cc + 1) * 128], cc == 0, cc == 1,
                        [("wuv",), ("olT", a2)], [("ps", 6, "o")])
            self.cp("act", oT[os_][:, h, :], ps[:, 6, 256:384], [("ps", 6, "o")], [("oT", os_, h)])

        stS(0)
        for h in range(HEADS):
            if h + 1 < HEADS:
                stS(h + 1)
            stTVO(h)
        for hf in range(2):
            for h in range(HEADS):
                self.mm(ps[:, hf, :], oT[os_][:, h, :], wo[:, h, hf * 512:(hf + 1) * 512], h == 0, h == HEADS - 1,
                        [("oT", os_, h), ("wo",)], [("ps", hf)])
        self.ln_tile((t, tc0, n), (0, 1), coef, gslot)
        self.t_phase((gi, tc0, n, [(t, tc0, n)]))
    if getattr(cfg, "skip_c", False):
        self.ln_tile(self.tiles[NT], None, 0.0, gslot)
        self.t_phase(self.groups[self.NG])
        return
    P.phase = "mlaC%d" % li
    P.fence()
    self.off = c_off
    NCH = NPG // 8
    NC = NCH + 1
    kvbk = [self.buf([8, KVR], BF16) for _ in range(6)]
    kvbr = [self.buf([8, ROPE], BF16) for _ in range(6)]
    idxt = self.buf([NSAMP, NCH])
    idx2 = self.buf([NSAMP, NCH], I32)
    self.ts("dve", idxt, self.idxf, float(j * cfg.NPOOL * 16), None, ALU.add, None, [("idxf",)], [("idxt",)])
    self.cp("dve", idx2, idxt, [("idxt",)], [("idx2",)])
    ckv_blk = d["cache_kv_latent"].rearrange("l n (g r) c -> (l n g) (r c)", r=8)
    kr_blk = d["cache_k_rope"].rearrange("l n (g r) c -> (l n g) (r c)", r=8)
    KT = [self.buf([2, 1024], BF16) for _ in range(2)]
    KrT = [self.buf([1024], BF16) for _ in range(2)]
    p_s = [self.buf([1024], BF16) for _ in range(2)]
    pT_s = [self.buf([64], BF16) for _ in range(2)]
    ssb = self.buf([NSAMP])
    mst = [self.buf([16]) for _ in range(2)]
    nmst = [self.buf([16]) for _ in range(2)]
    lst = [self.buf([16]) for _ in range(2)]
    wst = [self.buf([16]) for _ in range(2)]
    cst = [self.buf([8]) for _ in range(2)]
    ost = [self.buf([NC, KVR]) for _ in range(2)]
    acc = [self.buf([KVR]) for _ in range(2)]
    ol_s = [self.buf([KVR], BF16) for _ in range(2)]
    olT_s = self.buf([2, HEADS, NSAMP], BF16)
    oT_s = self.buf([HEADS, NSAMP], BF16)
    ckv_d = d["cache_kv_latent"]
    kr_d = d["cache_k_rope"]
    ckv_fl = ckv_d.rearrange("l n p c -> (l n) p c")
    kr_fl = kr_d.rearrange("l n p c -> (l n) p c")
    dsem = lambda key: self.dsem[key]
    bkA, bkB, bkC = 0, 1, 2
    kA = ps[:, bkA, :].bitcast(BF16)
    kB = ps[:, bkB, :].bitcast(BF16)
    kC = ps[:, bkC, :].bitcast(BF16)
    pTs_ps = ps[:, 3, 0:32].bitcast(BF16)
    olTs_ps = ps[:, 3, 64:72].bitcast(BF16)
    s_ps = ps[:, 6:8, :].rearrange("p a b -> p (a b)")
    HS = HEADS
    smask3 = self.smask.rearrange("p (b k) -> p b k", b=NSAMP)
    items = []
    for bsm in range(NSAMP):
        for ci in range(NCH):
            items.append((bsm, ci))
        items.append((bsm, NCH))
    NS = len(kvbk)

    def stageG(it):
        bsm, ci = items[it]
        if ci == NCH:
            return
        slot = it % NS
        self.gather(kvbk[slot].rearrange("p s c -> p (s c)"), ckv_blk, idx2[:, bsm, ci:ci + 1], ("kvk", slot))
        self.gather(kvbr[slot].rearrange("p s c -> p (s c)"), kr_blk, idx2[:, bsm, ci:ci + 1], ("kvr", slot))

    def stageA(it):
        bsm, ci = items[it]
        sb = bsm % 2
        if ci == 0:
            self.memset("dve", lst[sb][:HS, :], 0.0, [], [("lst", sb, c) for c in range(NC)])
        if ci == NCH:
            return
        slot = it % NS
        ks = it % 2
        for pg in range(8):
            self.tp(kA[:, pg * 128:(pg + 1) * 128], kvbk[slot][:, pg, 0:128], self.identb, [("kvk", slot), ("identb",)],
                    [("ps", bkA)])
            self.tp(kB[:, pg * 128:(pg + 1) * 128], kvbk[slot][:, pg, 128:256], self.identb, [("kvk", slot), ("identb",)],
                    [("ps", bkB)])
            self.tp(kC[:64, pg * 128:(pg + 1) * 128], kvbr[slot][:, pg, :], self.identb,
                    [("kvr", slot), ("identb",)], [("ps", bkC)])
        self.cp("act", KT[ks][:, 0, :], kA, [("ps", bkA)], [("KT", ks, 0)])
        self.cp("dve", KT[ks][:, 1, :], kB, [("ps", bkB)], [("KT", ks, 1)])
        self.cp("act", KrT[ks][:64, :], kC[:64, :], [("ps", bkC)], [("KrT", ks)])

    def stageB(it):
        bsm, ci = items[it]
        sb = bsm % 2
        ks = it % 2
        rq = [("qls", h_) for h_ in range(HS)]
        rr = [("qrs", h_) for h_ in range(HS)]
        if ci < NCH:
            for hh in range(2):
                bs_ = 6 + hh
                ksl = slice(hh * 512, (hh + 1) * 512)
                self.mm(ps[:HS, bs_, :], self.qls[:, 0, bsm, :], KT[ks][:, 0, ksl], True, False,
                        rq + [("KT", ks, 0)], [("ps", bs_)])
                self.mm(ps[:HS, bs_, :], self.qls[:, 1, bsm, :], KT[ks][:, 1, ksl], False, False,
                        [("KT", ks, 1)], [("ps", bs_)])
                self.mm(ps[:HS, bs_, :], self.qrs[:64, bsm, :], KrT[ks][:64, ksl], False, True,
                        rr + [("KrT", ks)], [("ps", bs_)])
            src, nkk, rsrc = s_ps[:HS, :], 1024, [("ps", 6), ("ps", 7)]
        else:
            rself = [("ckvT", NT), ("krT", NT)]
            self.mm(ps[:HS, 5, 0:NSAMP], self.qls[:, 0, bsm, :], ckvT_s[:, 0, :], True, False, rq + rself, [("ps", 5)])
            self.mm(ps[:HS, 5, 0:NSAMP], self.qls[:, 1, bsm, :], ckvT_s[:, 1, :], False, False, rself, [("ps", 5)])
            self.mm(ps[:HS, 5, 0:NSAMP], self.qrs[:64, bsm, :], krT_s[:64, :], False, True, rr + rself, [("ps", 5)])
            self.tt("dve", ssb[:HS, :], ps[:HS, 5, 0:NSAMP], smask3[:HS, bsm, :], ALU.add, [("ps", 5), ("smask",)],
                    [("ssb",)])
            src, nkk, rsrc = ssb[:HS, :], NSAMP, [("ssb",)]
        self.red("dve", mst[sb][:HS, ci:ci + 1], src, ALU.max, rsrc, [("mst", sb, ci)])
        self.ts("dve", nmst[sb][:HS, ci:ci + 1], mst[sb][:HS, ci:ci + 1], -SC, None, ALU.mult, None,
                [("mst", sb, ci)], [("nmst", sb, ci)])
        self.af(p_s[ks][:HS, 0:nkk], src, AF.Exp, rsrc + [("nmst", sb, ci), ("lst", sb, ci)],
                [("p_s", ks), ("lst", sb, ci)], bias=nmst[sb][:HS, ci:ci + 1], scale=SC,
                accum_out=lst[sb][:HS, ci:ci + 1])

    def stageC(it):
        bsm, ci = items[it]
        sb = bsm % 2
        ks = it % 2
        slot = it % NS
        if ci < NCH:
            for pg in range(8):
                self.tp(pTs_ps[:, pg * 8:(pg + 1) * 8], p_s[ks][:HS, pg * 128:(pg + 1) * 128], self.identb[:HS, :HS],
                        [("p_s", ks), ("identb",)], [("ps", 3, "pT")])
            self.cp("dve", pT_s[ks], pTs_ps, [("ps", 3, "pT")], [("pT_s", ks)])
            for pg in range(8):
                self.mm(ps[:HS, 4, 0:KVR], pT_s[ks][:, pg * 8:(pg + 1) * 8], kvbk[slot][:, pg, :], pg == 0, pg == 7,
                        [("pT_s", ks), ("kvk", slot)], [("ps", 4)])
        else:
            self.tp(pTs_ps[:NSAMP, 0:8], p_s[ks][:HS, 0:NSAMP], self.identb[:HS, :HS], [("p_s", ks), ("identb",)],
                    [("ps", 3, "pT")])
            self.cp("dve", pT_s[ks][:NSAMP, 0:8], pTs_ps[:NSAMP, 0:8], [("ps", 3, "pT")], [("pT_s", ks)])
            self.mm(ps[:HS, 4, 0:KVR], pT_s[ks][:NSAMP, 0:8], ckv_tm_s[:NSAMP, :], True, True,
                    [("pT_s", ks), ("ckv_tm", NT)], [("ps", 4)])
        self.cp("act", ost[sb][:HS, ci, :], ps[:HS, 4, 0:KVR], [("ps", 4)], [("ost", sb, ci)])
        if ci < NCH:
            return
        allm = [("mst", sb, c) for c in range(NC)]
        self.red("dve", cst[sb][:HS, 0:1], mst[sb][:HS, 0:NC], ALU.max, allm, [("cst", sb, 0)])
        self.ts("dve", cst[sb][:HS, 1:2], cst[sb][:HS, 0:1], -SC, None, ALU.mult, None, [("cst", sb, 0)], [("cst", sb, 1)])
        self.af(wst[sb][:HS, 0:NC], mst[sb][:HS, 0:NC], AF.Exp, allm + [("cst", sb, 1)], [("wst", sb)],
                bias=cst[sb][:HS, 1:2], scale=SC)
        self.tt("dve", nmst[sb][:HS, 0:NC], wst[sb][:HS, 0:NC], lst[sb][:HS, 0:NC], ALU.mult,
                [("wst", sb)] + [("lst", sb, c) for c in range(NC)], [("nmst", sb, c) for c in range(NC)])
        self.red("dve", cst[sb][:HS, 2:3], nmst[sb][:HS, 0:NC], ALU.add, [("nmst", sb, c) for c in range(NC)],
                 [("cst", sb, 2)])
        self.recip(cst[sb][:HS, 3:4], cst[sb][:HS, 2:3], [("cst", sb, 2)], [("cst", sb, 3)])
        self.ts("dve", acc[sb][:HS, :], ost[sb][:HS, 0, :], wst[sb][:HS, 0:1], None, ALU.mult, None,
                [("ost", sb, 0), ("wst", sb)], [("acc", sb)])
        for c in range(1, NC):
            self.stt("dve", acc[sb][:HS, :], ost[sb][:HS, c, :], wst[sb][:HS, c:c + 1], acc[sb][:HS, :], ALU.mult, ALU.add,
                     [("ost", sb, c), ("wst", sb), ("acc", sb)], [("acc", sb)])
        self.af(ol_s[sb][:HS, :], acc[sb][:HS, :], AF.Identity, [("acc", sb), ("cst", sb, 3)], [("ol_s", sb)],
                scale=cst[sb][:HS, 3:4])
        for cc in range(2):
            self.tp(olTs_ps[:, cc * 8:(cc + 1) * 8], ol_s[sb][:HS, cc * 128:(cc + 1) * 128], self.identb[:HS, :HS],
                    [("ol_s", sb), ("identb",)], [("ps", 3, "olT")])
        self.cp("dve", olT_s[:, :, :, bsm], olTs_ps.rearrange("p (c h) -> p c h", c=2), [("ps", 3, "olT")],
                [("olT_s", bsm)])

    NI = len(items)
    PF = 3
    for it in range(min(PF, NI)):
        stageG(it)
    for step in range(NI + 2):
        if step + PF < NI:
            stageG(step + PF)
        if step < NI:
            stageA(step)
        if 0 <= step - 1 < NI:
            stageB(step - 1)
        if 0 <= step - 2 < NI:
            stageC(step - 2)
    allo = [("olT_s", b_) for b_ in range(NSAMP)]
    for h in range(HEADS):
        b = self.nb()
        for cc in range(2):
            self.mm(ps[:, b, 0:NSAMP], wuv[:, cc, h, :], olT_s[:, cc, h, :], cc == 0, cc == 1, [("wuv",)] + allo, [("ps", b)])
        self.cp("act", oT_s[:, h, :], ps[:, b, 0:NSAMP], [("ps", b)], [("oT_s", h)])
    b0, b1 = self.nb(), self.nb()
    for hf, b in ((0, b0), (1, b1)):
        for h in range(HEADS):
            self.mm(ps[:NSAMP, b, :], oT_s[:, h, :], wo[:, h, hf * 512:(hf + 1) * 512], h == 0, h == HEADS - 1,
                    [("oT_s", h), ("wo",)], [("ps", b)])
    st_ = self.tiles[NT]
    self.ln_tile(st_, (b0, b1), coef, gslot)
    self.t_phase(self.groups[self.NG])


def _gather(self, out, src, idx, key):
    self.P.add("pool", lambda e: e.indirect_dma_start(out=out, out_offset=None, in_=src,
                                                      in_offset=bass.IndirectOffsetOnAxis(ap=idx, axis=0)),
               [("idx2",)], [key], dkey=key)


K.gather = _gather
K.mla = _mla
```

```python
import numpy as np
import concourse.bass as bass
import concourse.mybir as mybir
from concourse.bass_utils import run_bass_kernel_spmd

F32 = mybir.dt.float32
BF16 = mybir.dt.bfloat16
I32 = mybir.dt.int32
AF = mybir.ActivationFunctionType
ALU = mybir.AluOpType
AX = mybir.AxisListType

D = 1024
KD = 8
DEPTH = 4
ALPHA = (2 * DEPTH) ** 0.25
LN_EPS = 1e-5
RMS_EPS = 1e-6
DFF = 2688
NFF = 21
HEADS = 8
QR = 384
KVR = 256
ROPE = 64
DC = 3072
NSAMP = 16
PAGE = 128
ATT_SCALE = (128 + 64) ** -0.5
NEG = -30000.0
SBUF_BYTES = 212800

ENGS = ("pe", "act", "dve", "pool", "sp")


class Op:
    __slots__ = ("eng", "fn", "deps", "sig", "idx", "pos", "dkey", "dcnt", "waits", "gid", "ph")

    def __init__(self, eng, fn):
        self.eng = eng
        self.fn = fn
        self.deps = ()
        self.sig = False
        self.idx = 0
        self.pos = 0
        self.dkey = None
        self.dcnt = 0
        self.waits = ()


class Prog:
    def __init__(self):
        self.ops = {e: [] for e in ENGS}
        self.lastw = {}
        self.readers = {}
        self.dma_tot = {}
        self.dma_eng = {}
        self.n = 0
        self.pending = {}
        self.last_dma = {}
        self.bank_rd = {}
        self.phase = "setup"

    def fence(self):
        deps = [self.ops[e][-1] for e in ENGS if self.ops[e]]
        deps += list(self.last_dma.values())
        for e in ENGS:
            self.pending[e] = list(deps)

    def add(self, eng, fn, reads=(), writes=(), dkey=None, dinc=16):
        op = Op(eng, fn)
        op.ph = self.phase
        op.gid = self.n
        self.n += 1
        deps = {}
        for r in reads:
            w = self.lastw.get(r)
            if w is not None:
                deps[id(w)] = w
        for wr in writes:
            w = self.lastw.get(wr)
            if w is not None:
                deps[id(w)] = w
            rd = self.readers.get(wr)
            if rd:
                for o in rd.values():
                    deps[id(o)] = o
        if eng in ("act", "dve"):
            other = "dve" if eng == "act" else "act"
            for r in reads:
                if r[0] == "ps":
                    o = self.bank_rd.get((r[1], other))
                    if o is not None:
                        deps[id(o)] = o
                    self.bank_rd[(r[1], eng)] = op
        if eng == "pe":
            for wr in writes:
                if wr[0] == "ps":
                    for other in ("act", "dve"):
                        o = self.bank_rd.get((wr[1], other))
                        if o is not None:
                            deps[id(o)] = o
        pend = self.pending.pop(eng, None)
        if pend:
            for o in pend:
                deps[id(o)] = o
        deps.pop(id(op), None)
        op.deps = tuple(deps.values())
        op.pos = len(self.ops[eng])
        self.ops[eng].append(op)
        if dkey is not None:
            assert self.dma_eng.setdefault(dkey, eng) == eng
            self.dma_tot[dkey] = self.dma_tot.get(dkey, 0) + dinc
            op.dkey = dkey
            op.dcnt = self.dma_tot[dkey]
            self.last_dma[dkey] = op
        for r in reads:
            d = self.readers.setdefault(r, {})
            d[eng if dkey is None else ("dma", id(op))] = op
        for wr in writes:
            self.lastw[wr] = op
            self.readers[wr] = {}
        return op

    def resolve(self):
        for e in ENGS:
            for op in self.ops[e]:
                for d in op.deps:
                    if d.dkey is not None:
                        continue
                    if d.eng == op.eng:
                        if op.eng == "pe" or op.dkey is not None:
                            if op.eng == "pe":
                                continue
                        if op.pos - d.pos > 3 and op.dkey is None:
                            continue
                    d.sig = True
        for e in ENGS:
            c = 0
            for op in self.ops[e]:
                if op.sig:
                    c += 1
                    op.idx = c
        for e in ENGS:
            known = {}
            for op in self.ops[e]:
                need = {}
                for d in op.deps:
                    if d.dkey is not None:
                        k = ("d", d.dkey)
                        v = d.dcnt
                    else:
                        if not d.sig:
                            continue
                        if d.eng == op.eng:
                            if op.eng == "pe":
                                continue
                            if op.pos - d.pos > 3 and op.dkey is None:
                                continue
                        k = ("e", d.eng)
                        v = d.idx
                    if v > need.get(k, 0):
                        need[k] = v
                w = []
                for k, v in need.items():
                    if v > known.get(k, 0):
                        known[k] = v
                        w.append((k, v))
                op.waits = tuple(w)


def _np_bf16():
    import ml_dtypes
    return ml_dtypes.bfloat16


class Cfg:
    def __init__(self, S=2048, NPG=64, NPOOL=10240, layers=(0, 1, 2, 3), mixers=True):
        self.S = S
        self.NT = S // 128
        self.NPG = NPG
        self.NPOOL = NPOOL
        self.TT = S + NSAMP
        self.layers = layers
        self.mixers = mixers


def rope_tables(cfg):
    half = ROPE // 2
    inv = (np.float32(10000.0) ** (-np.arange(half, dtype=np.float32) / np.float32(half))).astype(np.float32)
    pos = np.concatenate([np.arange(cfg.S), np.full(NSAMP, cfg.NPG * PAGE)]).astype(np.float32)
    ang = (pos[:, None] * inv[None, :]).astype(np.float32)
    cos = np.cos(ang).astype(np.float32)
    sin = np.sin(ang).astype(np.float32)
    tm = np.zeros((128, cfg.NT + 1, 64), np.float32)
    for t in range(cfg.NT):
        tm[:, t, :32] = cos[t * 128:(t + 1) * 128]
        tm[:, t, 32:] = sin[t * 128:(t + 1) * 128]
    tm[:NSAMP, cfg.NT, :32] = cos[cfg.S:]
    tm[:NSAMP, cfg.NT, 32:] = sin[cfg.S:]
    fm = np.zeros((64, 2, cfg.TT), np.float32)
    fm[:32, 0] = cos.T
    fm[32:, 0] = cos.T
    fm[:32, 1] = sin.T
    fm[32:, 1] = sin.T
    return tm, fm


def const_inputs(cfg):
    tm, fm = rope_tables(cfg)
    ident = np.eye(128, dtype=np.float32)
    cm = np.where(np.arange(128)[None, :] <= np.arange(128)[:, None], 0.0, NEG).astype(np.float32)
    tril = np.tril(np.ones((128, 128), np.float32))
    sm = np.full((HEADS, NSAMP, NSAMP), NEG, np.float32)
    for b in range(NSAMP):
        sm[:, b, b] = 0.0
    pc = np.ones((128, 8, 15), np.float32)
    for g, w in enumerate((2, 4, 8, 16)):
        for t in range(15):
            pc[:, 2 * g:2 * g + 2, t] = w / min(w, t + 1)
    return {"c_ident": ident, "c_cmask": cm, "c_tril": tril, "c_smask": sm.reshape(HEADS, NSAMP * NSAMP),
            "c_rope_tm": tm, "c_rope_fm": fm, "c_poolc": pc.reshape(128, 120),
            "c_pmod": (np.arange(128) % 16).astype(np.float32).reshape(128, 1)}


class K:
    def __init__(self, cfg):
        self.cfg = cfg
        self.nc = bass.Bass("TRN2", target_bir_lowering=False)
        self.P = Prog()
        self.dram = {}
        self.off = 0
        self.sems = {}

    def din(self, name, shape, dt=F32):
        self.dram[name] = self.nc.dram_tensor(name, list(shape), dt, kind="ExternalInput").ap()
        return self.dram[name]

    def dout(self, name, shape, dt=F32):
        self.dram[name] = self.nc.dram_tensor(name, list(shape), dt, kind="ExternalOutput").ap()
        return self.dram[name]

    def alloc(self, nbytes):
        o = self.off
        self.off += (nbytes + 63) // 64 * 64
        assert self.off <= self.arena_bytes, (self.off, self.arena_bytes, getattr(self, 'phase_kind', '?'), self.phase_off)
        return o

    def view(self, off, shape, dt=F32, parts=128):
        n = int(np.prod(shape))
        esz = 4 if dt in (F32, I32) else 2
        nb = n * esz
        assert off % 4 == 0 and nb % 4 == 0
        ap = self.arena[0:parts, off // 4:(off + nb) // 4]
        if dt != F32:
            ap = ap.bitcast(dt)
        if len(shape) == 2:
            ap = ap.rearrange("p (a b) -> p a b", a=shape[0])
        elif len(shape) == 3:
            ap = ap.rearrange("p (a b c) -> p a b c", a=shape[0], b=shape[1])
        elif len(shape) == 4:
            ap = ap.rearrange("p (a b c d) -> p a b c d", a=shape[0], b=shape[1], c=shape[2])
        return ap

    def buf(self, shape, dt=F32):
        n = int(np.prod(shape))
        esz = 4 if dt in (F32, I32) else 2
        off = self.alloc(n * esz)
        return self.view(off, shape, dt)

    def pe(self, fn, reads=(), writes=()):
        return self.P.add("pe", fn, reads, writes)

    def act(self, fn, reads=(), writes=()):
        return self.P.add("act", fn, reads, writes)

    def dve(self, fn, reads=(), writes=()):
        return self.P.add("dve", fn, reads, writes)

    def dma(self, q, out, in_, key, reads=(), writes=(), **kw):
        return self.P.add(q, lambda e: e.dma_start(out=out, in_=in_, **kw), reads, writes, dkey=key)

    def mm(self, out, lhsT, rhs, start, stop, reads=(), writes=()):
        return self.pe(lambda e: e.matmul(out, lhsT=lhsT, rhs=rhs, start=start, stop=stop), reads, writes)

    def tp(self, out, in_, ident, reads=(), writes=()):
        return self.pe(lambda e: e.transpose(out=out, in_=in_, identity=ident), reads, writes)

    def emit(self):
        P = self.P
        P.resolve()
        nc = self.nc
        esem = self.esem
        dsem = self.dsem
        engobj = {"pe": "tensor", "act": "scalar", "dve": "vector", "pool": "gpsimd", "sp": "sync"}

        def run(ename):
            def body(eng):
                for op in P.ops[ename]:
                    for (k, v) in op.waits:
                        if k[0] == "d":
                            eng.wait_ge(dsem[k[1]], v)
                        else:
                            eng.wait_ge(esem[k[1]], v)
                    ins = op.fn(eng)
                    if op.dkey is not None:
                        ins.then_inc(dsem[op.dkey], 16)
                    elif op.sig:
                        ins.then_inc(esem[ename], 1)
                if ename == "sp":
                    for key, tot in P.dma_tot.items():
                        eng.wait_ge(dsem[key], tot)
            return body

        with nc.Block() as block:
            for ename in ENGS:
                if P.ops[ename] or ename == "sp":
                    getattr(block, engobj[ename])(run(ename))


def _needs(op, d):
    if d.dkey is not None:
        return True
    if d.eng != op.eng:
        return True
    if op.dkey is not None:
        return True
    if op.eng == "pe":
        return False
    return (op.pos - d.pos) <= 3


def _resolve(self):
    for e in ENGS:
        for op in self.ops[e]:
            for d in op.deps:
                if d.dkey is None and _needs(op, d):
                    d.sig = True
    for e in ENGS:
        c = 0
        for op in self.ops[e]:
            if op.sig:
                c += 1
                op.idx = c
    for e in ENGS:
        known = {}
        for op in self.ops[e]:
            need = {}
            for d in op.deps:
                if not _needs(op, d):
                    continue
                if d.dkey is not None:
                    k = ("d", d.dkey)
                    v = d.dcnt
                else:
                    k = ("e", d.eng)
                    v = d.idx
                if v > need.get(k, 0):
                    need[k] = v
            w = []
            for k, v in need.items():
                if v > known.get(k, 0):
                    known[k] = v
                    w.append((k, v))
            op.waits = tuple(w)


Prog.resolve = _resolve


def _stt(k, eng, out, in0, scalar, in1, op0, op1, reads, writes):
    return k.P.add(eng, lambda e: e.scalar_tensor_tensor(out=out, in0=in0, scalar=scalar, in1=in1, op0=op0, op1=op1),
                   reads, writes)


def _tt(k, eng, out, in0, in1, op, reads, writes):
    return k.P.add(eng, lambda e: e.tensor_tensor(out=out, in0=in0, in1=in1, op=op), reads, writes)


def _ts(k, eng, out, in0, s1, s2, op0, op1, reads, writes):
    if s2 is None:
        return k.P.add(eng, lambda e: e.tensor_scalar(out=out, in0=in0, scalar1=s1, scalar2=None, op0=op0), reads, writes)
    return k.P.add(eng, lambda e: e.tensor_scalar(out=out, in0=in0, scalar1=s1, scalar2=s2, op0=op0, op1=op1),
                   reads, writes)


def _cp(k, eng, out, in_, reads, writes):
    if eng == "act":
        return k.P.add(eng, lambda e: e.activation(out=out, in_=in_, func=AF.Copy), reads, writes)
    return k.P.add(eng, lambda e: e.tensor_copy(out=out, in_=in_), reads, writes)


def _af(k, out, in_, func, reads, writes, **kw):
    return k.P.add("act", lambda e: e.activation(out=out, in_=in_, func=func, **kw), reads, writes)


def _red(k, eng, out, in_, op, reads, writes):
    return k.P.add(eng, lambda e: e.tensor_reduce(out=out, in_=in_, axis=AX.X, op=op), reads, writes)


def _recip(k, out, in_, reads, writes):
    return k.P.add("dve", lambda e: e.reciprocal(out=out, in_=in_), reads, writes)


def _memset(k, eng, ap, val, reads, writes):
    return k.P.add(eng, lambda e: e.memset(ap, val), reads, writes)


K.stt = _stt
K.tt = _tt
K.ts = _ts
K.cp = _cp
K.af = _af
K.red = _red
K.recip = _recip
K.memset = _memset


def _xr(self, gi, k, lo=0, hi=512):
    if gi == self.NG:
        return [("xT", gi, 0, k)]
    return [("xT", gi, hp, k) for hp in range(2) if lo < (hp + 1) * 256 and hi > hp * 256]


def _nb(self):
    b = self.bank
    self.bank = (self.bank + 1) % 8
    return b


def _setup(self):
    cfg = self.cfg
    S, NT, TT = cfg.S, cfg.NT, cfg.TT
    self.bank = 0
    self.tiles = [(t, t * 128, 128) for t in range(NT)] + [(NT, S, NSAMP)]
    ng = NT // 4
    self.groups = [(g, g * 512, 512, [self.tiles[4 * g + i] for i in range(4)]) for g in range(ng)]
    self.groups.append((ng, S, NSAMP, [self.tiles[NT]]))
    self.NG = ng
    self.passes = [[self.groups[g]] for g in range(ng)]
    self.passes[-1].append(self.groups[ng])
    self.x = self.buf([NT + 1, D])
    self.xT = self.buf([KD, TT], BF16)
    self.gb = [self.buf([2, D])]
    self.ident = self.buf([128])
    self.identb = self.buf([128], BF16)
    self.st = self.buf([4, 32])
    self.ones_row = self.buf([128], BF16)
    self.gbcnt = 0
    self.qls = self.buf([2, NSAMP, HEADS], BF16)
    self.qrs = self.buf([NSAMP, HEADS], BF16)
    nch = max(cfg.NPG // 8, 1)
    self.idxi = self.buf([NSAMP, nch], I32)
    self.idxf = self.buf([NSAMP, nch])
    self.pmod = self.buf([1])
    self.cmaskb = self.buf([128], BF16)
    self.cmaskf = self.buf([128])
    self.smask = self.buf([NSAMP * NSAMP])
    self.phase_off = self.off
    d = self.dram
    self.dma("sp", self.ident, d["c_ident"], "ident", writes=[("ident",)])
    self.cp("dve", self.identb, self.ident, [("ident",)], [("identb",)])
    self.memset("dve", self.ones_row, 1.0, [], [("ones",)])
    pt3 = d["page_table"].rearrange("b (c k) -> b c k", k=8)
    for k8 in range(8):
        self.dma("sp", self.idxi[16 * k8:16 * k8 + 16, :, :], pt3[:, :, k8].partition_broadcast(16), ("pt", k8),
                 writes=[("idxi", k8)], allow_slow_non_contiguous=True)
    self.dma("sp", self.pmod, d["c_pmod"], "pmod", writes=[("pmod",)])
    self.cp("dve", self.idxf, self.idxi, [("idxi", k8) for k8 in range(8)], [("idxf",)])
    self.ts("dve", self.idxf, self.idxf, 16.0, self.pmod[:, 0:1], ALU.mult, ALU.add, [("idxf",), ("pmod",)], [("idxf",)])
    self.dma("sp", self.cmaskf, d["c_cmask"], "cmask", writes=[("cmaskf",)])
    self.cp("dve", self.cmaskb, self.cmaskf, [("cmaskf",)], [("cmaskb",)])
    self.dma("sp", self.smask[:HEADS, :], d["c_smask"], "smask", writes=[("smask",)])
    xp = d["x_prompt"].rearrange("(t p) d -> p t d", p=128)
    for g in range(ng):
        self.dma("sp", self.x[:, 4 * g:4 * g + 4, :], xp[:, 4 * g:4 * g + 4, :], ("xin", g),
                 writes=[("x", t, h) for t in range(4 * g, 4 * g + 4) for h in range(2)])
    self.dma("sp", self.x[:NSAMP, NT, :], d["x_sample"], ("xin", ng), writes=[("x", NT, 0), ("x", NT, 1)])
    for grp in self.groups:
        self.t_phase(grp)


def _load_gb(self, li, w):
    slot = 0
    d = self.dram
    self.dma("sp", self.gb[slot][:, 0, :], d["ln_g"][li, w, :].partition_broadcast(128), ("gb", slot, 0),
             writes=[("gb", slot, 0)])
    self.dma("sp", self.gb[slot][:, 1, :], d["ln_b"][li, w, :].partition_broadcast(128), ("gb", slot, 1),
             writes=[("gb", slot, 1)])
    return slot


def _ln_tile(self, tile, banks, coef, gslot):
    t, c0, n = tile
    x, ps = self.x, self.ps
    sl = t % 4
    st = self.st
    rx = [("x", t, 0), ("x", t, 1)]
    rs = ("st", sl)
    if banks is not None:
        for hf in range(2):
            self.stt("dve", x[:n, t, hf * 512:(hf + 1) * 512], ps[:n, banks[hf], :], float(coef),
                     x[:n, t, hf * 512:(hf + 1) * 512], ALU.mult, ALU.add,
                     [("ps", banks[hf]), ("x", t, hf)], [("x", t, hf)])
    for hf in range(2):
        self.P.add("dve", (lambda e, hf=hf: e.bn_stats(out=st[:n, sl, hf * 6:hf * 6 + 6],
                                                       in_=x[:n, t, hf * 512:(hf + 1) * 512])),
                   [("x", t, hf)], [("st", sl, hf)])
    self.P.add("dve", lambda e: e.bn_aggr(out=st[:n, sl, 12:14],
                                          in_=st[:n, sl, 0:12].rearrange("p (a b) -> p a b", a=2)),
               [("st", sl, 0), ("st", sl, 1)], [("st", sl, "mv")])
    self.af(st[:n, sl, 14:15], st[:n, sl, 13:14], AF.Sqrt, [("st", sl, "mv")], [("st", sl, "sd")],
            bias=float(LN_EPS / ALPHA ** 2), scale=1.0)
    self.recip(st[:n, sl, 15:16], st[:n, sl, 14:15], [("st", sl, "sd")], [("st", sl, "rs")])
    g = self.gb[gslot]
    for hf in range(2):
        sli = slice(hf * 512, (hf + 1) * 512)
        self.stt("dve", x[:n, t, sli], x[:n, t, sli], st[:n, sl, 12:13], g[:n, 0, sli], ALU.subtract, ALU.mult,
                 [("x", t, hf), ("st", sl, "mv"), ("gb", gslot, 0)], [("x", t, hf)])
    for hf in range(2):
        sli = slice(hf * 512, (hf + 1) * 512)
        self.stt("dve", x[:n, t, sli], x[:n, t, sli], st[:n, sl, 15:16], g[:n, 1, sli], ALU.mult, ALU.add,
                 [("x", t, hf), ("st", sl, "rs"), ("gb", gslot, 1)], [("x", t, hf)])


def _t_phase(self, grp):
    gi, c0, ncols, tiles = grp
    x, ps, xT = self.x, self.ps, self.xT
    if len(tiles) == 1 and ncols <= 128:
        (t, tc0, n) = tiles[0]
        g0 = (c0 // 512) * 512 if gi != self.NG else c0
        for kb in range(2):
            b = self.nb()
            for kk in range(4):
                k = kb * 4 + kk
                self.tp(ps[:, b, kk * 128:kk * 128 + n], x[:n, t, k * 128:(k + 1) * 128], self.ident[:n, :n],
                        [("x", t, k // 4), ("ident",)], [("ps", b)])
            wr = []
            for kk in range(4):
                wr += self.xr(gi, kb * 4 + kk, c0 - g0, c0 - g0 + ncols)
            self.cp("act", xT[:, kb * 4:kb * 4 + 4, c0:c0 + n],
                    ps[:, b, :].rearrange("p (k q) -> p k q", k=4)[:, :, 0:n], [("ps", b)], wr)
        return
    for k in range(KD):
        b = self.nb()
        for (t, tc0, n) in tiles:
            off = tc0 - c0
            self.tp(ps[:, b, off:off + n], x[:n, t, k * 128:(k + 1) * 128], self.ident[:n, :n],
                    [("x", t, k // 4), ("ident",)], [("ps", b)])
        g0 = (c0 // 512) * 512 if gi != self.NG else c0
        self.cp("act", xT[:, k, c0:c0 + ncols], ps[:, b, 0:ncols], [("ps", b)], self.xr(gi, k, c0 - g0, c0 - g0 + ncols))


def _ffn(self, li, w, lnw):
    cfg = self.cfg
    self.P.phase = "ffn%d_%d" % (li, w)
    d = self.dram
    ps, xT = self.ps, self.xT
    wg = d["ffn_w_gate"][li, w].rearrange("(k p) n -> p k n", p=128)
    wu = d["ffn_w_up"][li, w].rearrange("(k p) n -> p k n", p=128)
    wd = d["ffn_w_down"][li, w].rearrange("(c p) n -> p c n", p=128)
    gslot = self.load_gb(li, lnw)
    coef = 0.5 / ALPHA
    for pgroups in self.passes:
        pc0 = pgroups[0][1]
        for u in range(7):
            slot = self.cnt_gu % 2
            self.cnt_gu += 1
            wt = self.wgu[slot]
            self.dma("pool", wt[:, 0, :, :], wg[:, :, u * 384:(u + 1) * 384], ("wgu", slot, 0), writes=[("wgu", slot, 0)])
            self.dma("pool", wt[:, 1, :, :], wu[:, :, u * 384:(u + 1) * 384], ("wgu", slot, 1), writes=[("wgu", slot, 1)])
            for c in range(3):
                ff = u * 3 + c
                for gidx, (gi, c0, ncols, tiles) in enumerate(pgroups):
                    bA, bB = self.nb(), self.nb()
                    for which, b in ((0, bA), (1, bB)):
                        for k in range(KD):
                            self.mm(ps[:, b, 0:ncols], wt[:, which, k, c * 128:(c + 1) * 128], xT[:, k, c0:c0 + ncols],
                                    k == 0, k == KD - 1, [("wgu", slot, which)] + self.xr(gi, k), [("ps", b)])
                    ss = self.cnt_sg % 2
                    self.cnt_sg += 1
                    self.af(self.sg[ss][:, 0:ncols], ps[:, bA, 0:ncols], AF.Silu, [("ps", bA)], [("sg", ss)])
                    self.tt("dve", self.hT[:, ff, c0 - pc0:c0 - pc0 + ncols], self.sg[ss][:, 0:ncols], ps[:, bB, 0:ncols],
                            ALU.mult, [("sg", ss), ("ps", bB)], [("hT", ff, gidx)])
        for gidx, (gi, c0, ncols, tiles) in enumerate(pgroups):
            for u in range(7):
                slot = self.cnt_wd % 2
                self.cnt_wd += 1
                wt = self.wdb[slot]
                self.dma("pool", wt, wd[:, u * 3:(u + 1) * 3, :], ("wd", slot), writes=[("wd", slot)])
                for i, (t, tc0, n) in enumerate(tiles):
                    for hf in range(2):
                        for c in range(3):
                            ff = u * 3 + c
                            self.mm(ps[:n, 2 * i + hf, :], self.hT[:, ff, tc0 - pc0:tc0 - pc0 + n],
                                    wt[:, c, hf * 512:(hf + 1) * 512], u == 0 and c == 0, u == 6 and c == 2,
                                    [("hT", ff, gidx), ("wd", slot)], [("ps", 2 * i + hf)])
            for i, tile in enumerate(tiles):
                self.ln_tile(tile, (2 * i, 2 * i + 1), coef, gslot)
            self.t_phase((gi, c0, ncols, tiles))


def _alloc_ffn(self):
    if self.phase_kind != "ffn":
        self.P.fence()
    self.phase_kind = "ffn"
    self.off = self.phase_off
    pw = 512 + NSAMP
    self.hT = self.buf([NFF, pw], BF16)
    self.wgu = [self.buf([2, KD, 384], BF16) for _ in range(2)]
    self.wdb = [self.buf([3, D], BF16) for _ in range(2)]
    self.sg = [self.buf([512]) for _ in range(2)]
    self.ffn_end = self.off


def _ln_only(self, li, lnw):
    gslot = self.load_gb(li, lnw)
    for grp in self.groups:
        for tile in grp[3]:
            self.ln_tile(tile, None, 0.0, gslot)
        self.t_phase(grp)


def _finish(self):
    d = self.dram
    cfg = self.cfg
    yp = d["y_prompt"].rearrange("(t p) d -> p t d", p=128)
    for g in range(self.NG):
        self.dma("sp", yp[:, 4 * g:4 * g + 4, :], self.x[:, 4 * g:4 * g + 4, :], ("yout", g),
                 reads=[("x", t, h) for t in range(4 * g, 4 * g + 4) for h in range(2)])
    self.dma("sp", d["y_sample"], self.x[:NSAMP, cfg.NT, :], ("yout", self.NG),
             reads=[("x", cfg.NT, 0), ("x", cfg.NT, 1)])


K.nb = _nb
K.xr = _xr
K.setup = _setup
K.load_gb = _load_gb
K.ln_tile = _ln_tile
K.t_phase = _t_phase
K.ffn = _ffn
K.alloc_ffn = _alloc_ffn
K.ln_only = _ln_only
K.finish = _finish


IN_SPECS = [
    ("x_prompt", lambda c: (c.S, D), F32), ("x_sample", lambda c: (NSAMP, D), F32),
    ("cache_kv_latent", lambda c: (2, c.NPOOL, PAGE, KVR), F32), ("cache_k_rope", lambda c: (2, c.NPOOL, PAGE, ROPE), F32),
    ("state_pool", lambda c: (NSAMP * 15, D), F32), ("page_table", lambda c: (NSAMP, c.NPG), I32),
    ("ln_g", lambda c: (DEPTH, 3, D), F32), ("ln_b", lambda c: (DEPTH, 3, D), F32),
    ("ffn_w_gate", lambda c: (DEPTH, 2, D, DFF), F32), ("ffn_w_up", lambda c: (DEPTH, 2, D, DFF), F32),
    ("ffn_w_down", lambda c: (DEPTH, 2, DFF, D), F32),
    ("a_w_in", lambda c: (2, D, 704), F32), ("a_q_norm", lambda c: (2, QR), F32), ("a_kv_norm", lambda c: (2, KVR), F32),
    ("a_w_uq", lambda c: (2, QR, 1536), F32), ("a_w_uk", lambda c: (2, KVR, HEADS, 128), F32),
    ("a_w_uv", lambda c: (2, KVR, HEADS, 128), F32), ("a_w_o", lambda c: (2, D, D), F32),
    ("b_w_in", lambda c: (1, D, 2 * DC), F32), ("b_b_in", lambda c: (1, 2 * DC), F32),
    ("b_v_norm_g", lambda c: (1, DC), F32), ("b_v_norm_b", lambda c: (1, DC), F32),
    ("b_w_s", lambda c: (1, HEADS, 128, 128), F32), ("b_b_s", lambda c: (1, HEADS, 128), F32),
    ("b_w_out", lambda c: (1, DC, D), F32),
    ("c_w_in", lambda c: (1, D, D), F32), ("c_w_grp", lambda c: (1, 4, 256, 256), F32),
    ("c_scale", lambda c: (1, D), F32), ("c_w_out", lambda c: (1, D, D), F32),
    ("c_ident", lambda c: (128, 128), F32), ("c_cmask", lambda c: (128, 128), F32), ("c_tril", lambda c: (128, 128), F32),
    ("c_smask", lambda c: (HEADS, NSAMP * NSAMP), F32), ("c_rope_tm", lambda c: (128, c.NT + 1, 64), F32),
    ("c_rope_fm", lambda c: (64, 2, c.TT), F32), ("c_poolc", lambda c: (128, 120), F32),
    ("c_pmod", lambda c: (128, 1), F32),
]
OUT_SPECS = [
    ("y_prompt", lambda c: (c.S, D)), ("y_sample", lambda c: (NSAMP, D)),
    ("o_ckv_p", lambda c: (2, c.S, KVR)), ("o_kr_p", lambda c: (2, c.S, ROPE)),
    ("o_ckv_s", lambda c: (2, NSAMP, KVR)), ("o_kr_s", lambda c: (2, NSAMP, ROPE)),
    ("o_v_s", lambda c: (NSAMP, DC)), ("o_pool_p", lambda c: (15, D)), ("o_pool_s", lambda c: (NSAMP, 15, D)),
]
DEBUG = False


def build(cfg):
    from contextlib import ExitStack
    k = K(cfg)
    nc = k.nc
    for name, shp, dt in IN_SPECS:
        k.din(name, shp(cfg), dt)
    for name, shp in OUT_SPECS:
        k.dout(name, shp(cfg))
    if DEBUG:
        k.dout("o_dbg", (128, 8192))
    k.dbgcol = 0
    k.arena_bytes = SBUF_BYTES
    k.cnt_gu = k.cnt_wd = k.cnt_sg = 0
    k.cnt_wv = k.cnt_vg = k.cnt_wu = k.cnt_wo = k.cnt_uc = 0
    k.phase_kind = "ffn"
    with ExitStack() as es:
        k.arena = es.enter_context(nc.sbuf_tensor("arena", [128, SBUF_BYTES // 4], F32))
        k.ps = es.enter_context(nc.psum_tensor("ps", [128, 8, 512], F32))
        k.setup()
        for li in cfg.layers:
            kind = li % 3
            j = li // 3
            k.alloc_ffn()
            k.ffn(li, 0, 0)
            if not cfg.mixers:
                k.ln_only(li, 1)
            elif kind == 0:
                k.mla(li, j)
            elif kind == 1:
                k.cmlp(li)
            else:
                k.poolmix(li)
            k.alloc_ffn()
            k.ffn(li, 1, 2)
        k.finish()
        k.esem = {e: es.enter_context(nc.semaphore("e_" + e)) for e in ENGS}
        k.dsem = {}
        for i, key in enumerate(k.P.dma_tot):
            k.dsem[key] = es.enter_context(nc.semaphore("d%d" % i))
        k.emit()
    return nc, k


def make_in_maps(cfg, inputs, ncores=8):
    consts = const_inputs(cfg)
    maps = []
    for c in range(ncores):
        m = {}
        m["x_prompt"] = np.ascontiguousarray(inputs["x_prompt"][c])
        m["x_sample"] = np.ascontiguousarray(inputs["x_sample"][c * NSAMP:(c + 1) * NSAMP, 0])
        m["cache_kv_latent"] = inputs["cache_kv_latent"]
        m["cache_k_rope"] = inputs["cache_k_rope"]
        m["state_pool"] = np.ascontiguousarray(inputs["state_pool"][0, c * NSAMP:(c + 1) * NSAMP]).reshape(NSAMP * 15, D)
        m["page_table"] = np.ascontiguousarray(inputs["page_table"][c * NSAMP:(c + 1) * NSAMP]).astype(np.int32)
        for name, _, _ in IN_SPECS:
            if name in m:
                continue
            if name in consts:
                m[name] = consts[name]
            else:
                m[name] = np.asarray(inputs[name])
        maps.append(m)
    return maps


def gather_outputs(cfg, results, ncores=8):
    S = cfg.S
    R = results
    y_p = np.stack([R[c]["y_prompt"] for c in range(ncores)])
    y_s = np.concatenate([R[c]["y_sample"] for c in range(ncores)])[:, None, :]
    ckv_p = np.stack([R[c]["o_ckv_p"] for c in range(ncores)], axis=1)
    kr_p = np.stack([R[c]["o_kr_p"] for c in range(ncores)], axis=1)
    ckv_s = np.concatenate([R[c]["o_ckv_s"] for c in range(ncores)], axis=1)[:, :, None, :]
    kr_s = np.concatenate([R[c]["o_kr_s"] for c in range(ncores)], axis=1)[:, :, None, :]
    v_s = np.concatenate([R[c]["o_v_s"] for c in range(ncores)])[None, :, None, :]
    pool_p = np.stack([R[c]["o_pool_p"] for c in range(ncores)])[None]
    pool_s = np.concatenate([R[c]["o_pool_s"] for c in range(ncores)])[None]
    outs = (y_p, y_s, ckv_p, kr_p, ckv_s, kr_s, v_s, pool_p, pool_s)
    return tuple(np.ascontiguousarray(o, dtype=np.float32) for o in outs)


_CACHE = {}


def kernel(**inputs):
    cfg = Cfg()
    inputs = {k_: np.asarray(v) for k_, v in inputs.items()}
    if "nc" not in _CACHE:
        _CACHE["nc"] = build(cfg)[0]
    nc = _CACHE["nc"]
    maps = make_in_maps(cfg, inputs)
    res = run_bass_kernel_spmd(nc, maps, core_ids=list(range(8)))
    return gather_outputs(cfg, res.results)


def _load_colvec(self, dram1d, nk, dst, key, rname):
    tmp = self.buf([128])
    self.dma("sp", tmp[:nk, :], dram1d.rearrange("(k p) -> k p", p=128), key, writes=[(rname, "tmp")])
    b = self.nb()
    self.tp(self.ps[:, b, 0:nk], tmp[:nk, :], self.ident[:nk, :nk], [(rname, "tmp"), ("ident",)], [("ps", b)])
    self.cp("dve", dst, self.ps[:, b, 0:nk], [("ps", b)], [(rname,)])


def _proj_out_ln(self, grp, actT, kc_n, w_sb, wres, act_res, coef, gslot):
    gi, c0, ncols, tiles = grp
    ps = self.ps
    for (t, tc0, n) in tiles:
        b0, b1 = self.nb(), self.nb()
        for hf, b in ((0, b0), (1, b1)):
            for kc in range(kc_n):
                self.mm(ps[:n, b, :], actT[:, kc, tc0 - c0:tc0 - c0 + n], w_sb[:, kc, hf * 512:(hf + 1) * 512],
                        kc == 0, kc == kc_n - 1, [act_res(kc), wres], [("ps", b)])
        self.ln_tile((t, tc0, n), (b0, b1), coef, gslot)
    self.t_phase(grp)


def _poolmix(self, li):
    cfg = self.cfg
    d = self.dram
    S, NT = cfg.S, cfg.NT
    ps, xT = self.ps, self.xT
    self.P.phase = "pool"
    self.P.fence()
    self.phase_kind = "pool"
    self.off = self.phase_off
    cwin = self.buf([KD, D], BF16)
    cwout = self.buf([KD, D], BF16)
    wgrp = self.buf([4, 2, 256], BF16)
    cscale = self.buf([8])
    poolc = self.buf([8, 15])
    hb_off = self.off
    hb = [self.buf([8, 527]) for _ in range(2)]
    tmpA = self.buf([2, 527])
    tmpB = self.buf([2, 527])
    plT = self.buf([8, 512], BF16)
    zT = self.buf([8, 512], BF16)
    hrow = self.view(hb_off, [D])
    hsrow = self.view(hb_off + 4096, [D])
    stt_tm = self.view(hb_off + 8192, [2, D])
    hsT = self.view(hb_off + 16384, [8, NSAMP, 16])
    sw = self.view(hb_off + 16384 + 8192, [8, NSAMP])
    assert 16384 + 8192 + 512 <= 2 * 8 * 527 * 4
    gslot = self.load_gb(li, 1)
    coef = 1.0 / ALPHA
    WINS = (2, 4, 8, 16)
    self.dma("pool", cwin, d["c_w_in"][0].rearrange("(k p) n -> p k n", p=128), "cwin", writes=[("cwin",)])
    self.dma("pool", cwout, d["c_w_out"][0].rearrange("(k p) n -> p k n", p=128), "cwout", writes=[("cwout",)])
    self.dma("pool", wgrp, d["c_w_grp"][0].rearrange("g (c p) e -> p g c e", p=128), "wgrp", writes=[("wgrp",)])
    self.dma("sp", poolc, d["c_poolc"].rearrange("p (a b) -> p a b", a=8), "poolc", writes=[("poolc",)])
    self.load_colvec(d["c_scale"][0], 8, cscale, "cscale", "cscale")
    lt = self.tiles[NT - 1]
    for (tile, stage, n) in ((lt, hrow, 128), (self.tiles[NT], hsrow, NSAMP)):
        t, tc0, _ = tile
        gi = min(t // 4, self.NG)
        for hf in range(2):
            b = self.nb()
            for k in range(KD):
                self.mm(ps[:n, b, :], xT[:, k, tc0:tc0 + n], cwin[:, k, hf * 512:(hf + 1) * 512], k == 0, k == KD - 1,
                        self.xr(gi, k) + [("cwin",)], [("ps", b)])
            self.cp("act", stage[:n, hf * 512:(hf + 1) * 512], ps[:n, b, :], [("ps", b)], [("hrow", t, hf)])
    self.dma("sp", d["o_pool_p"], hrow[113:128, :], "o_pool_p", reads=[("hrow", NT - 1, 0), ("hrow", NT - 1, 1)])
    self.dma("sp", d["o_pool_s"][:, 14, :], hsrow[:NSAMP, :], "o_pool_s", reads=[("hrow", NT, 0), ("hrow", NT, 1)])
    sp3 = d["state_pool"].rearrange("(b r) d -> b r d", r=15)
    self.dma("sp", d["o_pool_s"][:, 0:14, :], sp3[:, 1:15, :], "o_pool_s2")
    P_ = self.P
    P_.fence()
    prev = None
    for grp in self.groups:
        gi, c0, ncols, tiles = grp
        samp = gi == self.NG
        cur = gi % 2
        H = hb[cur]
        if samp:
            P_.fence()
            self.dma("sp", stt_tm[:120, :, :], d["state_pool"].rearrange("(a p) d -> p a d", p=120), "stt", writes=[("stt",)])
            for a in range(2):
                for c in range(8):
                    b = self.nb()
                    self.tp(ps[:, b, 0:120], stt_tm[:120, a, c * 128:(c + 1) * 128], self.ident[:120, :120],
                            [("stt",), ("ident",)], [("ps", b)])
                    self.cp("dve", hsT[:, c, 8 * a:8 * a + 8, 0:15], ps[:, b, 0:120].rearrange("p (s r) -> p s r", r=15),
                            [("ps", b)], [("hsT", c, a)])
        if not samp:
            if gi == 0:
                self.memset("dve", H[:, :, 0:15], 0.0, [], [("hb", cur, "halo")])
            else:
                self.cp("act", H[:, :, 0:15], hb[prev][:, :, 512:527], [("hb", prev, c) for c in range(8)],
                        [("hb", cur, "halo")])
        for c in range(8):
            b = self.nb()
            for k in range(KD):
                self.mm(ps[:, b, 0:ncols], cwin[:, k, c * 128:(c + 1) * 128], xT[:, k, c0:c0 + ncols], k == 0, k == KD - 1,
                        [("cwin",)] + self.xr(gi, k), [("ps", b)])
            if samp:
                self.cp("act", hsT[:, c, :, 15], ps[:, b, 0:ncols], [("ps", b)], [("hsT", c, 2)])
            else:
                self.cp("act", H[:, c, 15:527], ps[:, b, 0:512], [("ps", b)], [("hb", cur, c)])
        for g, w in enumerate(WINS):
            cs = slice(2 * g, 2 * g + 2)
            rin = [("hb", cur, 2 * g), ("hb", cur, 2 * g + 1), ("hb", cur, "halo")]
            if samp:
                for c in (2 * g, 2 * g + 1):
                    self.red("dve", sw[:, c, :], hsT[:, c, :, 16 - w:16], ALU.add,
                             [("hsT", c, 0), ("hsT", c, 1), ("hsT", c, 2)], [("sw", c)])
                    self.stt("dve", plT[:, c, 0:NSAMP], sw[:, c, :], 1.0 / w, hsT[:, c, :, 15], ALU.mult, ALU.subtract,
                             [("sw", c), ("hsT", c, 2)], [("plT", c)])
                continue
            src = H[:, cs, :]
            lo = 0
            bufs = (tmpA, tmpB)
            for si in range(g + 1):
                sh = 1 << si
                dst = bufs[si % 2]
                nlo = lo + sh
                self.tt("dve", dst[:, :, nlo:527], src[:, :, nlo:527], src[:, :, nlo - sh:527 - sh], ALU.add,
                        rin + [("ptmp", (si + 1) % 2)], [("ptmp", si % 2)])
                src = dst
                lo = nlo
            last = ("ptmp", g % 2)
            if gi == 0:
                self.tt("dve", src[:, :, 15:30], src[:, :, 15:30], poolc[:, cs, :], ALU.mult, [last, ("poolc",)], [last])
            self.stt("dve", plT[:, cs, :], src[:, :, 15:527], 1.0 / w, H[:, cs, 15:527], ALU.mult, ALU.subtract,
                     [last] + rin, [("plT", 2 * g), ("plT", 2 * g + 1)])
        for g in range(4):
            for ec in range(2):
                b = self.nb()
                for dc in range(2):
                    self.mm(ps[:, b, 0:ncols], wgrp[:, g, dc, ec * 128:(ec + 1) * 128], plT[:, 2 * g + dc, 0:ncols],
                            dc == 0, dc == 1, [("wgrp",), ("plT", 2 * g + dc)], [("ps", b)])
                self.af(zT[:, 2 * g + ec, 0:ncols], ps[:, b, 0:ncols], AF.Identity, [("ps", b), ("cscale",)],
                        [("zT", 2 * g + ec)], scale=cscale[:, 2 * g + ec:2 * g + ec + 1])
        if DEBUG and gi == 0:
            self.dbgmap = {}
            self.dbgmap["cscale"] = self.dbg(cscale, 128, 8, [("cscale",)])
            self.dbgmap["hb0"] = self.dbg(H[:, 0, :], 128, 527, [("hb", cur, 0), ("hb", cur, "halo")])
            self.dbgmap["hb7"] = self.dbg(H[:, 7, :], 128, 527, [("hb", cur, 7), ("hb", cur, "halo")])
            self.dbgmap["pl0"] = self.dbg(plT[:, 0, :], 128, 512, [("plT", 0)])
            self.dbgmap["pl7"] = self.dbg(plT[:, 7, :], 128, 512, [("plT", 7)])
            self.dbgmap["z0"] = self.dbg(zT[:, 0, :], 128, 512, [("zT", 0)])
            self.dbgmap["z7"] = self.dbg(zT[:, 7, :], 128, 512, [("zT", 7)])
            self.dbgmap["cwout"] = self.dbg(cwout[:, 0, :], 128, 1024, [("cwout",)])
        self.proj_out_ln(grp, zT, 8, cwout, ("cwout",), lambda kc: ("zT", kc), coef, gslot)
        prev = cur


def _dbg(self, ap, parts, n, reads):
    col = self.dbgcol
    self.dbgcol += n
    q = "sp" if ap.dtype == F32 else "pool"
    self.dma(q, self.dram["o_dbg"][0:parts, col:col + n], ap, ("dbg", col), reads=reads)
    return col


K.dbg = _dbg
K.load_colvec = _load_colvec
K.proj_out_ln = _proj_out_ln
K.poolmix = _poolmix


def _cmlp(self, li):
    cfg = self.cfg
    d = self.dram
    S, NT = cfg.S, cfg.NT
    ps, xT = self.ps, self.xT
    self.P.phase = "cmlp"
    self.P.fence()
    self.phase_kind = "cmlp"
    self.off = self.phase_off
    winu = [self.buf([KD, 256], BF16) for _ in range(2)]
    winv = [self.buf([KD, 512], BF16) for _ in range(2)]
    binv = [self.buf([512], BF16) for _ in range(2)]
    wout = [self.buf([2, D], BF16) for _ in range(2)]
    vgb = [self.buf([2, 512]) for _ in range(2)]
    vf_off = self.off
    v_f = self.buf([2, DC])
    vn = self.buf([2, DC], BF16)
    uc = [self.buf([256], BF16) for _ in range(2)]
    huc = [self.buf([256], BF16) for _ in range(2)]
    WsT = self.buf([HEADS, 128], BF16)
    wsS = self.buf([HEADS, NSAMP], BF16)
    bsrow = self.buf([HEADS, 128], BF16)
    bsS = self.buf([HEADS, NSAMP], BF16)
    tril = self.buf([128])
    w00 = self.buf([HEADS])
    bs0 = self.buf([HEADS])
    binu = self.buf([24])
    vst = self.buf([2, 48])
    stage = self.view(vf_off, [HEADS, 128])
    gslot = self.load_gb(li, 1)
    coef = 1.0 / ALPHA
    w_in = d["b_w_in"][0].rearrange("(k p) n -> p k n", p=128)
    w_out = d["b_w_out"][0].rearrange("(c p) n -> p c n", p=128)
    b_in = d["b_b_in"][0]
    vfall = [("vf", ti, b) for ti in range(2) for b in range(6)]
    self.dma("sp", stage, d["b_w_s"][0].rearrange("h i j -> i h j"), "ws", writes=vfall)
    self.dma("sp", tril, d["c_tril"], "tril", writes=[("tril",)])
    for h in range(HEADS):
        self.tt("dve", stage[:, h, :], stage[:, h, :], tril, ALU.mult, vfall + [("tril",)], [("wsm", h)])
    for h in range(HEADS):
        b = self.nb()
        self.tp(ps[:, b, 0:128], stage[:, h, :], self.ident, [("wsm", h), ("ident",)], [("ps", b)])
        self.cp("act", WsT[:, h, :], ps[:, b, 0:128], [("ps", b)], [("WsT",)])
    self.P.add("dve", lambda e: e.memset(vst[:, 0, 0:1], 0.0), [("wsm", h) for h in range(HEADS)], vfall)
    self.dma("pool", bsrow[0:1, :, :], d["b_b_s"][0:1], "bsrow", writes=[("bsrow",)])
    self.dma("sp", w00[:NSAMP, :], d["b_w_s"][0][:, 0, 0].partition_broadcast(NSAMP), "w00", writes=[("w00",)],
             allow_slow_non_contiguous=True)
    self.dma("sp", bs0[0:1, :], d["b_b_s"][0:1, :, 0], "bs0", writes=[("bs0",)], allow_slow_non_contiguous=True)
    for h in range(HEADS):
        self.ts("dve", wsS[:NSAMP, h, :], self.ident[:NSAMP, :NSAMP], w00[:NSAMP, h:h + 1], None, ALU.mult, None,
                [("w00",), ("ident",)], [("wsS",)])
        self.ts("dve", bsS[0:1, h, :], self.ones_row[0:1, 0:NSAMP], bs0[0:1, h:h + 1], None, ALU.mult, None,
                [("bs0",), ("ones",)], [("bsS",)])
    self.load_colvec(b_in[0:DC], 24, binu, "binu", "binu")
    plist = []
    for g in range(self.NG):
        for hp in range(2):
            tl = [self.tiles[4 * g + 2 * hp], self.tiles[4 * g + 2 * hp + 1]]
            plist.append((g, g * 512 + hp * 256, 256, tl))
    plist.append((self.NG, S, NSAMP, [self.tiles[NT]]))
    for (gi, c0, ncols, tiles) in plist:
        samp = gi == self.NG
        nt = len(tiles)
        lo = c0 - (gi * 512 if not samp else c0)
        for blk in range(6):
            sl = self.cnt_wv % 2
            self.cnt_wv += 1
            self.dma("pool", winv[sl], w_in[:, :, DC + blk * 512:DC + (blk + 1) * 512], ("winv", sl), writes=[("winv", sl)])
            self.dma("pool", binv[sl][0:1, :], b_in[DC + blk * 512:DC + (blk + 1) * 512].rearrange("(o n) -> o n", o=1),
                     ("binv", sl), writes=[("binv", sl)])
            for ti, (t, tc0, n) in enumerate(tiles):
                b = self.nb()
                for k in range(KD):
                    self.mm(ps[:n, b, :], xT[:, k, tc0:tc0 + n], winv[sl][:, k, :], k == 0, False,
                            self.xr(gi, k, lo, lo + ncols) + [("winv", sl)], [("ps", b)])
                self.mm(ps[:n, b, :], self.ones_row[0:1, 0:n], binv[sl][0:1, :], False, True,
                        [("ones",), ("binv", sl)], [("ps", b)])
                self.af(v_f[:n, ti, blk * 512:(blk + 1) * 512], ps[:n, b, :], AF.Gelu_apprx_tanh, [("ps", b)],
                        [("vf", ti, blk)])
                self.P.add("dve", (lambda e, n=n, ti=ti, blk=blk: e.bn_stats(
                    out=vst[:n, ti, blk * 6:blk * 6 + 6], in_=v_f[:n, ti, blk * 512:(blk + 1) * 512])),
                    [("vf", ti, blk)], [("vst", ti, blk)])
        for ti, (t, tc0, n) in enumerate(tiles):
            self.P.add("dve", (lambda e, n=n, ti=ti: e.bn_aggr(
                out=vst[:n, ti, 36:38], in_=vst[:n, ti, 0:36].rearrange("p (a b) -> p a b", a=6))),
                [("vst", ti, b_) for b_ in range(6)], [("vst", ti, "mv")])
            self.af(vst[:n, ti, 38:39], vst[:n, ti, 37:38], AF.Sqrt, [("vst", ti, "mv")], [("vst", ti, "sd")],
                    bias=float(LN_EPS), scale=1.0)
            self.recip(vst[:n, ti, 39:40], vst[:n, ti, 38:39], [("vst", ti, "sd")], [("vst", ti, "rs")])
        for blk in range(6):
            sl = self.cnt_vg % 2
            self.cnt_vg += 1
            bsl = slice(blk * 512, (blk + 1) * 512)
            self.dma("sp", vgb[sl][:, 0, :], d["b_v_norm_g"][0, bsl].partition_broadcast(128), ("vgb", sl, 0),
                     writes=[("vgb", sl, 0)])
            self.dma("sp", vgb[sl][:, 1, :], d["b_v_norm_b"][0, bsl].partition_broadcast(128), ("vgb", sl, 1),
                     writes=[("vgb", sl, 1)])
            for ti, (t, tc0, n) in enumerate(tiles):
                self.stt("dve", v_f[:n, ti, bsl], v_f[:n, ti, bsl], vst[:n, ti, 36:37], vgb[sl][:n, 0, :],
                         ALU.subtract, ALU.mult, [("vf", ti, blk), ("vst", ti, "mv"), ("vgb", sl, 0)], [("vf", ti, blk)])
                if samp:
                    self.stt("dve", v_f[:n, ti, bsl], v_f[:n, ti, bsl], vst[:n, ti, 39:40], vgb[sl][:n, 1, :],
                             ALU.mult, ALU.add, [("vf", ti, blk), ("vst", ti, "rs"), ("vgb", sl, 1)], [("vf", ti, blk)])
                    self.cp("act", vn[:n, ti, bsl], v_f[:n, ti, bsl], [("vf", ti, blk)], [("vn", ti, blk)])
                else:
                    self.stt("dve", vn[:n, ti, bsl], v_f[:n, ti, bsl], vst[:n, ti, 39:40], vgb[sl][:n, 1, :],
                             ALU.mult, ALU.add, [("vf", ti, blk), ("vst", ti, "rs"), ("vgb", sl, 1)], [("vn", ti, blk)])
        if samp:
            self.dma("sp", d["o_v_s"], v_f[:NSAMP, 0, :], "o_v_s", reads=[("vf", 0, b_) for b_ in range(6)])
        accb = [[self.nb(), self.nb()] for _ in range(nt)]
        used = set(b for pr in accb for b in pr)
        for ch in range(24):
            h = ch // 3
            if ch % 2 == 0:
                su = self.cnt_wu % 2
                self.cnt_wu += 1
                self.dma("pool", winu[su], w_in[:, :, ch * 128:(ch + 2) * 128], ("winu", su), writes=[("winu", su)])
                so = self.cnt_wo % 2
                self.cnt_wo += 1
                self.dma("pool", wout[so], w_out[:, ch:ch + 2, :], ("wout", so), writes=[("wout", so)])
            b = self.nb()
            while b in used:
                b = self.nb()
            for k in range(KD):
                self.mm(ps[:, b, 0:ncols], winu[su][:, k, (ch % 2) * 128:(ch % 2 + 1) * 128], xT[:, k, c0:c0 + ncols],
                        k == 0, k == KD - 1, [("winu", su)] + self.xr(gi, k, lo, lo + ncols), [("ps", b)])
            us = self.cnt_uc % 2
            self.cnt_uc += 1
            self.af(uc[us][:, 0:ncols], ps[:, b, 0:ncols], AF.Gelu_apprx_tanh, [("ps", b), ("binu",)], [("uc", us)],
                    bias=binu[:, ch:ch + 1], scale=1.0)
            b2 = self.nb()
            while b2 in used:
                b2 = self.nb()
            for ti, (t, tc0, n) in enumerate(tiles):
                o = tc0 - c0
                blk = ch // 4
                if samp:
                    self.mm(ps[:, b2, o:o + n], vn[:n, ti, ch * 128:(ch + 1) * 128], wsS[:n, h, :], True, False,
                            [("vn", ti, blk), ("wsS",)], [("ps", b2)])
                    self.mm(ps[:, b2, o:o + n], self.ones_row[0:1, 0:128], bsS[0:1, h, :], False, True,
                            [("ones",), ("bsS",)], [("ps", b2)])
                else:
                    self.mm(ps[:, b2, o:o + n], vn[:n, ti, ch * 128:(ch + 1) * 128], WsT[:, h, :], True, False,
                            [("vn", ti, blk), ("WsT",)], [("ps", b2)])
                    self.mm(ps[:, b2, o:o + n], self.ones_row[0:1, 0:128], bsrow[0:1, h, :], False, True,
                            [("ones",), ("bsrow",)], [("ps", b2)])
            self.tt("dve", huc[us][:, 0:ncols], uc[us][:, 0:ncols], ps[:, b2, 0:ncols], ALU.mult,
                    [("uc", us), ("ps", b2)], [("huc", us)])
            for ti, (t, tc0, n) in enumerate(tiles):
                o = tc0 - c0
                for hf in range(2):
                    self.mm(ps[:n, accb[ti][hf], :], huc[us][:, o:o + n], wout[so][:, ch % 2, hf * 512:(hf + 1) * 512],
                            ch == 0, ch == 23, [("huc", us), ("wout", so)], [("ps", accb[ti][hf])])
        for ti, tile in enumerate(tiles):
            self.ln_tile(tile, accb[ti], coef, gslot)
        self.t_phase((gi, c0, ncols, tiles))


K.cmlp = _cmlp


def _mla(self, li, j):
    cfg = self.cfg
    d = self.dram
    S, NT, TT, NPG = cfg.S, cfg.NT, cfg.TT, cfg.NPG
    ps, xT = self.ps, self.xT
    P = self.P
    P.phase = "mlaA%d" % li
    P.fence()
    self.phase_kind = "mla"
    self.off = self.phase_off
    gslot = self.load_gb(li, 1)
    coef = 1.0 / ALPHA
    SC = float(ATT_SCALE)
    wuv = self.buf([2, HEADS, 128], BF16)
    wo = self.buf([HEADS, D], BF16)
    qn_c = self.buf([3])
    kvn_bc = self.buf([KVR])
    ast = self.buf([2, 8])
    ckvT_s = self.buf([2, NSAMP], BF16)
    krT_s = self.buf([NSAMP], BF16)
    ckv_tm_s = self.buf([KVR], BF16)
    c_off = self.off
    cqT = self.buf([3, TT], BF16)
    ckvT = self.buf([2, TT], BF16)
    krT = self.buf([TT], BF16)
    ckv_tm = self.buf([NT, KVR], BF16)
    mid_off = self.off
    cs_tm = self.buf([NT + 1, 64])
    self.dma("pool", wuv, d["a_w_uv"][j].rearrange("(c p) h v -> p c h v", p=128), "wuv", writes=[("wuv",)])
    self.dma("pool", wo, d["a_w_o"][j].rearrange("(k p) n -> p k n", p=128), "wo", writes=[("wo",)])
    self.dma("sp", kvn_bc, d["a_kv_norm"][j].partition_broadcast(128), "kvn", writes=[("kvn",)])
    self.dma("sp", cs_tm, d["c_rope_tm"], "cs_tm", writes=[("cs_tm",)])
    self.load_colvec(d["a_q_norm"][j], 3, qn_c, "qn", "qn")
    win = self.buf([KD, 704], BF16)
    cqn = [self.buf([QR]) for _ in range(2)]
    ckvf = [self.buf([KVR]) for _ in range(2)]
    krf = [self.buf([128]) for _ in range(2)]
    for kk in range(2):
        self.memset("dve", krf[kk], 0.0, [], [("krf", kk, 0), ("krf", kk, 1)])
    ktmp = [self.buf([128]) for _ in range(2)]
    junk = self.buf([QR])
    self.dma("pool", win, d["a_w_in"][j].rearrange("(k p) n -> p k n", p=128), "win", writes=[("win",)])
    for (t, tc0, n) in self.tiles:
        samp = t == NT
        gi = min(t // 4, self.NG)
        lo = tc0 - gi * 512 if not samp else 0
        s2 = t % 2
        bq, bk = self.nb(), self.nb()
        for k in range(KD):
            self.mm(ps[:n, bq, 0:QR], xT[:, k, tc0:tc0 + n], win[:, k, 0:QR], k == 0, k == KD - 1,
                    self.xr(gi, k, lo, lo + n) + [("win",)], [("ps", bq)])
        for k in range(KD):
            self.mm(ps[:n, bk, 0:320], xT[:, k, tc0:tc0 + n], win[:, k, QR:704], k == 0, k == KD - 1,
                    self.xr(gi, k, lo, lo + n) + [("win",)], [("ps", bk)])
        self.memset("dve", ast[:n, s2, 0:2], 0.0, [], [("ast", s2, "ss")])
        self.af(junk[:n, :], ps[:n, bq, 0:QR], AF.Square, [("ps", bq), ("ast", s2, "ss")], [("junk",), ("ast", s2, "ssq")],
                accum_out=ast[:n, s2, 0:1])
        self.af(junk[:n, 0:KVR], ps[:n, bk, 0:KVR], AF.Square, [("ps", bk), ("ast", s2, "ss")],
                [("junk",), ("ast", s2, "ssk")], accum_out=ast[:n, s2, 1:2])
        self.af(ast[:n, s2, 2:3], ast[:n, s2, 0:1], AF.Sqrt, [("ast", s2, "ssq")], [("ast", s2, "sdq")],
                bias=float(RMS_EPS), scale=1.0 / QR)
        self.af(ast[:n, s2, 3:4], ast[:n, s2, 1:2], AF.Sqrt, [("ast", s2, "ssk")], [("ast", s2, "sdk")],
                bias=float(RMS_EPS), scale=1.0 / KVR)
        self.recip(ast[:n, s2, 4:6], ast[:n, s2, 2:4], [("ast", s2, "sdq"), ("ast", s2, "sdk")], [("ast", s2, "rs")])
        self.af(cqn[s2][:n, :], ps[:n, bq, 0:QR], AF.Identity, [("ps", bq), ("ast", s2, "rs")], [("cqn", s2)],
                scale=ast[:n, s2, 4:5])
        self.stt("dve", ckvf[s2][:n, :], ps[:n, bk, 0:KVR], ast[:n, s2, 5:6], kvn_bc[:n, :], ALU.mult, ALU.mult,
                 [("ps", bk), ("ast", s2, "rs"), ("kvn",)], [("ckvf", s2)])
        if samp:
            self.dma("sp", d["o_ckv_s"][j], ckvf[s2][:n, :], ("o_ckv", s2), reads=[("ckvf", s2)])
        else:
            self.dma("sp", d["o_ckv_p"][j, tc0:tc0 + n, :], ckvf[s2][:n, :], ("o_ckv", s2), reads=[("ckvf", s2)])
        if samp:
            self.cp("act", ckv_tm_s[:n, :], ckvf[s2][:n, :], [("ckvf", s2)], [("ckv_tm", t)])
        else:
            self.cp("act", ckv_tm[:n, t, :], ckvf[s2][:n, :], [("ckvf", s2)], [("ckv_tm", t)])
        sk = getattr(cfg, "skip", ())
        if "all" in sk:
            continue
        x1 = ps[:n, bk, 256:288]
        x2 = ps[:n, bk, 288:320]
        cosv = cs_tm[:n, t, 0:32]
        sinv = cs_tm[:n, t, 32:64]
        kt_ = ktmp[s2]
        for (o_, a_, b_) in ((0, x1, cosv), (32, x2, sinv), (64, x1, sinv), (96, x2, cosv)):
            self.tt("dve", kt_[:n, o_:o_ + 32], a_, b_, ALU.mult, [("ps", bk), ("cs_tm",)], [("ktmp", s2, o_)])
        self.tt("dve", krf[s2][:n, 0:32], kt_[:n, 0:32], kt_[:n, 32:64], ALU.subtract,
                [("ktmp", s2, 0), ("ktmp", s2, 32)], [("krf", s2, 0)])
        self.tt("dve", krf[s2][:n, 32:64], kt_[:n, 64:96], kt_[:n, 96:128], ALU.add,
                [("ktmp", s2, 64), ("ktmp", s2, 96)], [("krf", s2, 1)])
        if samp:
            self.dma("sp", d["o_kr_s"][j], krf[s2][:n, 0:ROPE], ("o_kr", s2), reads=[("krf", s2, 0), ("krf", s2, 1)])
        else:
            self.dma("sp", d["o_kr_p"][j, tc0:tc0 + n, :], krf[s2][:n, 0:ROPE], ("o_kr", s2),
                     reads=[("krf", s2, 0), ("krf", s2, 1)])
        if "tr" in sk:
            continue
        bX, bY = self.nb(), self.nb()
        for k in range(3):
            self.tp(ps[:, bX, k * 128:k * 128 + n], cqn[s2][:n, k * 128:(k + 1) * 128], self.ident[:n, :n],
                    [("cqn", s2), ("ident",)], [("ps", bX)])
        for c in range(2):
            self.tp(ps[:, bY, c * 128:c * 128 + n], ckvf[s2][:n, c * 128:(c + 1) * 128], self.ident[:n, :n],
                    [("ckvf", s2), ("ident",)], [("ps", bY)])
        self.tp(ps[:, bY, 256:256 + n], krf[s2][:n, :], self.ident[:n, :n],
                [("krf", s2, 0), ("krf", s2, 1), ("ident",)], [("ps", bY)])
        for k in range(3):
            if "e1" in sk:
                continue
            self.af(cqT[:, k, tc0:tc0 + n], ps[:, bX, k * 128:k * 128 + n], AF.Identity, [("ps", bX), ("qn",)],
                    [("cqT", t, k)], scale=qn_c[:, k:k + 1])
        kdst = ckvT_s[:, :, 0:n] if samp else ckvT[:, :, tc0:tc0 + n]
        rdst = krT_s[:64, 0:n] if samp else krT[:64, tc0:tc0 + n]
        if "e2" not in sk:
            self.cp("dve", kdst, ps[:, bY, 0:256].rearrange("p (c q) -> p c q", c=2)[:, :, 0:n],
                    [("ps", bY)], [("ckvT", t)])
        if "e3" not in sk:
            self.cp("act", rdst, ps[:64, bY, 256:256 + n], [("ps", bY)], [("krT", t)])
    stop = getattr(cfg, "stop", "")
    if stop == "A":
        for grp in self.groups:
            for tile in grp[3]:
                self.ln_tile(tile, None, 0.0, gslot)
            self.t_phase(grp)
        return
    P.phase = "mlaB%d" % li
    P.fence()
    self.off = mid_off
    wuq = self.buf([3, 1536], BF16)
    wqs = self.buf([3, HEADS, ROPE], BF16)
    wukT = self.buf([HEADS, KVR], BF16)
    csT = self.buf([2, 128])
    qlT = self.buf([HEADS, 2, 128], BF16)
    qrT = self.buf([HEADS, 128], BF16)
    t12 = [self.buf([2, 128])]
    pb_off = self.off
    pb = self.buf([max(NT * 128, 2048)], BF16)
    wuk = self.view(pb_off, [2, HEADS, 128], BF16)
    pb2 = [pb, self.buf([NT * 128], BF16)]
    pT_off = self.off
    pT = self.buf([max(NT * 128, 1024)], BF16)
    qnA = self.view(pT_off, [HEADS, 128], BF16)
    ol = [self.buf([KVR], BF16) for _ in range(2)]
    olT = [self.buf([KVR], BF16) for _ in range(2)]
    oT = [self.buf([HEADS, 128], BF16)]
    ats = self.buf([2, 8])
    self.dma("pool", wuq, d["a_w_uq"][j].rearrange("(k p) n -> p k n", p=128), "wuq", writes=[("wuq",)])
    self.dma("pool", wuk, d["a_w_uk"][j].rearrange("(c p) h d -> p c h d", p=128), "wuk", writes=[("pb", 0)])
    wr = wuq.rearrange("p k (h e) -> p k h e", h=HEADS)
    self.ts("dve", wqs[:, :, :, 0:32], wr[:, :, :, 160:192], -1.0, None, ALU.mult, None, [("wuq",)], [("wqs", 0)])
    self.cp("dve", wqs[:, :, :, 32:64], wr[:, :, :, 128:160], [("wuq",)], [("wqs", 1)])
    for h in range(HEADS):
        b = 7
        pv = ps[:, b, 0:128].bitcast(BF16)
        for cc in range(2):
            self.tp(pv[:, cc * 128:(cc + 1) * 128], wuk[:, cc, h, :], self.identb, [("pb", 0), ("identb",)], [("ps", b)])
        self.cp("dve", wukT[:, h, :], pv, [("ps", b)], [("wukT",)])
    pTps = ps[:, 4:6, :].rearrange("p a b -> p (a b)").bitcast(BF16)
    olTps = ps[:, 7, 0:128].bitcast(BF16)
    sall = ps[:, 0:4, :].rearrange("p a b -> p (a b)")
    for (t, tc0, n) in self.tiles:
        samp = t == NT
        gi = min(t // 4, self.NG)
        self.dma("sp", csT[:64, :, 0:n], d["c_rope_fm"][:, :, tc0:tc0 + n], "csT", writes=[("csT",)])
        os_ = 0
        rcq = [("cqT", t, k) for k in range(3)]
        for hb_ in range(2):
            b = 4 + hb_
            for hh in range(4):
                h = hb_ * 4 + hh
                for k in range(3):
                    self.mm(ps[:, b, hh * 128:hh * 128 + n], wuq[:, k, h * 192:h * 192 + 128], cqT[:, k, tc0:tc0 + n],
                            k == 0, k == 2, [("wuq",)] + rcq, [("ps", b)])
            self.cp("act" if hb_ == 0 else "dve", qnA[:, hb_ * 4:hb_ * 4 + 4, 0:n],
                    ps[:, b, :].rearrange("p (h q) -> p h q", h=4)[:, :, 0:n], [("ps", b)], [("pT", hb_)])
        for bi in range(4):
            for hh in range(2):
                h = bi * 2 + hh
                for cc in range(2):
                    o_ = (hh * 2 + cc) * 128
                    self.mm(ps[:, bi, o_:o_ + n], wukT[:, h, cc * 128:(cc + 1) * 128], qnA[:, h, 0:n], True, True,
                            [("wukT",), ("pT", h // 4)], [("ps", bi)])
            src = ps[:, bi, :].rearrange("p (h c q) -> p h c q", h=2, c=2)[:, :, :, 0:n]
            eng_ = "act" if bi % 2 == 0 else "dve"
            if samp:
                self.cp(eng_, self.qls[:, :, :, bi * 2:bi * 2 + 2].rearrange("p c q h -> p h c q"), src, [("ps", bi)],
                        [("qls", bi * 2), ("qls", bi * 2 + 1)])
            else:
                self.cp(eng_, qlT[:, bi * 2:bi * 2 + 2, :, 0:n], src, [("ps", bi)], [("qlT", bi * 2), ("qlT", bi * 2 + 1)])
        for hb_ in range(2):
            for hh in range(4):
                h = hb_ * 4 + hh
                for k in range(3):
                    self.mm(ps[:64, 4 + hb_, hh * 128:hh * 128 + n], wuq[:, k, h * 192 + 128:h * 192 + 192],
                            cqT[:, k, tc0:tc0 + n], k == 0, k == 2, [("wuq",)] + rcq, [("ps", 4 + hb_)])
                for k in range(3):
                    self.mm(ps[:64, 6 + hb_, hh * 128:hh * 128 + n], wqs[:, k, h, :], cqT[:, k, tc0:tc0 + n],
                            k == 0, k == 2, [("wqs", 0), ("wqs", 1)] + rcq, [("ps", 6 + hb_)])
        for h in range(HEADS):
            hb_, hh = h // 4, h % 4
            self.tt("dve", t12[0][:64, 0, 0:n], ps[:64, 4 + hb_, hh * 128:hh * 128 + n], csT[:64, 0, 0:n], ALU.mult,
                    [("ps", 4 + hb_), ("csT",)], [("t12", 0, 0)])
            self.tt("dve", t12[0][:64, 1, 0:n], ps[:64, 6 + hb_, hh * 128:hh * 128 + n], csT[:64, 1, 0:n], ALU.mult,
                    [("ps", 6 + hb_), ("csT",)], [("t12", 0, 1)])
            if samp:
                self.tt("dve", self.qrs[:64, :, h], t12[0][:64, 0, 0:n], t12[0][:64, 1, 0:n], ALU.add,
                        [("t12", 0, 0), ("t12", 0, 1)], [("qrs", h)])
            else:
                self.tt("dve", qrT[:64, h, 0:n], t12[0][:64, 0, 0:n], t12[0][:64, 1, 0:n], ALU.add,
                        [("t12", 0, 0), ("t12", 0, 1)], [("qrT", h)])
        if samp:
            continue
        if stop == "B0":
            self.ln_tile((t, tc0, n), None, 0.0, gslot)
            self.t_phase((gi, tc0, n, [(t, tc0, n)]))
            continue
        nk = t + 1
        nkeys = nk * 128
        nch = (nkeys + 511) // 512
        rps = [("ps", kc) for kc in range(nch)]

        def stS(h):
            a2 = h % 2
            pbh = pb2[a2]
            for kc in range(nch):
                k0 = kc * 512
                w = min(512, nkeys - k0)
                diag = kc == nch - 1
                rk = [("ckvT", tt_) for tt_ in range(k0 // 128, (k0 + w) // 128)]
                rr = [("krT", tt_) for tt_ in range(k0 // 128, (k0 + w) // 128)]
                self.mm(ps[:, kc, 0:w], qlT[:, h, 0, :], ckvT[:, 0, k0:k0 + w], True, False, [("qlT", h)] + rk, [("ps", kc)])
                self.mm(ps[:, kc, 0:w], qlT[:, h, 1, :], ckvT[:, 1, k0:k0 + w], False, False, [("qlT", h)] + rk, [("ps", kc)])
                self.mm(ps[:, kc, 0:w], qrT[:64, h, :], krT[:64, k0:k0 + w], False, not diag, [("qrT", h)] + rr, [("ps", kc)])
                if diag:
                    self.mm(ps[:, kc, w - 128:w], self.identb, self.cmaskb, False, True, [("identb",), ("cmaskb",)],
                            [("ps", kc)])
            self.red("dve", ats[:, a2, 0:1], sall[:, 0:nkeys], ALU.max, rps, [("ats", a2, "mx")])
            self.ts("dve", ats[:, a2, 1:2], ats[:, a2, 0:1], -SC, None, ALU.mult, None, [("ats", a2, "mx")], [("ats", a2, "nm")])
            self.memset("dve", ats[:, a2, 2:3], 0.0, [], [("ats", a2, "l")])
            self.af(pbh[:, 0:nkeys], sall[:, 0:nkeys], AF.Exp, rps + [("ats", a2, "nm"), ("ats", a2, "l")],
                    [("pb", a2), ("ats", a2, "l")], bias=ats[:, a2, 1:2], scale=SC, accum_out=ats[:, a2, 2:3])
            self.recip(ats[:, a2, 3:4], ats[:, a2, 2:3], [("ats", a2, "l")], [("ats", a2, "ri")])

        def stTVO(h):
            a2 = h % 2
            pbh = pb2[a2]
            for kt in range(nk):
                self.tp(pTps[:, kt * 128:(kt + 1) * 128], pbh[:, kt * 128:(kt + 1) * 128], self.identb,
                        [("pb", a2), ("identb",)], [("ps", 4 + kt // 8)])
            hk = min(nk, 8)
            self.cp("act", pT[:, 0:hk * 128], pTps[:, 0:hk * 128], [("ps", 4)], [("pT", 0)])
            if nk > hk:
                self.cp("dve", pT[:, hk * 128:nkeys], pTps[:, hk * 128:nkeys], [("ps", 5)], [("pT", 1)])
            for kt in range(nk):
                self.mm(ps[:, 6, 0:KVR], pT[:, kt * 128:(kt + 1) * 128], ckv_tm[:, kt, :], kt == 0, kt == nk - 1,
                        [("pT", 0), ("pT", 1), ("ckv_tm", kt)], [("ps", 6, "pv")])
            self.af(ol[a2], ps[:, 6, 0:KVR], AF.Identity, [("ps", 6, "pv"), ("ats", a2, "ri")], [("ol", a2)],
                    scale=ats[:, a2, 3:4])
            for cc in range(2):
                self.tp(olTps[:, cc * 128:(cc + 1) * 128], ol[a2][:, cc * 128:(cc + 1) * 128], self.identb,
                        [("ol", a2), ("identb",)], [("ps", 7)])
            self.cp("dve", olT[a2], olTps, [("ps", 7)], [("olT", a2)])
            for cc in range(2):
                self.mm(ps[:, 6, 256:384], wuv[:, cc, h, :], olT[a2][:, cc * 128:(cc + 1) * 128], cc == 0, cc == 1,
                        [("wuv",), ("olT", a2)], [("ps", 6, "o")])
            self.cp("act", oT[os_][:, h, :], ps[:, 6, 256:384], [("ps", 6, "o")], [("oT", os_, h)])

        stS(0)
        for h in range(HEADS):
            if h + 1 < HEADS:
                stS(h + 1)
            stTVO(h)
        for hf in range(2):
            for h in range(HEADS):
                self.mm(ps[:, hf, :], oT[os_][:, h, :], wo[:, h, hf * 512:(hf + 1) * 512], h == 0, h == HEADS - 1,
                        [("oT", os_, h), ("wo",)], [("ps", hf)])
        self.ln_tile((t, tc0, n), (0, 1), coef, gslot)
        self.t_phase((gi, tc0, n, [(t, tc0, n)]))
    if getattr(cfg, "skip_c", False):
        self.ln_tile(self.tiles[NT], None, 0.0, gslot)
        self.t_phase(self.groups[self.NG])
        return
    P.phase = "mlaC%d" % li
    P.fence()
    self.off = c_off
    NCH = NPG // 8
    NC = NCH + 1
    kvbk = [self.buf([8, KVR], BF16) for _ in range(6)]
    kvbr = [self.buf([8, ROPE], BF16) for _ in range(6)]
    idxt = self.buf([NSAMP, NCH])
    idx2 = self.buf([NSAMP, NCH], I32)
    self.ts("dve", idxt, self.idxf, float(j * cfg.NPOOL * 16), None, ALU.add, None, [("idxf",)], [("idxt",)])
    self.cp("dve", idx2, idxt, [("idxt",)], [("idx2",)])
    ckv_blk = d["cache_kv_latent"].rearrange("l n (g r) c -> (l n g) (r c)", r=8)
    kr_blk = d["cache_k_rope"].rearrange("l n (g r) c -> (l n g) (r c)", r=8)
    KT = [self.buf([2, 1024], BF16) for _ in range(2)]
    KrT = [self.buf([1024], BF16) for _ in range(2)]
    p_s = [self.buf([1024], BF16) for _ in range(2)]
    pT_s = [self.buf([64], BF16) for _ in range(2)]
    ssb = self.buf([NSAMP])
    mst = [self.buf([16]) for _ in range(2)]
    nmst = [self.buf([16]) for _ in range(2)]
    lst = [self.buf([16]) for _ in range(2)]
    wst = [self.buf([16]) for _ in range(2)]
    cst = [self.buf([8]) for _ in range(2)]
    ost = [self.buf([NC, KVR]) for _ in range(2)]
    acc = [self.buf([KVR]) for _ in range(2)]
    ol_s = [self.buf([KVR], BF16) for _ in range(2)]
    olT_s = self.buf([2, HEADS, NSAMP], BF16)
    oT_s = self.buf([HEADS, NSAMP], BF16)
    ckv_d = d["cache_kv_latent"]
    kr_d = d["cache_k_rope"]
    ckv_fl = ckv_d.rearrange("l n p c -> (l n) p c")
    kr_fl = kr_d.rearrange("l n p c -> (l n) p c")
    dsem = lambda key: self.dsem[key]
    bkA, bkB, bkC = 0, 1, 2
    kA = ps[:, bkA, :].bitcast(BF16)
    kB = ps[:, bkB, :].bitcast(BF16)
    kC = ps[:, bkC, :].bitcast(BF16)
    pTs_ps = ps[:, 3, 0:32].bitcast(BF16)
    olTs_ps = ps[:, 3, 64:72].bitcast(BF16)
    s_ps = ps[:, 6:8, :].rearrange("p a b -> p (a b)")
    HS = HEADS
    smask3 = self.smask.rearrange("p (b k) -> p b k", b=NSAMP)
    items = []
    for bsm in range(NSAMP):
        for ci in range(NCH):
            items.append((bsm, ci))
        items.append((bsm, NCH))
    NS = len(kvbk)

    def stageG(it):
        bsm, ci = items[it]
        if ci == NCH:
            return
        slot = it % NS
        self.gather(kvbk[slot].rearrange("p s c -> p (s c)"), ckv_blk, idx2[:, bsm, ci:ci + 1], ("kvk", slot))
        self.gather(kvbr[slot].rearrange("p s c -> p (s c)"), kr_blk, idx2[:, bsm, ci:ci + 1], ("kvr", slot))

    def stageA(it):
        bsm, ci = items[it]
        sb = bsm % 2
        if ci == 0:
            self.memset("dve", lst[sb][:HS, :], 0.0, [], [("lst", sb, c) for c in range(NC)])
        if ci == NCH:
            return
        slot = it % NS
        ks = it % 2
        for pg in range(8):
            self.tp(kA[:, pg * 128:(pg + 1) * 128], kvbk[slot][:, pg, 0:128], self.identb, [("kvk", slot), ("identb",)],
                    [("ps", bkA)])
            self.tp(kB[:, pg * 128:(pg + 1) * 128], kvbk[slot][:, pg, 128:256], self.identb, [("kvk", slot), ("identb",)],
                    [("ps", bkB)])
            self.tp(kC[:64, pg * 128:(pg + 1) * 128], kvbr[slot][:, pg, :], self.identb,
                    [("kvr", slot), ("identb",)], [("ps", bkC)])
        self.cp("act", KT[ks][:, 0, :], kA, [("ps", bkA)], [("KT", ks, 0)])
        self.cp("dve", KT[ks][:, 1, :], kB, [("ps", bkB)], [("KT", ks, 1)])
        self.cp("act", KrT[ks][:64, :], kC[:64, :], [("ps", bkC)], [("KrT", ks)])

    def stageB(it):
        bsm, ci = items[it]
        sb = bsm % 2
        ks = it % 2
        rq = [("qls", h_) for h_ in range(HS)]
        rr = [("qrs", h_) for h_ in range(HS)]
        if ci < NCH:
            for hh in range(2):
                bs_ = 6 + hh
                ksl = slice(hh * 512, (hh + 1) * 512)
                self.mm(ps[:HS, bs_, :], self.qls[:, 0, bsm, :], KT[ks][:, 0, ksl], True, False,
                        rq + [("KT", ks, 0)], [("ps", bs_)])
                self.mm(ps[:HS, bs_, :], self.qls[:, 1, bsm, :], KT[ks][:, 1, ksl], False, False,
                        [("KT", ks, 1)], [("ps", bs_)])
                self.mm(ps[:HS, bs_, :], self.qrs[:64, bsm, :], KrT[ks][:64, ksl], False, True,
                        rr + [("KrT", ks)], [("ps", bs_)])
            src, nkk, rsrc = s_ps[:HS, :], 1024, [("ps", 6), ("ps", 7)]
        else:
            rself = [("ckvT", NT), ("krT", NT)]
            self.mm(ps[:HS, 5, 0:NSAMP], self.qls[:, 0, bsm, :], ckvT_s[:, 0, :], True, False, rq + rself, [("ps", 5)])
            self.mm(ps[:HS, 5, 0:NSAMP], self.qls[:, 1, bsm, :], ckvT_s[:, 1, :], False, False, rself, [("ps", 5)])
            self.mm(ps[:HS, 5, 0:NSAMP], self.qrs[:64, bsm, :], krT_s[:64, :], False, True, rr + rself, [("ps", 5)])
            self.tt("dve", ssb[:HS, :], ps[:HS, 5, 0:NSAMP], smask3[:HS, bsm, :], ALU.add, [("ps", 5), ("smask",)],
                    [("ssb",)])
            src, nkk, rsrc = ssb[:HS, :], NSAMP, [("ssb",)]
        self.red("dve", mst[sb][:HS, ci:ci + 1], src, ALU.max, rsrc, [("mst", sb, ci)])
        self.ts("dve", nmst[sb][:HS, ci:ci + 1], mst[sb][:HS, ci:ci + 1], -SC, None, ALU.mult, None,
                [("mst", sb, ci)], [("nmst", sb, ci)])
        self.af(p_s[ks][:HS, 0:nkk], src, AF.Exp, rsrc + [("nmst", sb, ci), ("lst", sb, ci)],
                [("p_s", ks), ("lst", sb, ci)], bias=nmst[sb][:HS, ci:ci + 1], scale=SC,
                accum_out=lst[sb][:HS, ci:ci + 1])

    def stageC(it):
        bsm, ci = items[it]
        sb = bsm % 2
        ks = it % 2
        slot = it % NS
        if ci < NCH:
            for pg in range(8):
                self.tp(pTs_ps[:, pg * 8:(pg + 1) * 8], p_s[ks][:HS, pg * 128:(pg + 1) * 128], self.identb[:HS, :HS],
                        [("p_s", ks), ("identb",)], [("ps", 3, "pT")])
            self.cp("dve", pT_s[ks], pTs_ps, [("ps", 3, "pT")], [("pT_s", ks)])
            for pg in range(8):
                self.mm(ps[:HS, 4, 0:KVR], pT_s[ks][:, pg * 8:(pg + 1) * 8], kvbk[slot][:, pg, :], pg == 0, pg == 7,
                        [("pT_s", ks), ("kvk", slot)], [("ps", 4)])
        else:
            self.tp(pTs_ps[:NSAMP, 0:8], p_s[ks][:HS, 0:NSAMP], self.identb[:HS, :HS], [("p_s", ks), ("identb",)],
                    [("ps", 3, "pT")])
            self.cp("dve", pT_s[ks][:NSAMP, 0:8], pTs_ps[:NSAMP, 0:8], [("ps", 3, "pT")], [("pT_s", ks)])
            self.mm(ps[:HS, 4, 0:KVR], pT_s[ks][:NSAMP, 0:8], ckv_tm_s[:NSAMP, :], True, True,
                    [("pT_s", ks), ("ckv_tm", NT)], [("ps", 4)])
        self.cp("act", ost[sb][:HS, ci, :], ps[:HS, 4, 0:KVR], [("ps", 4)], [("ost", sb, ci)])
        if ci < NCH:
            return
        allm = [("mst", sb, c) for c in range(NC)]
        self.red("dve", cst[sb][:HS, 0:1], mst[sb][:HS, 0:NC], ALU.max, allm, [("cst", sb, 0)])
        self.ts("dve", cst[sb][:HS, 1:2], cst[sb][:HS, 0:1], -SC, None, ALU.mult, None, [("cst", sb, 0)], [("cst", sb, 1)])
        self.af(wst[sb][:HS, 0:NC], mst[sb][:HS, 0:NC], AF.Exp, allm + [("cst", sb, 1)], [("wst", sb)],
                bias=cst[sb][:HS, 1:2], scale=SC)
        self.tt("dve", nmst[sb][:HS, 0:NC], wst[sb][:HS, 0:NC], lst[sb][:HS, 0:NC], ALU.mult,
                [("wst", sb)] + [("lst", sb, c) for c in range(NC)], [("nmst", sb, c) for c in range(NC)])
        self.red("dve", cst[sb][:HS, 2:3], nmst[sb][:HS, 0:NC], ALU.add, [("nmst", sb, c) for c in range(NC)],
                 [("cst", sb, 2)])
        self.recip(cst[sb][:HS, 3:4], cst[sb][:HS, 2:3], [("cst", sb, 2)], [("cst", sb, 3)])
        self.ts("dve", acc[sb][:HS, :], ost[sb][:HS, 0, :], wst[sb][:HS, 0:1], None, ALU.mult, None,
                [("ost", sb, 0), ("wst", sb)], [("acc", sb)])
        for c in range(1, NC):
            self.stt("dve", acc[sb][:HS, :], ost[sb][:HS, c, :], wst[sb][:HS, c:c + 1], acc[sb][:HS, :], ALU.mult, ALU.add,
                     [("ost", sb, c), ("wst", sb), ("acc", sb)], [("acc", sb)])
        self.af(ol_s[sb][:HS, :], acc[sb][:HS, :], AF.Identity, [("acc", sb), ("cst", sb, 3)], [("ol_s", sb)],
                scale=cst[sb][:HS, 3:4])
        for cc in range(2):
            self.tp(olTs_ps[:, cc * 8:(cc + 1) * 8], ol_s[sb][:HS, cc * 128:(cc + 1) * 128], self.identb[:HS, :HS],
                    [("ol_s", sb), ("identb",)], [("ps", 3, "olT")])
        self.cp("dve", olT_s[:, :, :, bsm], olTs_ps.rearrange("p (c h) -> p c h", c=2), [("ps", 3, "olT")],
                [("olT_s", bsm)])

    NI = len(items)
    PF = 3
    for it in range(min(PF, NI)):
        stageG(it)
    for step in range(NI + 2):
        if step + PF < NI:
            stageG(step + PF)
        if step < NI:
            stageA(step)
        if 0 <= step - 1 < NI:
            stageB(step - 1)
        if 0 <= step - 2 < NI:
            stageC(step - 2)
    allo = [("olT_s", b_) for b_ in range(NSAMP)]
    for h in range(HEADS):
        b = self.nb()
        for cc in range(2):
            self.mm(ps[:, b, 0:NSAMP], wuv[:, cc, h, :], olT_s[:, cc, h, :], cc == 0, cc == 1, [("wuv",)] + allo, [("ps", b)])
        self.cp("act", oT_s[:, h, :], ps[:, b, 0:NSAMP], [("ps", b)], [("oT_s", h)])
    b0, b1 = self.nb(), self.nb()
    for hf, b in ((0, b0), (1, b1)):
        for h in range(HEADS):
            self.mm(ps[:NSAMP, b, :], oT_s[:, h, :], wo[:, h, hf * 512:(hf + 1) * 512], h == 0, h == HEADS - 1,
                    [("oT_s", h), ("wo",)], [("ps", b)])
    st_ = self.tiles[NT]
    self.ln_tile(st_, (b0, b1), coef, gslot)
    self.t_phase(self.groups[self.NG])


def _gather(self, out, src, idx, key):
    self.P.add("pool", lambda e: e.indirect_dma_start(out=out, out_offset=None, in_=src,
                                                      in_offset=bass.IndirectOffsetOnAxis(ap=idx, axis=0)),
               [("idx2",)], [key], dkey=key)


K.gather = _gather
K.mla = _mla
```
